# Optimizing a Trainium2 kernel written in Bass

```python
import jax
import jax.numpy as jnp
from jax import lax
import numpy as np

D_MODEL = 1024
BATCH = 8
SEQ = 4096
DEPTH = 4

GRID_W = 64
CTX_LEN = 256
N_MIXERS = 2
EXPAND = 2
D_INNER = EXPAND * D_MODEL
RW_HEAD = 64
RW_HEADS = D_INNER // RW_HEAD
R_DECAY = 64
R_AAA = 64
R_MV = 32
HG_DK = 128
HG_HEADS = D_INNER // HG_DK
HG_DV = D_INNER // HG_HEADS
CHUNK = 64
N_RW = (DEPTH + 1) // 2
N_HG = DEPTH // 2
NORM_EPS = 1e-6
LN_X_EPS = 64e-5

kernel_name = "hybrid_rwkv7_hgrn2_dit_trunk"


def _rmsnorm(x, g):
    xf = x.astype(jnp.float32)
    y = xf * lax.rsqrt(jnp.mean(xf * xf, axis=-1, keepdims=True) + NORM_EPS)
    return (y * g.astype(jnp.float32)).astype(x.dtype)


def _heads(t, n_heads):
    return t.reshape(t.shape[0], t.shape[1], n_heads, t.shape[2] // n_heads)


def _qshift_grid(h, rows):
    b, s, d = h.shape
    q = d // 4
    g = jnp.pad(h.reshape(b, rows, GRID_W, d), ((0, 0), (1, 1), (1, 1), (0, 0)))
    left = g[:, 1:-1, :-2, :q]
    right = g[:, 1:-1, 2:, q:2 * q]
    up = g[:, :-2, 1:-1, 2 * q:3 * q]
    down = g[:, 2:, 1:-1, 3 * q:]
    return jnp.concatenate([left, right, up, down], axis=-1).reshape(b, s, d)


def _shift_seq(h):
    half = h.shape[-1] // 2
    prev = jnp.pad(h[:, :-1, :half], ((0, 0), (1, 0), (0, 0)))
    nxt = jnp.pad(h[:, 1:, half:], ((0, 0), (0, 1), (0, 0)))
    return jnp.concatenate([prev, nxt], axis=-1)


def _rwkv7_features(h, hs, mix, proj, w0, w1, w2, a0, a1, a2, k_k, k_a, v_res, v_first):
    f32 = jnp.float32
    xx = hs - h
    xr, xw, xk, xv, xa, xg = [h + xx * mix[n] for n in range(6)]
    r = xr @ proj[0]
    k = xk @ proj[1]
    v = xv @ proj[2]
    z = jax.nn.silu(xg @ proj[3])
    v_raw = v
    if v_res is not None:
        v0, v1, v2 = v_res
        v = v + (v_first - v) * jax.nn.sigmoid(v0 + (xv @ v1) @ v2)
    kk = _heads(k * k_k, RW_HEADS).astype(f32)
    kk = kk * lax.rsqrt(jnp.maximum(jnp.sum(kk * kk, axis=-1, keepdims=True), 1e-24))
    dirs = []
    for d in range(2):
        u = (w0[d] + jnp.tanh(xw @ w1[d]) @ w2[d]).astype(f32)
        log_w = -jnp.exp(-jax.nn.softplus(-u) - 0.5)
        a = jax.nn.sigmoid((a0[d] + (xa @ a1[d]) @ a2[d]).astype(f32))
        k_d = k.astype(f32) * (1.0 + (a - 1.0) * k_a.astype(f32))
        dirs.append((_heads(log_w, RW_HEADS), _heads(a, RW_HEADS), _heads(k_d, RW_HEADS)))
    return (_heads(r, RW_HEADS).astype(f32), _heads(v, RW_HEADS).astype(f32), z, kk, dirs, v_raw)


def _rwkv7_scan(r, log_w, k, v, kk, a, s0, reverse, emit):
    def step(s, xs):
        w_t, k_t, v_t, kk_t, a_t = xs[:5]
        sa = jnp.einsum('bhvk,bhk->bhv', s, -kk_t)
        s = (s * jnp.exp(w_t)[:, :, None, :] + sa[..., None] * (kk_t * a_t)[:, :, None, :]
             + v_t[..., None] * k_t[:, :, None, :])
        y = jnp.einsum('bhvk,bhk->bhv', s, xs[5]) if emit else None
        return s, y
    xs = (log_w, k, v, kk, a) + ((r,) if emit else ())
    s_last, ys = lax.scan(step, s0, tuple(jnp.moveaxis(t, 1, 0) for t in xs), reverse=reverse)
    return (jnp.moveaxis(ys, 0, 1) if emit else None), s_last


def _rwkv7_out(ys, feats, r_k, ln_w, ln_b, w_o):
    r, v, z, _, dirs, _ = feats
    y = ys[0] + ys[1]
    mu = jnp.mean(y, axis=-1, keepdims=True)
    var = jnp.mean(jnp.square(y - mu), axis=-1, keepdims=True)
    y = (y - mu) * lax.rsqrt(var + LN_X_EPS)
    rk = r_k.astype(jnp.float32).reshape(RW_HEADS, RW_HEAD)
    bonus = sum(jnp.sum(r * kd * rk, axis=-1, keepdims=True) for (_, _, kd) in dirs) * v
    b, t = y.shape[:2]
    y = (y.reshape(b, t, D_INNER) * ln_w.astype(jnp.float32) + ln_b.astype(jnp.float32)
         + bonus.reshape(b, t, D_INNER))
    return (y.astype(z.dtype) * z) @ w_o


def _rwkv7_mixer(h_lat, h_ctx, rows, p, v_res, vf_lat, vf_ctx, ctx_out):
    mix, proj, w_o, w0, w1, w2, a0, a1, a2, k_k, k_a, r_k, ln_w, ln_b = p
    shared = (mix, proj, w0, w1, w2, a0, a1, a2, k_k, k_a, v_res)
    f_ctx = _rwkv7_features(h_ctx, _shift_seq(h_ctx), *shared, vf_ctx)
    f_lat = _rwkv7_features(h_lat, _qshift_grid(h_lat, rows), *shared, vf_lat)
    s0 = jnp.zeros((h_lat.shape[0], RW_HEADS, RW_HEAD, RW_HEAD), jnp.float32)

    def run(f, s_init, emit):
        r, v, _, kk, dirs, _ = f
        outs = [_rwkv7_scan(r, lw, kd, v, kk, a, s_init[d], d == 1, emit)
                for d, (lw, a, kd) in enumerate(dirs)]
        return [o[0] for o in outs], [o[1] for o in outs]

    ys_ctx, s_ctx = run(f_ctx, (s0, s0), ctx_out)
    ys_lat, _ = run(f_lat, s_ctx, True)
    y_lat = _rwkv7_out(ys_lat, f_lat, r_k, ln_w, ln_b, w_o)
    y_ctx = _rwkv7_out(ys_ctx, f_ctx, r_k, ln_w, ln_b, w_o) if ctx_out else None
    return y_lat, y_ctx, f_lat[5], f_ctx[5]


def _hgrn2_features(h, w_in, lb):
    f32 = jnp.float32
    q, f_fwd, f_bwd, i, g = jnp.split(h @ w_in, 5, axis=-1)
    lb = lb.astype(f32)
    dirs = []
    for fl in (f_fwd, f_bwd):
        f = lb + (1.0 - lb) * jax.nn.sigmoid(fl.astype(f32))
        dirs.append((_heads(1.0 - f, HG_HEADS), _heads(jnp.log(f), HG_HEADS)))
    return (_heads(jax.nn.silu(q), HG_HEADS).astype(f32), _heads(i, HG_HEADS).astype(f32), g, dirs)


def _to_chunks(t):
    b, s, h, d = t.shape
    return t.reshape(b, s // CHUNK, CHUNK, h, d).transpose(1, 0, 3, 2, 4)


def _hgrn2_chunk_scan(q, k, v, log_f, s0):
    mask = jnp.tril(jnp.ones((CHUNK, CHUNK), dtype=bool))[:, :, None]

    def step(s, xs):
        q_c, k_c, v_c, lf_c = xs
        cum = jnp.cumsum(lf_c, axis=2)
        diff = jnp.where(mask, cum[:, :, :, None, :] - cum[:, :, None, :, :], -jnp.inf)
        att = jnp.einsum('bhtk,bhsk,bhtsk->bhts', q_c, k_c, jnp.exp(diff))
        y = (jnp.einsum('bhts,bhsv->bhtv', att, v_c)
             + jnp.einsum('bhtk,bhkv->bhtv', q_c * jnp.exp(cum), s))
        last = cum[:, :, -1:, :]
        s = (jnp.exp(last[:, :, 0, :])[..., None] * s
             + jnp.einsum('bhsk,bhsv->bhkv', k_c * jnp.exp(last - cum), v_c))
        return s, y

    s_last, ys = lax.scan(step, s0, tuple(_to_chunks(t) for t in (q, k, v, log_f)))
    nc, b, h, c, d = ys.shape
    return ys.transpose(1, 0, 3, 2, 4).reshape(b, nc * c, h, d), s_last


def _hgrn2_final_state(k, v, log_f):
    cum = jnp.cumsum(log_f, axis=1)
    return jnp.einsum('bthk,bthv->bhkv', k * jnp.exp(cum[:, -1:] - cum), v)


def _hgrn2_out(y, g, gn, w_o):
    y = y * lax.rsqrt(jnp.mean(y * y, axis=-1, keepdims=True) + NORM_EPS) * gn.astype(jnp.float32)
    b, t = y.shape[:2]
    return (y.reshape(b, t, D_INNER).astype(g.dtype) * jax.nn.silu(g)) @ w_o


def _hgrn2_mixer(h_lat, h_ctx, w_in, w_o, gn, lb, ctx_out):
    rev = lambda t: t[:, ::-1]
    q_c, i_c, g_c, dirs_c = _hgrn2_features(h_ctx, w_in, lb)
    q_l, i_l, g_l, dirs_l = _hgrn2_features(h_lat, w_in, lb)
    (k_cf, lf_cf), (k_cb, lf_cb) = dirs_c
    (k_lf, lf_lf), (k_lb, lf_lb) = dirs_l
    if ctx_out:
        s0 = jnp.zeros((h_ctx.shape[0], HG_HEADS, HG_DK, HG_DV), jnp.float32)
        yc_f, s_f = _hgrn2_chunk_scan(q_c, k_cf, i_c, lf_cf, s0)
        yc_b, s_b = _hgrn2_chunk_scan(rev(q_c), rev(k_cb), rev(i_c), rev(lf_cb), s0)
        y_ctx = _hgrn2_out(yc_f + rev(yc_b), g_c, gn, w_o)
    else:
        s_f = _hgrn2_final_state(k_cf, i_c, lf_cf)
        s_b = _hgrn2_final_state(rev(k_cb), rev(i_c), rev(lf_cb))
        y_ctx = None
    yl_f, _ = _hgrn2_chunk_scan(q_l, k_lf, i_l, lf_lf, s_f)
    yl_b, _ = _hgrn2_chunk_scan(rev(q_l), rev(k_lb), rev(i_l), rev(lf_lb), s_b)
    y_lat = _hgrn2_out(yl_f + rev(yl_b), g_l, gn, w_o)
    return y_lat, y_ctx


def setup_inputs(seed: int = 0) -> dict:
    key = jax.random.key(seed)
    ks = iter(jax.random.split(key, 40))
    nrm = lambda shape, scale: scale * jax.random.normal(next(ks), shape, jnp.float32)
    D, DI = D_MODEL, D_INNER
    ramp = (jnp.arange(DI, dtype=jnp.float32) / (DI - 1)) ** 0.9
    return {
        "x": nrm((BATCH, SEQ, D), 1.0),
        "c": nrm((BATCH, D), 1.0),
        "ctx": nrm((BATCH, CTX_LEN, D), 1.0),
        "c_ctx": nrm((D,), 1.0),
        "mod_w": nrm((DEPTH, D, 3 * D), D ** -0.5),
        "mod_b": nrm((DEPTH, 3 * D), 0.01),
        "pre_g": 1.0 + nrm((DEPTH, D), 0.05),
        "post_g": 1.0 + nrm((DEPTH, D), 0.05),
        "rw_mix": jax.random.uniform(next(ks), (N_RW, 6, D), jnp.float32),
        "rw_proj": nrm((N_RW, 4, D, DI), D ** -0.5),
        "rw_wo": nrm((N_RW, DI, D), DI ** -0.5),
        "rw_w0": -6.0 + 5.0 * ramp + nrm((N_RW, 2, DI), 0.1),
        "rw_w1": nrm((N_RW, 2, D, R_DECAY), D ** -0.5),
        "rw_w2": nrm((N_RW, 2, R_DECAY, DI), 0.5 * R_DECAY ** -0.5),
        "rw_a0": nrm((N_RW, 2, DI), 0.1),
        "rw_a1": nrm((N_RW, 2, D, R_AAA), D ** -0.5),
        "rw_a2": nrm((N_RW, 2, R_AAA, DI), R_AAA ** -0.5),
        "rw_v0": nrm((N_RW - 1, DI), 0.1),
        "rw_v1": nrm((N_RW - 1, D, R_MV), D ** -0.5),
        "rw_v2": nrm((N_RW - 1, R_MV, DI), R_MV ** -0.5),
        "rw_kk": 0.85 + nrm((N_RW, DI), 0.05),
        "rw_ka": 1.0 + nrm((N_RW, DI), 0.05),
        "rw_rk": nrm((N_RW, DI), 0.1),
        "rw_lnw": 1.0 + nrm((N_RW, DI), 0.05),
        "rw_lnb": nrm((N_RW, DI), 0.01),
        "hg_win": nrm((N_HG, D, 5 * DI), D ** -0.5),
        "hg_wo": nrm((N_HG, DI, D), DI ** -0.5),
        "hg_gn": 1.0 + nrm((N_HG, HG_DV), 0.05),
        "hg_lb": nrm((DEPTH, DI), 0.1),
    }


def reference(x, c, ctx, c_ctx, mod_w, mod_b, pre_g, post_g, rw_mix, rw_proj, rw_wo, rw_w0, rw_w1,
              rw_w2, rw_a0, rw_a1, rw_a2, rw_v0, rw_v1, rw_v2, rw_kk, rw_ka, rw_rk, rw_lnw, rw_lnb,
              hg_win, hg_wo, hg_gn, hg_lb):
    rows = x.shape[1] // GRID_W
    sc = jax.nn.silu(c)
    scc = jax.nn.silu(c_ctx)
    p_lb = jax.nn.softmax(hg_lb.astype(jnp.float32), axis=0)
    lb_all = jnp.cumsum(p_lb, axis=0) - p_lb[0]
    vf_lat = None
    vf_ctx = None
    for i in range(DEPTH):
        ctx_out = i < DEPTH - 1
        shift, scale, gate = jnp.split(sc @ mod_w[i] + mod_b[i], 3, axis=-1)
        shift_c, scale_c, gate_c = jnp.split(scc @ mod_w[i] + mod_b[i], 3, axis=-1)
        h_lat = _rmsnorm(x, pre_g[i]) * (1.0 + scale[:, None]) + shift[:, None]
        h_ctx = _rmsnorm(ctx, pre_g[i]) * (1.0 + scale_c) + shift_c
        j = i // N_MIXERS
        if i % N_MIXERS == 0:
            p = (rw_mix[j], rw_proj[j], rw_wo[j], rw_w0[j], rw_w1[j], rw_w2[j], rw_a0[j], rw_a1[j],
                 rw_a2[j], rw_kk[j], rw_ka[j], rw_rk[j], rw_lnw[j], rw_lnb[j])
            v_res = None if j == 0 else (rw_v0[j - 1], rw_v1[j - 1], rw_v2[j - 1])
            y_lat, y_ctx, vfl, vfc = _rwkv7_mixer(h_lat, h_ctx, rows, p, v_res, vf_lat, vf_ctx, ctx_out)
            if j == 0:
                vf_lat, vf_ctx = vfl, vfc
        else:
            y_lat, y_ctx = _hgrn2_mixer(h_lat, h_ctx, hg_win[j], hg_wo[j], hg_gn[j], lb_all[i], ctx_out)
        x = x + gate[:, None] * _rmsnorm(y_lat, post_g[i])
        if ctx_out:
            ctx = ctx + gate_c * _rmsnorm(y_ctx, post_g[i])
    return x
```

```python
import contextlib
import numpy as np
import concourse.bass as bass
import concourse.mybir as mybir
from concourse.bass_utils import run_bass_kernel_spmd

F32 = mybir.dt.float32
AF = mybir.ActivationFunctionType
ALU = mybir.AluOpType
AX = mybir.AxisListType

D = 1024
DI = 2048
NORM_EPS = 1e-6
LN_X_EPS = 64e-5
WSCALE = -0.6065306597126334


class Buf:
    __slots__ = ("name", "t", "last_w", "readers", "dsem", "dcount")

    def __init__(self, name, t=None):
        self.name = name
        self.t = t
        self.last_w = None
        self.readers = []
        self.dsem = None
        self.dcount = 0

    def __getitem__(self, idx):
        return self.t[idx]


class Rot:
    def __init__(self, bufs):
        self.bufs = bufs
        self.i = 0

    def next(self):
        b = self.bufs[self.i % len(self.bufs)]
        self.i += 1
        return b


class Prog:
    ENGS = ("pe", "act", "dve", "pool", "sp")

    def __init__(self, nc, st, ndma=40):
        self.nc = nc
        self.ops = {e: [] for e in self.ENGS}
        self.count = {e: 0 for e in self.ENGS}
        self.seen = {e: {} for e in self.ENGS}
        self.sems = {}
        for e in self.ENGS:
            self.sems[e] = st.enter_context(nc.semaphore("s_" + e))
        self.dma_sems = [st.enter_context(nc.semaphore("s_d%d" % i)) for i in range(ndma)]
        self.dma_counts = [0] * ndma
        self.dma_next = 0
        self.nops = 0
        import os
        self.maxops = int(os.environ.get("K_MAXOPS", "100000000"))

    def _deps(self, eng, reads, writes, is_pe_mm=False):
        need = {}

        def add(tok):
            if tok is None:
                return
            k, v = tok
            if k == eng and is_pe_mm:
                return
            if need.get(k, 0) < v:
                need[k] = v
        for b in reads:
            add(b.last_w)
        for b in writes:
            add(b.last_w)
            for r in b.readers:
                add(r)
        waits = []
        seen = self.seen[eng]
        for k, v in need.items():
            if seen.get(k, 0) >= v:
                continue
            seen[k] = v
            waits.append((k, v))
        return waits

    def _commit(self, tok, reads, writes):
        for b in reads:
            b.readers.append(tok)
        for b in writes:
            b.last_w = tok
            b.readers = []

    def op(self, eng, fn, reads=(), writes=(), mm=False):
        if self.nops >= self.maxops:
            return None
        waits = self._deps(eng, reads, writes, is_pe_mm=mm)
        self.count[eng] += 1
        tok = (eng, self.count[eng])
        self.ops[eng].append((waits, fn, eng, 1))
        self._commit(tok, reads, writes)
        self.nops += 1
        return tok

    def dma(self, eng, fn, sb, reads=(), writes=()):
        if self.nops >= self.maxops:
            return None
        waits = self._deps(eng, reads, writes)
        if sb.dsem is None:
            sb.dsem = self.dma_next % len(self.dma_sems)
            self.dma_next += 1
        i = sb.dsem
        self.dma_counts[i] += 16
        key = ("d", i)
        tok = (key, self.dma_counts[i])
        self.ops[eng].append((waits, fn, key, 16))
        self._commit(tok, reads, writes)
        self.nops += 1
        return tok

    def _sem(self, k):
        return self.sems[k] if isinstance(k, str) else self.dma_sems[k[1]]

    def flush(self):
        nc = self.nc
        final = [(e, self.count[e]) for e in self.ENGS if self.count[e] > 0]
        final += [(("d", i), c) for i, c in enumerate(self.dma_counts) if c > 0]
        with nc.Block() as block:
            engmap = {"pe": block.tensor, "act": block.scalar, "dve": block.vector,
                      "pool": block.gpsimd, "sp": block.sync}

            def make(ename):
                oplist = self.ops[ename]
                seen = self.seen[ename]

                def body(e):
                    for waits, fn, ik, amt in oplist:
                        for k, v in waits:
                            e.wait_ge(self._sem(k), v)
                        fn(e).then_inc(self._sem(ik), amt)
                    for k, v in final:
                        if k == ename:
                            continue
                        if seen.get(k, 0) >= v:
                            continue
                        seen[k] = v
                        e.wait_ge(self._sem(k), v)
                return body
            for ename in self.ENGS:
                engmap[ename](make(ename))
        self.ops = {e: [] for e in self.ENGS}
        self.dma_next = 0


class Builder:
    def __init__(self, nc, rows=64, ctx_len=256, depth=4, debug=False):
        self.nc = nc
        self.rows = rows
        self.LAT = 64 * rows
        self.CTX = ctx_len
        self.NCB = ctx_len // 128
        self.NLB = self.LAT // 128
        self.NB = self.NCB + self.NLB
        self.T = self.NB * 128
        self.depth = depth
        self.debug = debug
        self.dbufs = {}

    def dram(self, name, shape, kind="Internal"):
        if self.debug and kind == "Internal":
            kind = "ExternalOutput"
        return self.nc.dram_tensor(name, list(shape), F32, kind=kind).ap()

    def dbuf(self, *key):
        b = self.dbufs.get(key)
        if b is None:
            b = Buf(str(key))
            self.dbufs[key] = b
        return b

    def tile(self, st, name, shape):
        self.uid = getattr(self, "uid", 0) + 1
        name = "%s_u%d" % (name, self.uid)
        return Buf(name, st.enter_context(self.nc.sbuf_tensor(name, list(shape), F32)))

    def rot(self, st, name, shape, n):
        return Rot([self.tile(st, "%s%d" % (name, i), shape) for i in range(n)])

    def mm(self, out, lhsT, rhs, start, stop, R, W):
        self.p.op("pe", lambda e: e.matmul(out, lhsT=lhsT, rhs=rhs, start=start, stop=stop),
                  reads=R, writes=W, mm=True)

    def tr(self, out, in_, R, W):
        k = in_.shape[0]
        ident = self.ident[0:k, 0:k]
        self.p.op("pe", lambda e: e.transpose(out, in_, ident), reads=list(R) + [self.ident], writes=W, mm=True)

    def act(self, out, in_, func, R, W, scale=None, bias=None, accum=None):
        kw = {}
        if scale is not None:
            kw["scale"] = scale
        if bias is not None:
            kw["bias"] = bias
        if accum is not None:
            kw["accum_out"] = accum
        self.p.op("act", lambda e: e.activation(out=out, in_=in_, func=func, **kw), reads=R, writes=W)

    def tt(self, eng, out, a, b, op, R, W):
        self.p.op(eng, lambda e: e.tensor_tensor(out=out, in0=a, in1=b, op=op), reads=R, writes=W)

    def ts(self, eng, out, a, s1, op0, R, W, s2=None, op1=None):
        if op1 is None:
            self.p.op(eng, lambda e: e.tensor_scalar(out=out, in0=a, scalar1=s1, scalar2=None, op0=op0),
                      reads=R, writes=W)
        else:
            self.p.op(eng, lambda e: e.tensor_scalar(out=out, in0=a, scalar1=s1, scalar2=s2, op0=op0, op1=op1),
                      reads=R, writes=W)

    def stt(self, out, a, scalar, b, op0, op1, R, W):
        self.p.op("dve", lambda e: e.scalar_tensor_tensor(out=out, in0=a, scalar=scalar, in1=b, op0=op0, op1=op1),
                  reads=R, writes=W)

    def cp(self, eng, out, in_, R, W):
        if eng == "act":
            self.act(out, in_, AF.Copy, R, W)
        else:
            self.p.op(eng, lambda e: e.tensor_copy(out=out, in_=in_), reads=R, writes=W)

    def memset(self, eng, out, val, W):
        self.p.op(eng, lambda e: e.memset(out, val), writes=W)

    def red(self, out, in_, R, W):
        self.p.op("dve", lambda e: e.tensor_reduce(out=out, in_=in_, axis=AX.X, op=ALU.add), reads=R, writes=W)

    def ld(self, buf, out_ap, in_ap, R=()):
        return self.p.dma("sp", lambda e: e.dma_start(out=out_ap, in_=in_ap), buf, reads=R, writes=[buf])

    def st_(self, buf, out_ap, in_ap, W=()):
        return self.p.dma("sp", lambda e: e.dma_start(out=out_ap, in_=in_ap), buf, reads=[buf], writes=W)

    def big(self):
        return self.psbig.next()

    def half(self):
        return self.pshalf.next()

    def build(self):
        nc = self.nc
        NB, T = self.NB, self.T
        inp = lambda name, shape: nc.dram_tensor(name, list(shape), F32, kind="ExternalInput").ap()
        I = self.I = {}
        I["x"] = inp("x", [self.LAT, D])
        I["c"] = inp("c", [D])
        I["ctx"] = inp("ctx", [self.CTX, D])
        I["c_ctx"] = inp("c_ctx", [D])
        I["mod_w"] = inp("mod_w", [4, D, 3 * D])
        I["mod_b"] = inp("mod_b", [4, 3 * D])
        I["pre_g"] = inp("pre_g", [4, D])
        I["post_g"] = inp("post_g", [4, D])
        I["rw_mix"] = inp("rw_mix", [2, 6, D])
        I["rw_proj"] = inp("rw_proj", [2, 4, D, DI])
        I["rw_wo"] = inp("rw_wo", [2, DI, D])
        I["rw_w0"] = inp("rw_w0", [2, 2, DI])
        I["rw_w1"] = inp("rw_w1", [2, 2, D, 64])
        I["rw_w2"] = inp("rw_w2", [2, 2, 64, DI])
        I["rw_a0"] = inp("rw_a0", [2, 2, DI])
        I["rw_a1"] = inp("rw_a1", [2, 2, D, 64])
        I["rw_a2"] = inp("rw_a2", [2, 2, 64, DI])
        I["rw_v0"] = inp("rw_v0", [1, DI])
        I["rw_v1"] = inp("rw_v1", [1, D, 32])
        I["rw_v2"] = inp("rw_v2", [1, 32, DI])
        for n in ("rw_kk", "rw_ka", "rw_rk", "rw_lnw", "rw_lnb"):
            I[n] = inp(n, [2, DI])
        I["hg_win"] = inp("hg_win", [2, D, 5 * DI])
        I["hg_wo"] = inp("hg_wo", [2, DI, D])
        I["hg_gn"] = inp("hg_gn", [2, 128])
        I["hg_lb"] = inp("hg_lb", [4, DI])
        self.out = nc.dram_tensor("out", [self.LAT, D], F32, kind="ExternalOutput").ap()

        S = self.S = {}
        S["xs"] = self.dram("xs", [T, D])
        S["hT"] = self.dram("hT", [D, T])
        S["vfirst"] = self.dram("vfirst", [T, DI])
        S["v"] = self.dram("sv", [T, DI])
        S["z"] = self.dram("sz", [T, DI])
        S["bon"] = self.dram("sbon", [T, 32])
        S["yd"] = self.dram("syd", [2, T, DI])
        for n in ("kT", "rT", "bT", "ktT"):
            S[n] = self.dram("s" + n, [2, NB, 32, 64, 128])
        for n in ("bh", "kh"):
            S[n] = self.dram("s" + n, [2, T, DI])
        S["pc"] = self.dram("spc", [2, NB, 64, 32])
        for n in ("qT", "gkT"):
            S[n] = self.dram("s" + n, [2, NB, 16, 128, 128])
        S["pcm"] = self.dram("spcm", [2, NB, 128, 16, 4])
        S["qbT"] = self.dram("sqbT", [2, NB, 16, 128, 128])
        S["mod"] = self.dram("smod", [2, 3 * D])
        S["lb"] = self.dram("slb", [2, DI])

        with contextlib.ExitStack() as gst:
            self.p = Prog(nc, gst)
            self.ps = [Buf("ps%d" % i, gst.enter_context(nc.psum_tensor("ps%d" % i, [128, 512], F32))) for i in range(8)]
            self.psbig = Rot(self.ps[0:4])
            self.pshalf = Rot(self.ps[4:8])
            with contextlib.ExitStack() as tst:
                self.setup_consts(gst, tst)
                self.p.flush()
            import os
            stop = int(os.environ.get("K_STOP", "999"))
            npass = 0
            for li in range(self.depth):
                seq = [lambda: self.pass_H(li)]
                if li % 2 == 0:
                    seq += [lambda: self.rw_pass_A(li), lambda: self.rw_pass_B(li), lambda: self.pass_C(li, rw=True)]
                else:
                    seq += [lambda: self.hg_pass_A(li), lambda: self.hg_pass_B(li), lambda: self.pass_C(li, rw=False)]
                for f in seq:
                    if npass < stop:
                        f()
                    npass += 1
        return nc

    def setup_consts(self, st, tst):
        t = lambda n, s: self.tile(st, n, s)
        tt_ = lambda n, s: self.tile(tst, n, s)
        self.ident = t("ident", [128, 128])
        self.ones = t("ones", [128, 128])
        UI, US, LI, LS = t("UI", [128, 128]), t("US", [128, 128]), t("LI", [128, 128]), t("LS", [128, 128])
        self.memset("pool", self.ones[:], 1.0, [self.ones])

        def sel(dst, step, cm, cmp):
            self.p.op("pool", lambda e: e.affine_select(out=dst[:], in_=self.ones[:], pattern=[[step, 128]],
                                                        compare_op=cmp, fill=0.0, base=0, channel_multiplier=cm),
                      reads=[self.ones], writes=[dst])
        sel(self.ident, -1, 1, ALU.is_equal)
        sel(UI, 1, -1, ALU.is_ge)
        sel(US, 1, -1, ALU.is_gt)
        sel(LI, -1, 1, ALU.is_ge)
        sel(LS, -1, 1, ALU.is_gt)
        self.incl_st = [UI, LI]
        self.strict_st = [US, LS]
        self.strict_ts = [LS, US]
        BD = {}
        for nb_, bs in ((4, 32), (2, 64)):
            E = t("E%d" % bs, [nb_, 128])
            self.p.op("pool", lambda e, E=E, nb_=nb_, bs=bs: e.affine_select(
                out=E[:], in_=self.ones[0:nb_, :], pattern=[[1, 128]], compare_op=ALU.is_ge, fill=0.0, base=0,
                channel_multiplier=-bs), reads=[self.ones], writes=[E])
            self.p.op("pool", lambda e, E=E, nb_=nb_, bs=bs: e.affine_select(
                out=E[:], in_=E[:], pattern=[[-1, 128]], compare_op=ALU.is_ge, fill=0.0, base=bs - 1,
                channel_multiplier=bs), reads=[E], writes=[E])
            pb = self.big()
            self.mm(pb[:, 0:128], E[:], E[:], True, True, [E], [pb])
            bd = t("BD%d" % bs, [128, 128])
            self.cp("dve", bd[:], pb[:, 0:128], [pb], [bd])
            BD[bs] = bd
        D64m32 = t("D64m32", [128, 128])
        self.tt("dve", D64m32[:], BD[64][:], BD[32][:], ALU.subtract, [BD[64], BD[32]], [D64m32])
        N64 = t("N64", [128, 128])
        self.ts("dve", N64[:], BD[64][:], -1.0, ALU.mult, [BD[64]], [N64], s2=1.0, op1=ALU.add)
        self.CM1, self.CM2, self.HD, self.OM, self.IB64, self.REM64 = [], [], [], [], [], []
        plt32, pge64, pge96 = t("plt32", [128, 1]), t("pge64", [128, 1]), t("pge96", [128, 1])
        for dst_, base_, cm_ in ((plt32, 31, -1), (pge64, -64, 1), (pge96, -96, 1)):
            self.p.op("pool", lambda e, dst_=dst_, base_=base_, cm_=cm_: e.affine_select(
                out=dst_[:], in_=self.ones[:, 0:1], pattern=[[0, 1]], compare_op=ALU.is_ge, fill=0.0, base=base_,
                channel_multiplier=cm_), reads=[self.ones], writes=[dst_])
        self.NBD_ts, self.C1_ts, self.C2_ts, self.C1_st, self.NBD_st = [], [], [], [], []
        for d in range(2):
            cm1 = t("CM1_%d" % d, [128, 256])
            cm2 = t("CM2_%d" % d, [128, 256])
            self.stt(cm1[:, 0:128], self.strict_st[d][:], -1.0, BD[32][:], ALU.mult, ALU.mult, [self.strict_st[d], BD[32]], [cm1])
            self.cp("dve", cm1[:, 128:256], self.incl_st[d][:], [self.incl_st[d]], [cm1])
            self.cp("dve", cm2[:, 0:128], self.strict_st[d][:], [self.strict_st[d]], [cm2])
            self.cp("dve", cm2[:, 128:256], self.incl_st[d][:], [self.incl_st[d]], [cm2])
            nbd = t("NBDts_%d" % d, [128, 128])
            self.stt(nbd[:], self.strict_ts[d][:], -1.0, BD[32][:], ALU.mult, ALU.mult, [self.strict_ts[d], BD[32]], [nbd])
            c1ts = t("C1ts_%d" % d, [128, 128])
            self.tt("dve", c1ts[:], self.strict_ts[d][:], D64m32[:], ALU.mult, [self.strict_ts[d], D64m32], [c1ts])
            c2ts = t("C2ts_%d" % d, [128, 128])
            self.tt("dve", c2ts[:], self.strict_ts[d][:], N64[:], ALU.mult, [self.strict_ts[d], N64], [c2ts])
            nbdst = t("NBDst_%d" % d, [128, 128])
            self.cp("dve", nbdst[:], cm1[:, 0:128], [cm1], [nbdst])
            self.NBD_st.append(nbdst)
            c1st = t("C1st_%d" % d, [128, 128])
            self.tt("dve", c1st[:], self.strict_st[d][:], D64m32[:], ALU.mult, [self.strict_st[d], D64m32], [c1st])
            self.CM1.append(cm1)
            self.CM2.append(cm2)
            self.NBD_ts.append(nbd)
            self.C1_ts.append(c1ts)
            self.C2_ts.append(c2ts)
            self.C1_st.append(c1st)
            mcol = t("mcol_%d" % d, [128, 1])
            self.tt("dve", mcol[:], plt32[:], pge64[:], ALU.add, [plt32, pge64], [mcol])
            self.tt("dve", mcol[:], mcol[:], pge96[:], ALU.subtract, [mcol, pge96], [mcol])
            if d == 1:
                self.ts("dve", mcol[:], mcol[:], -1.0, ALU.mult, [mcol], [mcol], s2=1.0, op1=ALU.add)
            midh = t("MIDH_%d" % d, [128, 128])
            self.ts("dve", midh[:], BD[64][:], mcol[:, 0:1], ALU.mult, [BD[64], mcol], [midh])
            ib = t("IB64_%d" % d, [128, 128])
            self.tt("dve", ib[:], self.incl_st[d][:], BD[64][:], ALU.mult, [self.incl_st[d], BD[64]], [ib])
            hd = t("HD_%d" % d, [128, 128])
            self.tt("dve", hd[:], ib[:], midh[:], ALU.subtract, [ib, midh], [hd])
            rem = t("REM64_%d" % d, [128, 128])
            self.tt("dve", rem[:], self.strict_ts[d][:], BD[64][:], ALU.mult, [self.strict_ts[d], BD[64]], [rem])
            om = t("OM_%d" % d, [128, 4])
            self.cp("dve", om[:, 0:1], BD[64][:, 0:1], [BD[64]], [om])
            self.cp("dve", om[:, 1:2], BD[64][:, 127:128], [BD[64]], [om])
            self.tt("dve", om[:, 2:3], om[:, 0:1], mcol[:], ALU.mult, [om, mcol], [om])
            self.tt("dve", om[:, 3:4], om[:, 1:2], mcol[:], ALU.mult, [om, mcol], [om])
            self.HD.append(hd)
            self.OM.append(om)
            self.IB64.append(ib)
            self.REM64.append(rem)
        self.sel2 = []
        for r in range(2):
            s = t("sel2_%d" % r, [2, 128])
            self.memset("pool", s[:], float(r), [s])
            self.memset("pool", s[0:1, :], float(1 - r), [s])
            self.sel2.append(s)
        self.scT = t("scT", [128, 8, 2])
        crow = tt_("crow", [2, D])
        self.ld(crow, crow[0:1, :], self.I["c"].unsqueeze(0))
        self.ld(crow, crow[1:2, :], self.I["c_ctx"].unsqueeze(0))
        self.act(crow[:], crow[:], AF.Silu, [crow], [crow])
        pb = self.big()
        for c in range(8):
            self.tr(pb[:, c * 2:c * 2 + 2], crow[:, c * 128:(c + 1) * 128], [crow], [pb])
        self.cp("dve", self.scT[:].rearrange("p c r -> p (c r)"), pb[:, 0:16], [pb], [self.scT])
        self.lbexp = tt_("lbexp", [1, 4, DI])
        self.ld(self.lbexp, self.lbexp[:].rearrange("p l c -> p (l c)"),
                self.I["hg_lb"].rearrange("l c -> (l c)").unsqueeze(0))
        self.act(self.lbexp[:], self.lbexp[:], AF.Exp, [self.lbexp], [self.lbexp])
        self.lbtot = tt_("lbtot", [1, DI])
        L = self.lbexp
        self.tt("dve", self.lbtot[:], L[:, 0, :], L[:, 1, :], ALU.add, [L], [self.lbtot])
        self.tt("dve", self.lbtot[:], self.lbtot[:], L[:, 2, :], ALU.add, [L, self.lbtot], [self.lbtot])
        self.tt("dve", self.lbtot[:], self.lbtot[:], L[:, 3, :], ALU.add, [L, self.lbtot], [self.lbtot])
        self.p.op("dve", lambda e: e.reciprocal(out=self.lbtot[:], in_=self.lbtot[:]), reads=[self.lbtot], writes=[self.lbtot])
        self.lbrow = tt_("lbrow", [1, 2, DI])
        self.tt("dve", self.lbrow[:, 0, :], L[:, 1, :], self.lbtot[:], ALU.mult, [L, self.lbtot], [self.lbrow])
        self.tt("dve", self.lbrow[:, 1, :], L[:, 1, :], L[:, 2, :], ALU.add, [L], [self.lbrow])
        self.tt("dve", self.lbrow[:, 1, :], self.lbrow[:, 1, :], L[:, 3, :], ALU.add, [L, self.lbrow], [self.lbrow])
        self.tt("dve", self.lbrow[:, 1, :], self.lbrow[:, 1, :], self.lbtot[:], ALU.mult, [self.lbrow, self.lbtot], [self.lbrow])
        self.st_(self.lbrow, self.S["lb"].unsqueeze(0), self.lbrow[:], W=[self.dbuf("lb")])

    def x_ap(self, li, b):
        if li == 0:
            if b < self.NCB:
                return self.I["ctx"][b * 128:(b + 1) * 128, :], self.dbuf("in_ctx", b)
            bb = b - self.NCB
            return self.I["x"][bb * 128:(bb + 1) * 128, :], self.dbuf("in_x", bb)
        return self.S["xs"][b * 128:(b + 1) * 128, :], self.dbuf("xs", b)

    def x_out_ap(self, li, b):
        if li == self.depth - 1 and b >= self.NCB:
            bb = b - self.NCB
            return self.out[bb * 128:(bb + 1) * 128, :], self.dbuf("out", bb)
        return self.S["xs"][b * 128:(b + 1) * 128, :], self.dbuf("xs", b)

    def bcast_rows(self, st, rows_buf, col0, ncol, name):
        outs = []
        for r in range(2):
            tl = self.tile(st, "%s_%d" % (name, r), [128, ncol])
            for j in range(0, ncol, 512):
                pb = self.big()
                self.mm(pb[:, 0:512], self.sel2[r][:], rows_buf[:, col0 + j:col0 + j + 512], True, True,
                        [self.sel2[r], rows_buf], [pb])
                self.cp("act", tl[:, j:j + 512], pb[:, 0:512], [pb], [tl])
            outs.append(tl)
        return outs

    def pass_H(self, li):
        with contextlib.ExitStack() as st:
            I = self.I
            mw = self.rot(st, "mw", [128, 3 * D], 2)
            banks = [self.ps[i] for i in range(6)]
            for c in range(8):
                w = mw.next()
                self.ld(w, w[:], I["mod_w"][li, c * 128:(c + 1) * 128, :])
                for j in range(6):
                    self.mm(banks[j][0:2, :], self.scT[:, c, :], w[:, j * 512:(j + 1) * 512], c == 0, c == 7,
                            [self.scT, w], [banks[j]])
            mb = self.tile(st, "mb", [2, 3 * D])
            self.ld(mb, mb[:], I["mod_b"][li].partition_broadcast(2))
            self.modrow = self.tile(st, "modrow", [2, 3 * D])
            for j in range(6):
                self.tt("dve", self.modrow[:, j * 512:(j + 1) * 512], banks[j][0:2, :], mb[:, j * 512:(j + 1) * 512],
                        ALU.add, [banks[j], mb], [self.modrow])
            self.st_(self.modrow, self.S["mod"], self.modrow[:], W=[self.dbuf("mod")])
            pg = self.tile(st, "pg", [2, D])
            self.ld(pg, pg[:], I["pre_g"][li].partition_broadcast(2))
            grow = self.tile(st, "grow", [2, D])
            self.stt(grow[:], self.modrow[:, D:2 * D], 1.0, pg[:], ALU.add, ALU.mult, [self.modrow, pg], [grow])
            G = self.bcast_rows(st, grow, 0, D, "G")
            Sh = self.bcast_rows(st, self.modrow, 0, D, "Sh")
            xt = self.rot(st, "xt", [128, D], 2)
            ht = self.rot(st, "ht", [128, D], 2)
            hTt = self.rot(st, "hTt", [128, 8, 128], 2)
            sst = self.rot(st, "ss", [128, 2], 2)
            junk = self.tile(st, "junk", [128, D])
            for b in range(self.NB):
                seg = 1 if b < self.NCB else 0
                xap, xb = self.x_ap(li, b)
                x = xt.next()
                self.ld(x, x[:], xap, R=[xb])
                ss = sst.next()
                self.act(junk[:], x[:], AF.Square, [x], [junk, ss], accum=ss[:, 0:1])
                self.act(ss[:, 1:2], ss[:, 0:1], AF.Ln, [ss], [ss], scale=1.0 / D, bias=NORM_EPS)
                self.act(ss[:, 1:2], ss[:, 1:2], AF.Exp, [ss], [ss], scale=-0.5)
                h = ht.next()
                self.stt(h[:], x[:], ss[:, 1:2], G[seg][:], ALU.mult, ALU.mult, [x, ss, G[seg]], [h])
                self.tt("pool", h[:], h[:], Sh[seg][:], ALU.add, [h, Sh[seg]], [h])
                hT = hTt.next()
                for g in range(2):
                    pb = self.big()
                    for c in range(4):
                        cc = g * 4 + c
                        self.tr(pb[:, c * 128:(c + 1) * 128], h[:, cc * 128:(cc + 1) * 128], [h], [pb])
                    self.cp("act" if g == 0 else "dve", hT[:, g * 4:(g + 1) * 4, :].rearrange("p c t -> p (c t)"),
                            pb[:, :], [pb], [hT])
                self.st_(hT, self.S["hT"].rearrange("(c p) t -> p c t", p=128)[:, :, b * 128:(b + 1) * 128], hT[:],
                         W=[self.dbuf("hT", b)])
            self.p.flush()

    def rw_pass_A(self, li):
        j = li // 2
        I, S = self.I, self.S
        NCB = self.NCB
        with contextlib.ExitStack() as st:
            t = lambda n, s: self.tile(st, n, s)
            mixrow = t("mixrow", [6, D])
            self.ld(mixrow, mixrow[:], I["rw_mix"][j])
            MIX = t("MIX", [128, 8, 6])
            pb = self.big()
            for c in range(8):
                self.tr(pb[:, c * 6:c * 6 + 6], mixrow[:, c * 128:(c + 1) * 128], [mixrow], [pb])
            self.cp("dve", MIX[:].rearrange("p c r -> p (c r)"), pb[:, 0:48], [pb], [MIX])
            W1 = t("W1", [128, 8, 128])
            A1 = t("A1", [128, 8, 128])
            for d in range(2):
                self.ld(W1, W1[:, :, d * 64:(d + 1) * 64], I["rw_w1"][j, d].rearrange("(c p) r -> p c r", p=128))
                self.ld(A1, A1[:, :, d * 64:(d + 1) * 64], I["rw_a1"][j, d].rearrange("(c p) r -> p c r", p=128))
            if j > 0:
                V1 = t("V1", [128, 8, 32])
                self.ld(V1, V1[:], I["rw_v1"][j - 1].rearrange("(c p) r -> p c r", p=128))
            TW = [t("TW%d" % d, [65, 128]) for d in range(2)]
            TA = [t("TA%d" % d, [65, 128]) for d in range(2)]
            TV = t("TV", [33, 128])
            for x_ in TW + TA:
                self.memset("pool", x_[64:65, :], 1.0, [x_])
            self.memset("pool", TV[32:33, :], 1.0, [TV])
            HW = t("HW", [128, 8, 256])
            XX = t("XX", [128, 8, 128])
            XN = [t("XN%d" % n, [128, 8, 128]) for n in range(6)]
            WT = self.rot(st, "WT", [128, 8, 512], 2)
            LR = self.rot(st, "LR", [65, 5, 512], 2)
            VP = t("VP", [128, 3, 512])
            wk = {n: t("wk_" + n, [128, 512]) for n in
                  ("r", "k", "v", "z", "kkr", "sq", "kkn", "rk", "vf", "sigv", "sg", "a", "t1", "kd", "bb",
                   "EI", "EnI", "ES", "ER", "o1", "o2", "o3", "o4", "o5", "o6", "prod")}
            sm = {n: t("sm_" + n, [128, 8]) for n in ("ss", "rn", "bon0", "bon1", "bon")}
            TT = {n: t("TT_" + n, [64, 8, 128]) for n in ("kT", "bT", "ktT", "rT")}
            PCt = t("PCt", [64, 8])

            def v3(ap):
                return ap.rearrange("p (h k) -> p h k", k=64)

            wbuf, lrbuf = {}, {}
            npieces = self.NB * 16

            def wload(k):
                if k >= npieces or k in wbuf:
                    return
                cg_, n_ = (k // 4) % 4, k % 4
                w = WT.next()
                self.ld(w, w[:], I["rw_proj"][j, n_, :, cg_ * 512:(cg_ + 1) * 512].rearrange("(c p) n -> p c n", p=128))
                wbuf[k] = w

            def lrload(k):
                if k >= self.NB * 4 or k in lrbuf:
                    return
                cs_ = slice((k % 4) * 512, (k % 4 + 1) * 512)
                lr = LR.next()
                for d in range(2):
                    self.ld(lr, lr[0:64, d, :], I["rw_w2"][j, d, :, cs_])
                    self.ld(lr, lr[64:65, d, :], I["rw_w0"][j, d, cs_].unsqueeze(0))
                    self.ld(lr, lr[0:64, 2 + d, :], I["rw_a2"][j, d, :, cs_])
                    self.ld(lr, lr[64:65, 2 + d, :], I["rw_a0"][j, d, cs_].unsqueeze(0))
                if j > 0:
                    self.ld(lr, lr[0:32, 4, :], I["rw_v2"][j - 1, :, cs_])
                    self.ld(lr, lr[32:33, 4, :], I["rw_v0"][j - 1, cs_].unsqueeze(0))
                lrbuf[k] = lr
            wload(0)
            wload(1)
            lrload(0)

            for b in range(self.NB):
                ctx = b < NCB
                s0, s1 = (0, NCB * 128) if ctx else (NCB * 128, self.T)
                w0, w1 = b * 128 - 64, b * 128 + 192
                v0, v1_ = max(w0, s0), min(w1, s1)
                if v0 > w0:
                    self.memset("pool", HW[:, :, 0:v0 - w0], 0.0, [HW])
                if v1_ < w1:
                    self.memset("pool", HW[:, :, v1_ - w0:256], 0.0, [HW])
                nb_lo, nb_hi = v0 // 128, (v1_ - 1) // 128
                self.ld(HW, HW[:, :, v0 - w0:v1_ - w0],
                        S["hT"].rearrange("(c p) t -> p c t", p=128)[:, :, v0:v1_],
                        R=[self.dbuf("hT", q) for q in range(nb_lo, nb_hi + 1)])
                hc = HW[:, :, 64:192]
                if ctx:
                    self.tt("dve", XX[:, 0:4, :], HW[:, 0:4, 63:191], HW[:, 0:4, 64:192], ALU.subtract, [HW], [XX])
                    self.tt("pool", XX[:, 4:8, :], HW[:, 4:8, 65:193], HW[:, 4:8, 64:192], ALU.subtract, [HW], [XX])
                else:
                    self.tt("dve", XX[:, 0:2, :], HW[:, 0:2, 63:191], HW[:, 0:2, 64:192], ALU.subtract, [HW], [XX])
                    self.tt("pool", XX[:, 2:4, :], HW[:, 2:4, 65:193], HW[:, 2:4, 64:192], ALU.subtract, [HW], [XX])
                    self.tt("dve", XX[:, 4:6, :], HW[:, 4:6, 0:128], HW[:, 4:6, 64:192], ALU.subtract, [HW], [XX])
                    self.tt("pool", XX[:, 6:8, :], HW[:, 6:8, 128:256], HW[:, 6:8, 64:192], ALU.subtract, [HW], [XX])
                    self.ts("dve", XX[:, 0:2, 0:128:64], HW[:, 0:2, 64:192:64], -1.0, ALU.mult, [HW], [XX])
                    self.ts("dve", XX[:, 2:4, 63:128:64], HW[:, 2:4, 127:192:64], -1.0, ALU.mult, [HW], [XX])
                for n in range(6):
                    for c in range(8):
                        self.stt(XN[n][:, c, :], XX[:, c, :], MIX[:, c, n:n + 1], HW[:, c, 64:192],
                                 ALU.mult, ALU.add, [XX, MIX, HW], [XN[n]])
                xr, xw, xk, xv, xa, xg = XN
                for d in range(2):
                    ph = self.half()
                    for c in range(8):
                        self.mm(ph[0:64, 0:128], W1[:, c, d * 64:(d + 1) * 64], xw[:, c, :], c == 0, c == 7, [W1, xw], [ph])
                    self.act(TW[d][0:64, :], ph[0:64, 0:128], AF.Tanh, [ph], [TW[d]])
                    ph = self.half()
                    for c in range(8):
                        self.mm(ph[0:64, 0:128], A1[:, c, d * 64:(d + 1) * 64], xa[:, c, :], c == 0, c == 7, [A1, xa], [ph])
                    self.cp("dve", TA[d][0:64, :], ph[0:64, 0:128], [ph], [TA[d]])
                if j > 0:
                    ph = self.half()
                    for c in range(8):
                        self.mm(ph[0:32, 0:128], V1[:, c, :], xv[:, c, :], c == 0, c == 7, [V1, xv], [ph])
                    self.cp("dve", TV[0:32, :], ph[0:32, 0:128], [ph], [TV])
                tok = slice(b * 128, (b + 1) * 128)
                for cg in range(4):
                    cs = slice(cg * 512, (cg + 1) * 512)
                    for i_, nm in enumerate(("rw_kk", "rw_ka", "rw_rk")):
                        self.ld(VP, VP[:, i_, :], I[nm][j, cs].partition_broadcast(128))
                    lrload(b * 4 + cg + 1)
                    lr = lrbuf.pop(b * 4 + cg)
                    pbs = []
                    for n_, xin in enumerate((xr, xk, xv, xg)):
                        kpiece = (b * 4 + cg) * 4 + n_
                        w = wbuf.pop(kpiece)
                        pb = self.big()
                        for c in range(8):
                            self.mm(pb[:, :], xin[:, c, :], w[:, c, :], c == 0, c == 7, [xin, w], [pb])
                        wload(kpiece + 2)
                        pbs.append(pb)
                        dst = wk[("r", "k", "v", "z")[n_]]
                        if n_ == 3:
                            self.act(dst[:], pb[:, :], AF.Silu, [pb], [dst])
                        else:
                            self.cp("act", dst[:], pb[:, :], [pb], [dst])
                    self.st_(wk["z"], S["z"][tok, cs], wk["z"][:], W=[self.dbuf("z", b, cg)])
                    if j == 0:
                        self.st_(wk["v"], S["vfirst"][tok, cs], wk["v"][:], W=[self.dbuf("vfirst", b, cg)])
                    else:
                        self.ld(wk["vf"], wk["vf"][:], S["vfirst"][tok, cs], R=[self.dbuf("vfirst", b, cg)])
                        pb = self.big()
                        self.mm(pb[:, :], TV[0:33, :], lr[0:33, 4, :], True, True, [TV, lr], [pb])
                        self.act(wk["sigv"][:], pb[:, :], AF.Sigmoid, [pb], [wk["sigv"]])
                        self.tt("pool", wk["vf"][:], wk["vf"][:], wk["v"][:], ALU.subtract, [wk["vf"], wk["v"]], [wk["vf"]])
                        self.tt("dve", wk["vf"][:], wk["vf"][:], wk["sigv"][:], ALU.mult, [wk["vf"], wk["sigv"]], [wk["vf"]])
                        self.tt("pool", wk["v"][:], wk["v"][:], wk["vf"][:], ALU.add, [wk["v"], wk["vf"]], [wk["v"]])
                    self.st_(wk["v"], S["v"][tok, cs], wk["v"][:], W=[self.dbuf("v", b, cg)])
                    self.tt("pool", wk["kkr"][:], wk["k"][:], VP[:, 0, :], ALU.mult, [wk["k"], VP], [wk["kkr"]])
                    self.tt("pool", wk["sq"][:], wk["kkr"][:], wk["kkr"][:], ALU.mult, [wk["kkr"]], [wk["sq"]])
                    self.red(sm["ss"][:], v3(wk["sq"][:]), [wk["sq"]], [sm["ss"]])
                    self.ts("dve", sm["ss"][:], sm["ss"][:], 1e-24, ALU.max, [sm["ss"]], [sm["ss"]])
                    self.act(sm["rn"][:], sm["ss"][:], AF.Ln, [sm["ss"]], [sm["rn"]])
                    self.act(sm["rn"][:], sm["rn"][:], AF.Exp, [sm["rn"]], [sm["rn"]], scale=-0.5)
                    self.tt("dve", v3(wk["kkn"][:]), v3(wk["kkr"][:]), sm["rn"][:].unsqueeze(2).broadcast_to([128, 8, 64]),
                            ALU.mult, [wk["kkr"], sm["rn"]], [wk["kkn"]])
                    self.tt("pool", wk["rk"][:], wk["r"][:], VP[:, 2, :], ALU.mult, [wk["r"], VP], [wk["rk"]])
                    for d in range(2):
                        pu = self.big()
                        self.mm(pu[:, :], TW[d][0:65, :], lr[0:65, d, :], True, True, [TW[d], lr], [pu])
                        self.act(wk["sg"][:], pu[:, :], AF.Sigmoid, [pu], [wk["sg"]])
                        pa = self.big()
                        self.mm(pa[:, :], TA[d][0:65, :], lr[0:65, 2 + d, :], True, True, [TA[d], lr], [pa])
                        self.act(wk["a"][:], pa[:, :], AF.Sigmoid, [pa], [wk["a"]])
                        self.stt(wk["t1"][:], wk["a"][:], -1.0, VP[:, 1, :], ALU.add, ALU.mult, [wk["a"], VP], [wk["t1"]])
                        self.stt(wk["kd"][:], wk["t1"][:], 1.0, wk["k"][:], ALU.add, ALU.mult, [wk["t1"], wk["k"]], [wk["kd"]])
                        self.tt("pool", wk["prod"][:], wk["rk"][:], wk["kd"][:], ALU.mult, [wk["rk"], wk["kd"]], [wk["prod"]])
                        bon = sm["bon%d" % d]
                        self.red(bon[:], v3(wk["prod"][:]), [wk["prod"]], [bon])
                        self.tt("pool", wk["bb"][:], wk["a"][:], wk["kkn"][:], ALU.mult, [wk["a"], wk["kkn"]], [wk["bb"]])
                        sg = wk["sg"]
                        pI = self.big()
                        self.mm(pI[:, :], self.incl_st[d][:], sg[:], True, True, [self.incl_st[d], sg], [pI])
                        self.act(wk["EI"][:], pI[:, :], AF.Exp, [pI], [wk["EI"]], scale=WSCALE)
                        self.act(wk["EnI"][:], pI[:, :], AF.Exp, [pI], [wk["EnI"]], scale=-WSCALE)
                        pS = self.big()
                        self.mm(pS[:, :], self.strict_st[d][:], sg[:], True, True, [self.strict_st[d], sg], [pS])
                        self.act(wk["ES"][:], pS[:, :], AF.Exp, [pS], [wk["ES"]], scale=WSCALE)
                        pR = self.big()
                        self.mm(pR[:, :], self.strict_ts[d][:], sg[:], True, True, [self.strict_ts[d], sg], [pR])
                        self.act(wk["ER"][:], pR[:, :], AF.Exp, [pR], [wk["ER"]], scale=WSCALE)
                        self.tt("dve", wk["o1"][:], wk["kkn"][:], wk["ES"][:], ALU.mult, [wk["kkn"], wk["ES"]], [wk["o1"]])
                        self.tt("pool", wk["o2"][:], wk["bb"][:], wk["EnI"][:], ALU.mult, [wk["bb"], wk["EnI"]], [wk["o2"]])
                        self.tt("dve", wk["o3"][:], wk["kd"][:], wk["EnI"][:], ALU.mult, [wk["kd"], wk["EnI"]], [wk["o3"]])
                        self.tt("pool", wk["o4"][:], wk["r"][:], wk["EI"][:], ALU.mult, [wk["r"], wk["EI"]], [wk["o4"]])
                        self.tt("dve", wk["o5"][:], wk["bb"][:], wk["ER"][:], ALU.mult, [wk["bb"], wk["ER"]], [wk["o5"]])
                        self.tt("pool", wk["o6"][:], wk["kd"][:], wk["ER"][:], ALU.mult, [wk["kd"], wk["ER"]], [wk["o6"]])
                        self.st_(wk["o5"], S["bh"][d, tok, cs], wk["o5"][:], W=[self.dbuf("bh", d, b, cg)])
                        self.st_(wk["o6"], S["kh"][d, tok, cs], wk["o6"][:], W=[self.dbuf("kh", d, b, cg)])
                        for src, nm in ((wk["o1"], "kT"), (wk["o2"], "bT"), (wk["o3"], "ktT"), (wk["o4"], "rT")):
                            dst = TT[nm]
                            for g in range(2):
                                pb = self.big()
                                for hh in range(4):
                                    h8 = g * 4 + hh
                                    self.tr(pb[0:64, hh * 128:(hh + 1) * 128], src[:, h8 * 64:(h8 + 1) * 64], [src], [pb])
                                self.cp("act" if g == 0 else "dve", dst[:, g * 4:(g + 1) * 4, :].rearrange("p h t -> p (h t)"),
                                        pb[0:64, :], [pb], [dst])
                            self.st_(dst, S[nm][d, b, cg * 8:(cg + 1) * 8].rearrange("h k t -> k h t"), dst[:],
                                     W=[self.dbuf(nm, d, b, cg)])
                        ph = self.half()
                        for h8 in range(8):
                            self.mm(ph[0:64, h8:h8 + 1], sg[:, h8 * 64:(h8 + 1) * 64], self.ones[:, 0:1], True, True,
                                    [sg, self.ones], [ph])
                        self.act(PCt[:], ph[0:64, 0:8], AF.Exp, [ph], [PCt], scale=WSCALE)
                        self.st_(PCt, S["pc"][d, b, :, cg * 8:(cg + 1) * 8], PCt[:], W=[self.dbuf("pc", d, b, cg)])
                    self.tt("dve", sm["bon"][:], sm["bon0"][:], sm["bon1"][:], ALU.add, [sm["bon0"], sm["bon1"]], [sm["bon"]])
                    self.st_(sm["bon"], S["bon"][tok, cg * 8:(cg + 1) * 8], sm["bon"][:], W=[self.dbuf("bon", b, cg)])
            self.p.flush()

    def block_order(self, d):
        NCB, NB = self.NCB, self.NB
        if d == 0:
            return list(range(NB))
        return list(range(NCB - 1, -1, -1)) + list(range(NB - 1, NCB - 1, -1))

    def rw_pass_B(self, li):
        S = self.S
        with contextlib.ExitStack() as st:
            t = lambda n, s: self.tile(st, n, s)
            allps = Rot(self.ps)
            ps = allps.next
            H = [[t("H%d_%d" % (d, g), [64, 8, 64]) for g in range(4)] for d in range(2)]
            for d in range(2):
                for g in range(4):
                    self.memset("pool", H[d][g][:], 0.0, [H[d][g]])
            KR = self.rot(st, "KR", [64, 8, 2, 128], 2)
            BT = self.rot(st, "BT", [64, 8, 128], 2)
            KtT = self.rot(st, "KtT", [64, 8, 128], 2)
            BH = self.rot(st, "BH", [128, 512], 2)
            KH = self.rot(st, "KH", [128, 512], 2)
            V = self.rot(st, "V", [128, 512], 2)
            PC = self.rot(st, "PC", [64, 8], 2)
            MF = self.rot(st, "MF", [128, 8, 128], 2)
            LKA = self.rot(st, "LKA", [128, 8, 256], 2)
            ARB = self.rot(st, "ARB", [128, 8, 128], 2)
            Rsb = self.rot(st, "Rsb", [128, 512], 2)
            Usb = self.rot(st, "Usb", [128, 512], 2)
            Y = self.rot(st, "Y", [128, 512], 2)
            Htmp = self.rot(st, "Htmp", [64, 512], 2)
            GL = [[t("GL%d_%d" % (g, i), [128, 4, 128]) for i in range(3)] for g in range(2)]
            GR = [self.rot(st, "GR%d_" % g, [128, 4, 128], 8) for g in range(2)]

            def flat(b_):
                return b_[:].rearrange("p h c -> p (h c)")

            def b4(m_, n=4):
                return m_[:].unsqueeze(1).broadcast_to([128, n, 128])

            def group_gen(d, kr, bt, ktt, lka, arb, mf, g4):
                c1t, c1, c2 = GL[g4]
                gr = GR[g4]
                heads = [g4 * 4 + q for q in range(4)]
                h0 = heads[0]
                p1 = [ps(), ps()]
                for q, h in enumerate(heads):
                    krh = kr[:, h, :, :].rearrange("k a t -> k (a t)")
                    self.mm(p1[q // 2][:, (q % 2) * 256:(q % 2 + 1) * 256], bt[:, h, :], krh, True, True, [bt, kr], [p1[q // 2]])
                qT = gr.next()
                for k in range(2):
                    pv = p1[k][:, :].rearrange("p (h c) -> p h c", c=256)
                    self.tt("dve", qT[:, 2 * k:2 * k + 2, :], pv[:, :, 0:128], b4(self.NBD_st[d], 2), ALU.mult,
                            [p1[k], self.NBD_st[d]], [qT])
                    self.tt("dve", arb[:, h0 + 2 * k:h0 + 2 * k + 2, :], pv[:, :, 128:256], b4(self.incl_st[d], 2), ALU.mult,
                            [p1[k], self.incl_st[d]], [arb])
                    self.tt("dve", c1t[:, 2 * k:2 * k + 2, :], pv[:, :, 0:128], b4(self.C1_st[d], 2), ALU.mult,
                            [p1[k], self.C1_st[d]], [c1t])
                yield
                p2 = [ps(), ps()]
                for q, h in enumerate(heads):
                    krh = kr[:, h, :, :].rearrange("k a t -> k (a t)")
                    self.mm(p2[q // 2][:, (q % 2) * 256:(q % 2 + 1) * 256], ktt[:, h, :], krh, True, True, [ktt, kr], [p2[q // 2]])
                for k in range(2):
                    self.tt("dve", lka[:, h0 + 2 * k:h0 + 2 * k + 2, :], p2[k][:, :].rearrange("p (h c) -> p h c", c=256),
                            self.CM2[d][:].unsqueeze(1).broadcast_to([128, 2, 256]), ALU.mult, [p2[k], self.CM2[d]], [lka])
                yield
                p3 = ps()
                for q, h in enumerate(heads):
                    self.mm(p3[:, q * 128:(q + 1) * 128], kr[:, h, 0, :], bt[:, h, :], True, True, [kr, bt], [p3])
                p3v = p3[:, :].rearrange("p (h c) -> p h c", c=128)
                q0 = gr.next()
                self.tt("dve", q0[:], p3v, b4(self.NBD_ts[d]), ALU.mult, [p3, self.NBD_ts[d]], [q0])
                self.tt("dve", c1[:], p3v, b4(self.C1_ts[d]), ALU.mult, [p3, self.C1_ts[d]], [c1])
                self.tt("dve", c2[:], p3v, b4(self.C2_ts[d]), ALU.mult, [p3, self.C2_ts[d]], [c2])
                m = gr.next()
                self.tt("pool", m[:], qT[:], b4(self.ident), ALU.add, [qT, self.ident], [m])
                yield
                Q, QT = q0, qT
                for lvl in range(1, 5):
                    pq = ps()
                    for q in range(4):
                        self.mm(pq[:, q * 128:(q + 1) * 128], QT[:, q, :], Q[:, q, :], True, True, [QT, Q], [pq])
                    qn = gr.next()
                    self.cp("act", flat(qn), pq[:, :], [pq], [qn])
                    qtn = None
                    if lvl < 4:
                        pqt = ps()
                        for q in range(4):
                            self.mm(pqt[:, q * 128:(q + 1) * 128], Q[:, q, :], QT[:, q, :], True, True, [QT, Q], [pqt])
                        qtn = gr.next()
                        self.cp("act", flat(qtn), pqt[:, :], [pqt], [qtn])
                    yield
                    pm = ps()
                    for q in range(4):
                        self.mm(pm[:, q * 128:(q + 1) * 128], qn[:, q, :], m[:, q, :], True, True, [qn, m], [pm])
                    mn = gr.next()
                    self.tt("dve", flat(mn), pm[:, :], flat(m), ALU.add, [pm, m], [mn])
                    m = mn
                    Q, QT = qn, qtn
                    yield
                Tt = m
                pt = ps()
                for q in range(4):
                    self.tr(pt[:, q * 128:(q + 1) * 128], Tt[:, q, :], [Tt], [pt])
                Tn = gr.next()
                self.cp("act", flat(Tn), pt[:, :], [pt], [Tn])
                yield

                def step(lhs, rhs):
                    pp = ps()
                    for q in range(4):
                        self.mm(pp[:, q * 128:(q + 1) * 128], lhs[:, q, :], rhs[:, q, :], True, True, [lhs, rhs], [pp])
                    return pp
                py1 = step(c1t, Tn)
                y1 = gr.next()
                self.cp("act", flat(y1), py1[:, :], [py1], [y1])
                yield
                pz1 = step(Tt, y1)
                T64 = gr.next()
                self.tt("dve", flat(T64), flat(Tn), pz1[:, :], ALU.subtract, [Tn, pz1], [T64])
                yield
                py2 = step(c1, Tt)
                y2 = gr.next()
                self.cp("act", flat(y2), py2[:, :], [py2], [y2])
                yield
                pz2 = step(Tn, y2)
                Tt64 = gr.next()
                self.tt("dve", flat(Tt64), flat(Tt), pz2[:, :], ALU.subtract, [Tt, pz2], [Tt64])
                yield
                py3 = step(c2, Tt64)
                y3 = gr.next()
                self.cp("act", flat(y3), py3[:, :], [py3], [y3])
                yield
                pz3 = step(T64, y3)
                self.tt("dve", mf[:, h0:h0 + 4, :].rearrange("p h c -> p (h c)"), flat(Tt64), pz3[:, :], ALU.subtract,
                        [Tt64, pz3], [mf])
                yield

            def seq_gen(d, b, cg, kr, bh, kh, v, pc, lka, arb, mf):
                tok = slice(b * 128, (b + 1) * 128)
                cs = slice(cg * 512, (cg + 1) * 512)
                Hd = H[d][cg]
                pr = ps()
                for h8 in range(8):
                    o = pr[:, h8 * 64:(h8 + 1) * 64]
                    self.mm(o, kr[:, h8, 0, :], Hd[:, h8, :], True, False, [kr, Hd], [pr])
                    self.mm(o, lka[:, h8, 0:128], v[:, h8 * 64:(h8 + 1) * 64], False, True, [lka, v], [pr])
                rsb = Rsb.next()
                self.act(rsb[:], pr[:, :], AF.Copy, [pr], [rsb], scale=-1.0)
                yield
                pu = ps()
                for h8 in range(8):
                    self.mm(pu[:, h8 * 64:(h8 + 1) * 64], mf[:, h8, :], rsb[:, h8 * 64:(h8 + 1) * 64], True, True,
                            [mf, rsb], [pu])
                usb = Usb.next()
                self.cp("dve", usb[:], pu[:, :], [pu], [usb])
                yield
                py = ps()
                pn = ps()
                for h8 in range(8):
                    o = py[:, h8 * 64:(h8 + 1) * 64]
                    hsl = slice(h8 * 64, (h8 + 1) * 64)
                    self.mm(o, kr[:, h8, 1, :], Hd[:, h8, :], True, False, [kr, Hd], [py])
                    self.mm(o, arb[:, h8, :], usb[:, hsl], False, False, [arb, usb], [py])
                    self.mm(o, lka[:, h8, 128:256], v[:, hsl], False, True, [lka, v], [py])
                    o2 = pn[0:64, hsl]
                    self.mm(o2, bh[:, hsl], usb[:, hsl], True, False, [bh, usb], [pn])
                    self.mm(o2, kh[:, hsl], v[:, hsl], False, True, [kh, v], [pn])
                y = Y.next()
                self.cp("act", y[:], py[:, :], [py], [y])
                self.st_(y, S["yd"][d, tok, cs], y[:], W=[self.dbuf("yd", d, b, cg)])
                ht_ = Htmp.next()
                self.tt("pool", ht_[:].rearrange("k (h v) -> k h v", v=64), Hd[:, :, :],
                        pc[:].unsqueeze(2).broadcast_to([64, 8, 64]), ALU.mult, [Hd, pc], [ht_])
                self.tt("dve", Hd[:, :, :], ht_[:].rearrange("k (h v) -> k h v", v=64),
                        pn[0:64, :].rearrange("k (h v) -> k h v", v=64), ALU.add, [ht_, pn], [Hd])
                yield

            active = []

            def run_round():
                for g in list(active):
                    try:
                        next(g)
                    except StopIteration:
                        active.remove(g)

            orders = [self.block_order(0), self.block_order(1)]
            for step_i in range(self.NB):
                for cg in range(4):
                    for d in range(2):
                        b = orders[d][step_i]
                        tok = slice(b * 128, (b + 1) * 128)
                        cs = slice(cg * 512, (cg + 1) * 512)
                        hs = slice(cg * 8, (cg + 1) * 8)
                        kr, bt, ktt, bh, kh, v, pc = KR.next(), BT.next(), KtT.next(), BH.next(), KH.next(), V.next(), PC.next()
                        self.ld(kr, kr[:, :, 0, :], S["kT"][d, b, hs].rearrange("h k t -> k h t"), R=[self.dbuf("kT", d, b, cg)])
                        self.ld(kr, kr[:, :, 1, :], S["rT"][d, b, hs].rearrange("h k t -> k h t"), R=[self.dbuf("rT", d, b, cg)])
                        self.ld(bt, bt[:], S["bT"][d, b, hs].rearrange("h k t -> k h t"), R=[self.dbuf("bT", d, b, cg)])
                        self.ld(ktt, ktt[:], S["ktT"][d, b, hs].rearrange("h k t -> k h t"), R=[self.dbuf("ktT", d, b, cg)])
                        self.ld(bh, bh[:], S["bh"][d, tok, cs], R=[self.dbuf("bh", d, b, cg)])
                        self.ld(kh, kh[:], S["kh"][d, tok, cs], R=[self.dbuf("kh", d, b, cg)])
                        self.ld(v, v[:], S["v"][tok, cs], R=[self.dbuf("v", b, cg)])
                        self.ld(pc, pc[:], S["pc"][d, b, :, hs], R=[self.dbuf("pc", d, b, cg)])
                        mf, lka, arb = MF.next(), LKA.next(), ARB.next()
                        g1 = group_gen(d, kr, bt, ktt, lka, arb, mf, 0)
                        g2 = group_gen(d, kr, bt, ktt, lka, arb, mf, 1)
                        active.extend([g1, g2])
                        while g1 in active or g2 in active:
                            run_round()
                        active.append(seq_gen(d, b, cg, kr, bh, kh, v, pc, lka, arb, mf))
            while active:
                run_round()
            self.p.flush()

    def pass_C(self, li, rw):
        j = li // 2
        I, S = self.I, self.S
        last = li == self.depth - 1
        with contextlib.ExitStack() as st:
            t = lambda n, s: self.tile(st, n, s)
            WO = t("WO", [128, 16, D])
            wsrc = I["rw_wo"][j] if rw else I["hg_wo"][j]
            for c4 in range(4):
                self.ld(WO, WO[:, c4 * 4:(c4 + 1) * 4, :],
                        wsrc[c4 * 512:(c4 + 1) * 512, :].rearrange("(c p) n -> p c n", p=128))
            pg = t("pg", [2, D])
            self.ld(pg, pg[:], I["post_g"][li].partition_broadcast(2))
            gprow = t("gprow", [2, D])
            gate = t("gate", [2, D])
            self.ld(gate, gate[:], S["mod"][:, 2 * D:3 * D], R=[self.dbuf("mod")])
            self.tt("dve", gprow[:], gate[:], pg[:], ALU.mult, [gate, pg], [gprow])
            GP = self.bcast_rows(st, gprow, 0, D, "GP")
            if rw:
                LNW = t("LNW", [128, DI])
                LNB = t("LNB", [128, DI])
                self.ld(LNW, LNW[:], I["rw_lnw"][j].partition_broadcast(128))
                self.ld(LNB, LNB[:], I["rw_lnb"][j].partition_broadcast(128))
            else:
                GN = t("GN", [128, 128])
                self.ld(GN, GN[:], I["hg_gn"][j].partition_broadcast(128))
            yf = self.rot(st, "yf", [128, 512], 2)
            yb = self.rot(st, "yb", [128, 512], 2)
            zt = self.rot(st, "zt", [128, 512], 2)
            vt = self.rot(st, "vt", [128, 512], 2)
            bont = self.rot(st, "bont", [128, 8], 2)
            w1 = t("w1", [128, 512])
            w2 = t("w2", [128, 512])
            w3 = t("w3", [128, 512])
            smA = t("smA", [128, 8])
            smB = t("smB", [128, 8])
            smC = t("smC", [128, 8])
            yzT = self.rot(st, "yzT", [128, 4, 128], 2)
            xt = self.rot(st, "xc", [128, D], 2)
            ym = t("ym", [128, D])
            junk = t("junkc", [128, D])
            ss = t("ssc", [128, 2])
            G_ = 64 if rw else 128
            ng = 512 // G_

            def v3(ap):
                return ap.rearrange("p (h k) -> p h k", k=G_)

            def bc(ap):
                return ap.unsqueeze(2).broadcast_to([128, ng, G_])
            acc = [self.ps[6], self.ps[7]]
            wsets = [(w1, w2, w3, smA, smB, smC),
                     (t("w1b", [128, 512]), t("w2b", [128, 512]), t("w3b", [128, 512]),
                      t("smAb", [128, 8]), t("smBb", [128, 8]), t("smCb", [128, 8]))]
            chains = []

            def run_round():
                for g in list(chains):
                    try:
                        next(g)
                    except StopIteration:
                        chains.remove(g)

            def cg_gen(b, cg, ws, cnt):
                w1, w2, w3, smA, smB, smC = ws
                tok = slice(b * 128, (b + 1) * 128)
                cs = slice(cg * 512, (cg + 1) * 512)
                f, bk, z = yf.next(), yb.next(), zt.next()
                self.ld(f, f[:], S["yd"][0, tok, cs], R=[self.dbuf("yd", 0, b, cg)])
                self.ld(bk, bk[:], S["yd"][1, tok, cs], R=[self.dbuf("yd", 1, b, cg)])
                self.ld(z, z[:], S["z"][tok, cs], R=[self.dbuf("z", b, cg)])
                if rw:
                    v = vt.next()
                    bon = bont.next()
                    self.ld(v, v[:], S["v"][tok, cs], R=[self.dbuf("v", b, cg)])
                    self.ld(bon, bon[:], S["bon"][tok, cg * 8:(cg + 1) * 8], R=[self.dbuf("bon", b, cg)])
                self.tt("pool", w1[:], f[:], bk[:], ALU.add, [f, bk], [w1])
                yield
                if rw:
                    self.red(smA[:], v3(w1[:]), [w1], [smA])
                    self.ts("dve", smA[:], smA[:], 1.0 / 64, ALU.mult, [smA], [smA])
                    yield
                    self.tt("dve", v3(w2[:]), v3(w1[:]), bc(smA[:]), ALU.subtract, [w1, smA], [w2])
                    self.tt("pool", w3[:], w2[:], w2[:], ALU.mult, [w2], [w3])
                    yield
                    self.red(smB[:], v3(w3[:]), [w3], [smB])
                    self.act(smC[:], smB[:], AF.Ln, [smB], [smC], scale=1.0 / 64, bias=LN_X_EPS)
                    self.act(smC[:], smC[:], AF.Exp, [smC], [smC], scale=-0.5)
                    yield
                    self.tt("dve", v3(w2[:]), v3(w2[:]), bc(smC[:]), ALU.mult, [w2, smC], [w2])
                    self.tt("pool", w2[:], w2[:], LNW[:, cs], ALU.mult, [w2, LNW], [w2])
                    self.tt("pool", w2[:], w2[:], LNB[:, cs], ALU.add, [w2, LNB], [w2])
                    yield
                    self.tt("dve", v3(w3[:]), v3(v[:]), bc(bon[:]), ALU.mult, [v, bon], [w3])
                    self.tt("pool", w2[:], w2[:], w3[:], ALU.add, [w2, w3], [w2])
                    self.tt("dve", w2[:], w2[:], z[:], ALU.mult, [w2, z], [w2])
                    yield
                else:
                    self.tt("pool", w3[:], w1[:], w1[:], ALU.mult, [w1], [w3])
                    self.red(smB[:, 0:ng], v3(w3[:]), [w3], [smB])
                    yield
                    self.act(smC[:, 0:ng], smB[:, 0:ng], AF.Ln, [smB], [smC], scale=1.0 / 128, bias=NORM_EPS)
                    self.act(smC[:, 0:ng], smC[:, 0:ng], AF.Exp, [smC], [smC], scale=-0.5)
                    yield
                    self.tt("dve", v3(w2[:]), v3(w1[:]), bc(smC[:, 0:ng]), ALU.mult, [w1, smC], [w2])
                    self.tt("pool", v3(w2[:]), v3(w2[:]), GN[:].unsqueeze(1).broadcast_to([128, ng, G_]), ALU.mult,
                            [w2, GN], [w2])
                    self.tt("dve", w2[:], w2[:], z[:], ALU.mult, [w2, z], [w2])
                    yield
                yz = yzT.next()
                pb = self.big()
                for c in range(4):
                    self.tr(pb[:, c * 128:(c + 1) * 128], w2[:, c * 128:(c + 1) * 128], [w2], [pb])
                self.cp("act", yz[:].rearrange("p c t -> p (c t)"), pb[:, :], [pb], [yz])
                yield
                for c in range(4):
                    cc = cg * 4 + c
                    first, last_ = cnt[0] == 0, cnt[0] == 15
                    cnt[0] += 1
                    for hf in range(2):
                        self.mm(acc[hf][:, :], yz[:, c, :], WO[:, cc, hf * 512:(hf + 1) * 512], first, last_,
                                [yz, WO], [acc[hf]])
                yield

            for b in range(self.NB):
                if last and b < self.NCB:
                    continue
                seg = 1 if b < self.NCB else 0
                tok = slice(b * 128, (b + 1) * 128)
                cnt = [0]
                for cg in range(4):
                    chains.append(cg_gen(b, cg, wsets[cg % 2], cnt))
                    while len(chains) >= 2:
                        run_round()
                while chains:
                    run_round()
                for hf in range(2):
                    self.cp("act", ym[:, hf * 512:(hf + 1) * 512], acc[hf][:, :], [acc[hf]], [ym])
                self.act(junk[:], ym[:], AF.Square, [ym], [junk, ss], accum=ss[:, 0:1])
                self.act(ss[:, 1:2], ss[:, 0:1], AF.Ln, [ss], [ss], scale=1.0 / D, bias=NORM_EPS)
                self.act(ss[:, 1:2], ss[:, 1:2], AF.Exp, [ss], [ss], scale=-0.5)
                self.stt(ym[:], ym[:], ss[:, 1:2], GP[seg][:], ALU.mult, ALU.mult, [ym, ss, GP[seg]], [ym])
                xap, xb = self.x_ap(li, b)
                x = xt.next()
                self.ld(x, x[:], xap, R=[xb])
                self.tt("dve", x[:], x[:], ym[:], ALU.add, [x, ym], [x])
                oap, ob = self.x_out_ap(li, b)
                self.st_(x, oap, x[:], W=[ob])
            self.p.flush()

    def hg_pass_A(self, li):
        j = li // 2
        I, S = self.I, self.S
        STH = 4
        with contextlib.ExitStack() as st:
            t = lambda n, s: self.tile(st, n, s)
            LB = t("LB", [128, DI])
            OMLB = t("OMLB", [128, DI])
            self.ld(LB, LB[:], S["lb"][j].partition_broadcast(128), R=[self.dbuf("lb")])
            self.ts("dve", OMLB[:], LB[:], -1.0, ALU.mult, [LB], [OMLB], s2=1.0, op1=ALU.add)
            HT = self.rot(st, "HT", [128, 8, 128], 2 * STH)
            WT = self.rot(st, "WTh", [128, 8, 512], 3)
            wkq = [t("hk_q%d" % i, [128, 512]) for i in range(STH)]
            sets = []
            for si in range(2):
                wk = {n: t("hk%d_%s" % (si, n), [128, 512]) for n in
                      ("v", "g", "sg", "f", "k", "lf", "E1", "E2", "E3", "o2", "o3", "o1a", "o1b")}
                wk["TTq"] = t("TTq%d" % si, [128, 4, 128])
                wk["TTqb"] = t("TTqb%d" % si, [128, 4, 128])
                wk["TTk"] = t("TTk%d" % si, [128, 4, 128])
                wk["PCM"] = t("PCM%d" % si, [128, 4, 4])
                self.memset("pool", wk["o1a"][:], 0.0, [wk["o1a"]])
                self.memset("pool", wk["o1b"][:], 0.0, [wk["o1b"]])
                sets.append(wk)
            sbs = [list(range(a, min(a + STH, self.NB))) for a in range(0, self.NB, STH)]
            pieces = []
            for si_, blks in enumerate(sbs):
                for cg in range(4):
                    for kind, col0 in (("q", 0), ("i", 3 * DI), ("g", 4 * DI), ("f0", DI), ("f1", 2 * DI)):
                        pieces.append((si_, cg, kind, col0))
            wbuf = {}

            def wload(k):
                if k >= len(pieces) or k in wbuf:
                    return
                si_, cg, kind, col0 = pieces[k]
                w = WT.next()
                self.ld(w, w[:], I["hg_win"][j, :, col0 + cg * 512:col0 + (cg + 1) * 512].rearrange("(c p) n -> p c n", p=128))
                wbuf[k] = w
            hbuf = {}

            def hload(si_):
                if si_ >= len(sbs) or si_ in hbuf:
                    return
                lst = []
                for b in sbs[si_]:
                    hT = HT.next()
                    self.ld(hT, hT[:], S["hT"].rearrange("(c p) t -> p c t", p=128)[:, :, b * 128:(b + 1) * 128],
                            R=[self.dbuf("hT", b)])
                    lst.append(hT)
                hbuf[si_] = lst
            chains = []

            def run_round():
                for g in list(chains):
                    try:
                        next(g)
                    except StopIteration:
                        chains.remove(g)

            def chain_gen(wk, d, b, cg, q_, pb):
                tok = slice(b * 128, (b + 1) * 128)
                cs = slice(cg * 512, (cg + 1) * 512)
                self.act(wk["sg"][:], pb[:, :], AF.Sigmoid, [pb], [wk["sg"]])
                yield
                self.tt("dve", wk["f"][:], wk["sg"][:], OMLB[:, cs], ALU.mult, [wk["sg"], OMLB], [wk["f"]])
                self.tt("pool", wk["f"][:], wk["f"][:], LB[:, cs], ALU.add, [wk["f"], LB], [wk["f"]])
                yield
                self.ts("dve", wk["k"][:], wk["f"][:], -1.0, ALU.mult, [wk["f"]], [wk["k"]], s2=1.0, op1=ALU.add)
                self.act(wk["lf"][:], wk["f"][:], AF.Ln, [wk["f"]], [wk["lf"]])
                yield
                lf = wk["lf"]
                p1 = self.half()
                self.mm(p1[:, :], self.HD[d][:], lf[:], True, True, [self.HD[d], lf], [p1])
                p2 = self.half()
                self.mm(p2[:, :], self.REM64[d][:], lf[:], True, True, [self.REM64[d], lf], [p2])
                ph = self.half()
                for hh in range(4):
                    self.mm(ph[:, hh * 4:hh * 4 + 4], lf[:, hh * 128:(hh + 1) * 128], self.OM[d][:], True, True,
                            [lf, self.OM[d]], [ph])
                self.act(wk["E1"][:], p1[:, :], AF.Exp, [p1], [wk["E1"]])
                self.act(wk["E2"][:], p1[:, :], AF.Exp, [p1], [wk["E2"]], scale=-1.0)
                self.act(wk["E3"][:], p2[:, :], AF.Exp, [p2], [wk["E3"]])
                PCM = wk["PCM"]
                self.act(PCM[:].rearrange("p h a -> p (h a)"), ph[:, 0:16], AF.Exp, [ph], [PCM])
                self.st_(PCM, S["pcm"][d, b, :, cg * 4:(cg + 1) * 4, :], PCM[:], W=[self.dbuf("pcm", d, b, cg)])
                yield
                o1a, o1b = wk["o1a"], wk["o1b"]
                self.tt("dve", o1a[0:64, :], q_[0:64, :], wk["E1"][0:64, :], ALU.mult, [q_, wk["E1"]], [o1a])
                self.tt("dve", o1b[64:128, :], q_[64:128, :], wk["E1"][64:128, :], ALU.mult, [q_, wk["E1"]], [o1b])
                self.tt("pool", wk["o2"][:], wk["k"][:], wk["E2"][:], ALU.mult, [wk["k"], wk["E2"]], [wk["o2"]])
                self.tt("pool", wk["o3"][:], wk["k"][:], wk["E3"][:], ALU.mult, [wk["k"], wk["E3"]], [wk["o3"]])
                self.st_(wk["o3"], S["kh"][d, tok, cs], wk["o3"][:], W=[self.dbuf("kh", d, b, cg)])
                yield
                for src, dst, nm in ((o1a, wk["TTq"], "qT"), (o1b, wk["TTqb"], "qbT"), (wk["o2"], wk["TTk"], "gkT")):
                    pbt = self.half()
                    for hh in range(4):
                        self.tr(pbt[:, hh * 128:(hh + 1) * 128], src[:, hh * 128:(hh + 1) * 128], [src], [pbt])
                    self.cp("act" if nm != "qbT" else "dve", dst[:].rearrange("p h t -> p (h t)"), pbt[:, :], [pbt], [dst])
                    self.st_(dst, S[nm][d, b, cg * 4:(cg + 1) * 4].rearrange("h k t -> k h t"), dst[:],
                             W=[self.dbuf(nm, d, b, cg)])
                    yield

            hload(0)
            wload(0)
            wload(1)
            nchain = 0
            for k, (si_, cg, kind, col0) in enumerate(pieces):
                wload(k + 2)
                if kind == "q":
                    while chains:
                        run_round()
                if cg == 0 and kind == "q":
                    hload(si_ + 1)
                w = wbuf.pop(k)
                blks = sbs[si_]
                cs = slice(cg * 512, (cg + 1) * 512)
                for bi, b in enumerate(blks):
                    tok = slice(b * 128, (b + 1) * 128)
                    hT = hbuf[si_][bi]
                    pb = self.big()
                    for c in range(8):
                        self.mm(pb[:, :], hT[:, c, :], w[:, c, :], c == 0, c == 7, [hT, w], [pb])
                    if kind == "q":
                        self.act(wkq[bi][:], pb[:, :], AF.Silu, [pb], [wkq[bi]])
                        continue
                    wk = sets[nchain % 2]
                    nchain += 1
                    if kind == "i":
                        self.cp("act", wk["v"][:], pb[:, :], [pb], [wk["v"]])
                        self.st_(wk["v"], S["v"][tok, cs], wk["v"][:], W=[self.dbuf("v", b, cg)])
                        continue
                    if kind == "g":
                        self.act(wk["g"][:], pb[:, :], AF.Silu, [pb], [wk["g"]])
                        self.st_(wk["g"], S["z"][tok, cs], wk["g"][:], W=[self.dbuf("z", b, cg)])
                        continue
                    d = 0 if kind == "f0" else 1
                    chains.append(chain_gen(wk, d, b, cg, wkq[bi], pb))
                    while len(chains) >= 2:
                        run_round()
            while chains:
                run_round()
            self.p.flush()

    def hg_pass_B(self, li):
        S = self.S
        with contextlib.ExitStack() as st:
            t = lambda n, s: self.tile(st, n, s)
            allps = Rot(self.ps)
            ps = allps.next
            ST = [[t("ST%d_%d" % (d, g), [128, 4, 128]) for g in range(4)] for d in range(2)]
            for d in range(2):
                for g in range(4):
                    self.memset("pool", ST[d][g][:], 0.0, [ST[d][g]])
            NP = 4
            QT = self.rot(st, "QT", [128, 4, 128], NP)
            QBT = self.rot(st, "QBT", [128, 4, 128], NP)
            KT = self.rot(st, "KT", [128, 4, 128], NP)
            KH = self.rot(st, "KHh", [128, 512], NP)
            V = self.rot(st, "Vh", [128, 512], NP)
            PCM = self.rot(st, "PCMh", [128, 4, 4], NP)
            AT = self.rot(st, "AT", [128, 4, 128], 4)
            SS = self.rot(st, "SS", [128, 4, 128], 6)
            SM = self.rot(st, "SM", [128, 4, 128], 4)
            TM = self.rot(st, "TM", [128, 4, 128], 4)
            Y = self.rot(st, "Yh", [128, 512], 3)

            def flat(b_):
                return b_[:].rearrange("p h c -> p (h c)")

            def unit_gen(d, b, cg, qt, qbt, kt, kh, v, pcm):
                tok = slice(b * 128, (b + 1) * 128)
                cs = slice(cg * 512, (cg + 1) * 512)
                Sd = ST[d][cg]
                f0, f1 = (0, 1) if d == 0 else (1, 0)
                r0 = slice(f0 * 64, (f0 + 1) * 64)
                r1 = slice(f1 * 64, (f1 + 1) * 64)
                qts = (qt, qbt)

                def pcb(col):
                    return pcm[:, :, col:col + 1].broadcast_to([128, 4, 128])
                pa = ps()
                for hh in range(4):
                    o = pa[:, hh * 128:(hh + 1) * 128]
                    self.mm(o, kt[:, hh, :], qt[:, hh, :], True, False, [kt, qt], [pa])
                    self.mm(o, kt[:, hh, :], qbt[:, hh, :], False, True, [kt, qbt], [pa])
                at = AT.next()
                self.tt("dve", at[:], pa[:, :].rearrange("p (h c) -> p h c", c=128),
                        self.IB64[d][:].unsqueeze(1).broadcast_to([128, 4, 128]), ALU.mult, [pa, self.IB64[d]], [at])
                s0 = SS.next()
                self.tt("pool", s0[:], Sd[:], pcb(2 + f0), ALU.mult, [Sd, pcm], [s0])
                yield
                pn0 = ps()
                for hh in range(4):
                    hsl = slice(hh * 128, (hh + 1) * 128)
                    self.mm(pn0[:, hsl], kh[r0, hsl], v[r0, hsl], True, True, [kh, v], [pn0])
                tm = TM.next()
                self.tt("pool", tm[:], Sd[:], pcb(f0), ALU.mult, [Sd, pcm], [tm])
                smid = SM.next()
                self.tt("dve", flat(smid), flat(tm), pn0[:, :], ALU.add, [tm, pn0], [smid])
                yield
                s1 = SS.next()
                self.tt("pool", s1[:], smid[:], pcb(2 + f1), ALU.mult, [smid, pcm], [s1])
                pn1 = ps()
                for hh in range(4):
                    hsl = slice(hh * 128, (hh + 1) * 128)
                    self.mm(pn1[:, hsl], kh[r1, hsl], v[r1, hsl], True, True, [kh, v], [pn1])
                tm2 = TM.next()
                self.tt("pool", tm2[:], smid[:], pcb(f1), ALU.mult, [smid, pcm], [tm2])
                self.tt("dve", flat(Sd), flat(tm2), pn1[:, :], ALU.add, [tm2, pn1], [Sd])
                yield
                py = ps()
                for hh in range(4):
                    hsl = slice(hh * 128, (hh + 1) * 128)
                    self.mm(py[:, hsl], qts[f0][:, hh, :], s0[:, hh, :], True, False, [qts[f0], s0], [py])
                    self.mm(py[:, hsl], qts[f1][:, hh, :], s1[:, hh, :], False, False, [qts[f1], s1], [py])
                    self.mm(py[:, hsl], at[:, hh, :], v[:, hsl], False, True, [at, v], [py])
                y = Y.next()
                self.cp("act", y[:], py[:, :], [py], [y])
                self.st_(y, S["yd"][d, tok, cs], y[:], W=[self.dbuf("yd", d, b, cg)])
                yield

            active = []

            def run_round():
                for g in list(active):
                    try:
                        next(g)
                    except StopIteration:
                        active.remove(g)
            orders = [self.block_order(0), self.block_order(1)]
            for step_i in range(self.NB):
                for cg in range(4):
                    gens = []
                    for d in range(2):
                        b = orders[d][step_i]
                        tok = slice(b * 128, (b + 1) * 128)
                        cs = slice(cg * 512, (cg + 1) * 512)
                        qt, qbt, kt, kh, v, pcm = QT.next(), QBT.next(), KT.next(), KH.next(), V.next(), PCM.next()
                        hsel = slice(cg * 4, (cg + 1) * 4)
                        self.ld(qt, qt[:], S["qT"][d, b, hsel].rearrange("h k t -> k h t"), R=[self.dbuf("qT", d, b, cg)])
                        self.ld(qbt, qbt[:], S["qbT"][d, b, hsel].rearrange("h k t -> k h t"), R=[self.dbuf("qbT", d, b, cg)])
                        self.ld(kt, kt[:], S["gkT"][d, b, hsel].rearrange("h k t -> k h t"), R=[self.dbuf("gkT", d, b, cg)])
                        self.ld(kh, kh[:], S["kh"][d, tok, cs], R=[self.dbuf("kh", d, b, cg)])
                        self.ld(v, v[:], S["v"][tok, cs], R=[self.dbuf("v", b, cg)])
                        self.ld(pcm, pcm[:], S["pcm"][d, b, :, hsel, :], R=[self.dbuf("pcm", d, b, cg)])
                        gens.append(unit_gen(d, b, cg, qt, qbt, kt, kh, v, pcm))
                    active.extend(gens)
                    while active:
                        run_round()
            self.p.flush()


def build_nc(rows=64, ctx_len=256, depth=4, debug=False):
    nc = bass.Bass("TRN2", target_bir_lowering=False)
    bld = Builder(nc, rows=rows, ctx_len=ctx_len, depth=depth, debug=debug)
    bld.build()
    return nc, bld


PARAM_NAMES = ["mod_w", "mod_b", "pre_g", "post_g", "rw_mix", "rw_proj", "rw_wo", "rw_w0", "rw_w1", "rw_w2",
               "rw_a0", "rw_a1", "rw_a2", "rw_v0", "rw_v1", "rw_v2", "rw_kk", "rw_ka", "rw_rk", "rw_lnw", "rw_lnb",
               "hg_win", "hg_wo", "hg_gn", "hg_lb"]


def kernel(**inputs):
    x = np.ascontiguousarray(inputs["x"], dtype=np.float32)
    B, SEQ, _ = x.shape
    ctx = np.ascontiguousarray(inputs["ctx"], dtype=np.float32)
    c = np.ascontiguousarray(inputs["c"], dtype=np.float32)
    shared = {n: np.ascontiguousarray(inputs[n], dtype=np.float32) for n in PARAM_NAMES}
    shared["c_ctx"] = np.ascontiguousarray(inputs["c_ctx"], dtype=np.float32)
    nc, _ = build_nc(rows=SEQ // 64, ctx_len=ctx.shape[1], depth=4)
    in_maps = []
    for b in range(B):
        m = dict(shared)
        m["x"] = x[b]
        m["ctx"] = ctx[b]
        m["c"] = c[b]
        in_maps.append(m)
    res = run_bass_kernel_spmd(nc, in_maps, core_ids=list(range(B)))
    return np.stack([np.asarray(r["out"]) for r in res.results], axis=0).astype(np.float32)
```

```python
import contextlib
import numpy as np
import concourse.bass as bass
import concourse.mybir as mybir
from concourse.bass_utils import run_bass_kernel_spmd

F32 = mybir.dt.float32
AF = mybir.ActivationFunctionType
ALU = mybir.AluOpType
AX = mybir.AxisListType

D = 1024
DI = 2048
NORM_EPS = 1e-6
LN_X_EPS = 64e-5
WSCALE = -0.6065306597126334


class Buf:
    __slots__ = ("name", "t", "last_w", "readers", "dsem", "dcount")

    def __init__(self, name, t=None):
        self.name = name
        self.t = t
        self.last_w = None
        self.readers = []
        self.dsem = None
        self.dcount = 0

    def __getitem__(self, idx):
        return self.t[idx]


class Rot:
    def __init__(self, bufs):
        self.bufs = bufs
        self.i = 0

    def next(self):
        b = self.bufs[self.i % len(self.bufs)]
        self.i += 1
        return b


class Prog:
    ENGS = ("pe", "act", "dve", "pool", "sp")

    def __init__(self, nc, st, ndma=40):
        self.nc = nc
        self.ops = {e: [] for e in self.ENGS}
        self.count = {e: 0 for e in self.ENGS}
        self.seen = {e: {} for e in self.ENGS}
        self.sems = {}
        for e in self.ENGS:
            self.sems[e] = st.enter_context(nc.semaphore("s_" + e))
        self.dma_sems = [st.enter_context(nc.semaphore("s_d%d" % i)) for i in range(ndma)]
        self.dma_counts = [0] * ndma
        self.dma_next = 0
        self.nops = 0
        import os
        self.maxops = int(os.environ.get("K_MAXOPS", "100000000"))

    def _deps(self, eng, reads, writes, is_pe_mm=False):
        need = {}

        def add(tok):
            if tok is None:
                return
            k, v = tok
            if k == eng and is_pe_mm:
                return
            if need.get(k, 0) < v:
                need[k] = v
        for b in reads:
            add(b.last_w)
        for b in writes:
            add(b.last_w)
            for r in b.readers:
                add(r)
        waits = []
        seen = self.seen[eng]
        for k, v in need.items():
            if seen.get(k, 0) >= v:
                continue
            seen[k] = v
            waits.append((k, v))
        return waits

    def _commit(self, tok, reads, writes):
        for b in reads:
            b.readers.append(tok)
        for b in writes:
            b.last_w = tok
            b.readers = []

    def op(self, eng, fn, reads=(), writes=(), mm=False):
        if self.nops >= self.maxops:
            return None
        waits = self._deps(eng, reads, writes, is_pe_mm=mm)
        self.count[eng] += 1
        tok = (eng, self.count[eng])
        self.ops[eng].append((waits, fn, eng, 1))
        self._commit(tok, reads, writes)
        self.nops += 1
        return tok

    def dma(self, eng, fn, sb, reads=(), writes=()):
        if self.nops >= self.maxops:
            return None
        waits = self._deps(eng, reads, writes)
        if sb.dsem is None:
            sb.dsem = self.dma_next % len(self.dma_sems)
            self.dma_next += 1
        i = sb.dsem
        self.dma_counts[i] += 16
        key = ("d", i)
        tok = (key, self.dma_counts[i])
        self.ops[eng].append((waits, fn, key, 16))
        self._commit(tok, reads, writes)
        self.nops += 1
        return tok

    def _sem(self, k):
        return self.sems[k] if isinstance(k, str) else self.dma_sems[k[1]]

    def flush(self):
        nc = self.nc
        final = [(e, self.count[e]) for e in self.ENGS if self.count[e] > 0]
        final += [(("d", i), c) for i, c in enumerate(self.dma_counts) if c > 0]
        with nc.Block() as block:
            engmap = {"pe": block.tensor, "act": block.scalar, "dve": block.vector,
                      "pool": block.gpsimd, "sp": block.sync}

            def make(ename):
                oplist = self.ops[ename]
                seen = self.seen[ename]

                def body(e):
                    for waits, fn, ik, amt in oplist:
                        for k, v in waits:
                            e.wait_ge(self._sem(k), v)
                        fn(e).then_inc(self._sem(ik), amt)
                    for k, v in final:
                        if k == ename:
                            continue
                        if seen.get(k, 0) >= v:
                            continue
                        seen[k] = v
                        e.wait_ge(self._sem(k), v)
                return body
            for ename in self.ENGS:
                engmap[ename](make(ename))
        self.ops = {e: [] for e in self.ENGS}
        self.dma_next = 0


class Builder:
    def __init__(self, nc, rows=64, ctx_len=256, depth=4, debug=False):
        self.nc = nc
        self.rows = rows
        self.LAT = 64 * rows
        self.CTX = ctx_len
        self.NCB = ctx_len // 128
        self.NLB = self.LAT // 128
        self.NB = self.NCB + self.NLB
        self.T = self.NB * 128
        self.depth = depth
        self.debug = debug
        self.dbufs = {}

    def dram(self, name, shape, kind="Internal"):
        if self.debug and kind == "Internal":
            kind = "ExternalOutput"
        return self.nc.dram_tensor(name, list(shape), F32, kind=kind).ap()

    def dbuf(self, *key):
        b = self.dbufs.get(key)
        if b is None:
            b = Buf(str(key))
            self.dbufs[key] = b
        return b

    def tile(self, st, name, shape):
        self.uid = getattr(self, "uid", 0) + 1
        name = "%s_u%d" % (name, self.uid)
        return Buf(name, st.enter_context(self.nc.sbuf_tensor(name, list(shape), F32)))

    def rot(self, st, name, shape, n):
        return Rot([self.tile(st, "%s%d" % (name, i), shape) for i in range(n)])

    def mm(self, out, lhsT, rhs, start, stop, R, W):
        self.p.op("pe", lambda e: e.matmul(out, lhsT=lhsT, rhs=rhs, start=start, stop=stop),
                  reads=R, writes=W, mm=True)

    def tr(self, out, in_, R, W):
        k = in_.shape[0]
        ident = self.ident[0:k, 0:k]
        self.p.op("pe", lambda e: e.transpose(out, in_, ident), reads=list(R) + [self.ident], writes=W, mm=True)

    def act(self, out, in_, func, R, W, scale=None, bias=None, accum=None):
        kw = {}
        if scale is not None:
            kw["scale"] = scale
        if bias is not None:
            kw["bias"] = bias
        if accum is not None:
            kw["accum_out"] = accum
        self.p.op("act", lambda e: e.activation(out=out, in_=in_, func=func, **kw), reads=R, writes=W)

    def tt(self, eng, out, a, b, op, R, W):
        self.p.op(eng, lambda e: e.tensor_tensor(out=out, in0=a, in1=b, op=op), reads=R, writes=W)

    def ts(self, eng, out, a, s1, op0, R, W, s2=None, op1=None):
        if op1 is None:
            self.p.op(eng, lambda e: e.tensor_scalar(out=out, in0=a, scalar1=s1, scalar2=None, op0=op0),
                      reads=R, writes=W)
        else:
            self.p.op(eng, lambda e: e.tensor_scalar(out=out, in0=a, scalar1=s1, scalar2=s2, op0=op0, op1=op1),
                      reads=R, writes=W)

    def stt(self, out, a, scalar, b, op0, op1, R, W):
        self.p.op("dve", lambda e: e.scalar_tensor_tensor(out=out, in0=a, scalar=scalar, in1=b, op0=op0, op1=op1),
                  reads=R, writes=W)

    def cp(self, eng, out, in_, R, W):
        if eng == "act":
            self.act(out, in_, AF.Copy, R, W)
        else:
            self.p.op(eng, lambda e: e.tensor_copy(out=out, in_=in_), reads=R, writes=W)

    def memset(self, eng, out, val, W):
        self.p.op(eng, lambda e: e.memset(out, val), writes=W)

    def red(self, out, in_, R, W):
        self.p.op("dve", lambda e: e.tensor_reduce(out=out, in_=in_, axis=AX.X, op=ALU.add), reads=R, writes=W)

    def ld(self, buf, out_ap, in_ap, R=()):
        return self.p.dma("sp", lambda e: e.dma_start(out=out_ap, in_=in_ap), buf, reads=R, writes=[buf])

    def st_(self, buf, out_ap, in_ap, W=()):
        return self.p.dma("sp", lambda e: e.dma_start(out=out_ap, in_=in_ap), buf, reads=[buf], writes=W)

    def big(self):
        return self.psbig.next()

    def half(self):
        return self.pshalf.next()

    def build(self):
        nc = self.nc
        NB, T = self.NB, self.T
        inp = lambda name, shape: nc.dram_tensor(name, list(shape), F32, kind="ExternalInput").ap()
        I = self.I = {}
        I["x"] = inp("x", [self.LAT, D])
        I["c"] = inp("c", [D])
        I["ctx"] = inp("ctx", [self.CTX, D])
        I["c_ctx"] = inp("c_ctx", [D])
        I["mod_w"] = inp("mod_w", [4, D, 3 * D])
        I["mod_b"] = inp("mod_b", [4, 3 * D])
        I["pre_g"] = inp("pre_g", [4, D])
        I["post_g"] = inp("post_g", [4, D])
        I["rw_mix"] = inp("rw_mix", [2, 6, D])
        I["rw_proj"] = inp("rw_proj", [2, 4, D, DI])
        I["rw_wo"] = inp("rw_wo", [2, DI, D])
        I["rw_w0"] = inp("rw_w0", [2, 2, DI])
        I["rw_w1"] = inp("rw_w1", [2, 2, D, 64])
        I["rw_w2"] = inp("rw_w2", [2, 2, 64, DI])
        I["rw_a0"] = inp("rw_a0", [2, 2, DI])
        I["rw_a1"] = inp("rw_a1", [2, 2, D, 64])
        I["rw_a2"] = inp("rw_a2", [2, 2, 64, DI])
        I["rw_v0"] = inp("rw_v0", [1, DI])
        I["rw_v1"] = inp("rw_v1", [1, D, 32])
        I["rw_v2"] = inp("rw_v2", [1, 32, DI])
        for n in ("rw_kk", "rw_ka", "rw_rk", "rw_lnw", "rw_lnb"):
            I[n] = inp(n, [2, DI])
        I["hg_win"] = inp("hg_win", [2, D, 5 * DI])
        I["hg_wo"] = inp("hg_wo", [2, DI, D])
        I["hg_gn"] = inp("hg_gn", [2, 128])
        I["hg_lb"] = inp("hg_lb", [4, DI])
        self.out = nc.dram_tensor("out", [self.LAT, D], F32, kind="ExternalOutput").ap()

        S = self.S = {}
        S["xs"] = self.dram("xs", [T, D])
        S["hT"] = self.dram("hT", [D, T])
        S["vfirst"] = self.dram("vfirst", [T, DI])
        S["v"] = self.dram("sv", [T, DI])
        S["z"] = self.dram("sz", [T, DI])
        S["bon"] = self.dram("sbon", [T, 32])
        S["yd"] = self.dram("syd", [2, T, DI])
        for n in ("kT", "rT", "bT", "ktT"):
            S[n] = self.dram("s" + n, [2, NB, 32, 64, 128])
        for n in ("bh", "kh"):
            S[n] = self.dram("s" + n, [2, T, DI])
        S["pc"] = self.dram("spc", [2, NB, 64, 32])
        for n in ("qT", "gkT"):
            S[n] = self.dram("s" + n, [2, NB, 16, 128, 128])
        S["pcm"] = self.dram("spcm", [2, NB, 128, 16, 4])
        S["qbT"] = self.dram("sqbT", [2, NB, 16, 128, 128])
        S["mod"] = self.dram("smod", [2, 3 * D])
        S["lb"] = self.dram("slb", [2, DI])

        with contextlib.ExitStack() as gst:
            self.p = Prog(nc, gst)
            self.ps = [Buf("ps%d" % i, gst.enter_context(nc.psum_tensor("ps%d" % i, [128, 512], F32))) for i in range(8)]
            self.psbig = Rot(self.ps[0:4])
            self.pshalf = Rot(self.ps[4:8])
            with contextlib.ExitStack() as tst:
                self.setup_consts(gst, tst)
                self.p.flush()
            import os
            stop = int(os.environ.get("K_STOP", "999"))
            npass = 0
            for li in range(self.depth):
                seq = [lambda: self.pass_H(li)]
                if li % 2 == 0:
                    seq += [lambda: self.rw_pass_A(li), lambda: self.rw_pass_B(li), lambda: self.pass_C(li, rw=True)]
                else:
                    seq += [lambda: self.hg_pass_A(li), lambda: self.hg_pass_B(li), lambda: self.pass_C(li, rw=False)]
                for f in seq:
                    if npass < stop:
                        f()
                    npass += 1
        return nc

    def setup_consts(self, st, tst):
        t = lambda n, s: self.tile(st, n, s)
        tt_ = lambda n, s: self.tile(tst, n, s)
        self.ident = t("ident", [128, 128])
        self.ones = t("ones", [128, 128])
        UI, US, LI, LS = t("UI", [128, 128]), t("US", [128, 128]), t("LI", [128, 128]), t("LS", [128, 128])
        self.memset("pool", self.ones[:], 1.0, [self.ones])

        def sel(dst, step, cm, cmp):
            self.p.op("pool", lambda e: e.affine_select(out=dst[:], in_=self.ones[:], pattern=[[step, 128]],
                                                        compare_op=cmp, fill=0.0, base=0, channel_multiplier=cm),
                      reads=[self.ones], writes=[dst])
        sel(self.ident, -1, 1, ALU.is_equal)
        sel(UI, 1, -1, ALU.is_ge)
        sel(US, 1, -1, ALU.is_gt)
        sel(LI, -1, 1, ALU.is_ge)
        sel(LS, -1, 1, ALU.is_gt)
        self.incl_st = [UI, LI]
        self.strict_st = [US, LS]
        self.strict_ts = [LS, US]
        BD = {}
        for nb_, bs in ((4, 32), (2, 64)):
            E = t("E%d" % bs, [nb_, 128])
            self.p.op("pool", lambda e, E=E, nb_=nb_, bs=bs: e.affine_select(
                out=E[:], in_=self.ones[0:nb_, :], pattern=[[1, 128]], compare_op=ALU.is_ge, fill=0.0, base=0,
                channel_multiplier=-bs), reads=[self.ones], writes=[E])
            self.p.op("pool", lambda e, E=E, nb_=nb_, bs=bs: e.affine_select(
                out=E[:], in_=E[:], pattern=[[-1, 128]], compare_op=ALU.is_ge, fill=0.0, base=bs - 1,
                channel_multiplier=bs), reads=[E], writes=[E])
            pb = self.big()
            self.mm(pb[:, 0:128], E[:], E[:], True, True, [E], [pb])
            bd = t("BD%d" % bs, [128, 128])
            self.cp("dve", bd[:], pb[:, 0:128], [pb], [bd])
            BD[bs] = bd
        D64m32 = t("D64m32", [128, 128])
        self.tt("dve", D64m32[:], BD[64][:], BD[32][:], ALU.subtract, [BD[64], BD[32]], [D64m32])
        N64 = t("N64", [128, 128])
        self.ts("dve", N64[:], BD[64][:], -1.0, ALU.mult, [BD[64]], [N64], s2=1.0, op1=ALU.add)
        self.CM1, self.CM2, self.HD, self.OM, self.IB64, self.REM64 = [], [], [], [], [], []
        plt32, pge64, pge96 = t("plt32", [128, 1]), t("pge64", [128, 1]), t("pge96", [128, 1])
        for dst_, base_, cm_ in ((plt32, 31, -1), (pge64, -64, 1), (pge96, -96, 1)):
            self.p.op("pool", lambda e, dst_=dst_, base_=base_, cm_=cm_: e.affine_select(
                out=dst_[:], in_=self.ones[:, 0:1], pattern=[[0, 1]], compare_op=ALU.is_ge, fill=0.0, base=base_,
                channel_multiplier=cm_), reads=[self.ones], writes=[dst_])
        self.NBD_ts, self.C1_ts, self.C2_ts, self.C1_st, self.NBD_st = [], [], [], [], []
        for d in range(2):
            cm1 = t("CM1_%d" % d, [128, 256])
            cm2 = t("CM2_%d" % d, [128, 256])
            self.stt(cm1[:, 0:128], self.strict_st[d][:], -1.0, BD[32][:], ALU.mult, ALU.mult, [self.strict_st[d], BD[32]], [cm1])
            self.cp("dve", cm1[:, 128:256], self.incl_st[d][:], [self.incl_st[d]], [cm1])
            self.cp("dve", cm2[:, 0:128], self.strict_st[d][:], [self.strict_st[d]], [cm2])
            self.cp("dve", cm2[:, 128:256], self.incl_st[d][:], [self.incl_st[d]], [cm2])
            nbd = t("NBDts_%d" % d, [128, 128])
            self.stt(nbd[:], self.strict_ts[d][:], -1.0, BD[32][:], ALU.mult, ALU.mult, [self.strict_ts[d], BD[32]], [nbd])
            c1ts = t("C1ts_%d" % d, [128, 128])
            self.tt("dve", c1ts[:], self.strict_ts[d][:], D64m32[:], ALU.mult, [self.strict_ts[d], D64m32], [c1ts])
            c2ts = t("C2ts_%d" % d, [128, 128])
            self.tt("dve", c2ts[:], self.strict_ts[d][:], N64[:], ALU.mult, [self.strict_ts[d], N64], [c2ts])
            nbdst = t("NBDst_%d" % d, [128, 128])
            self.cp("dve", nbdst[:], cm1[:, 0:128], [cm1], [nbdst])
            self.NBD_st.append(nbdst)
            c1st = t("C1st_%d" % d, [128, 128])
            self.tt("dve", c1st[:], self.strict_st[d][:], D64m32[:], ALU.mult, [self.strict_st[d], D64m32], [c1st])
            self.CM1.append(cm1)
            self.CM2.append(cm2)
            self.NBD_ts.append(nbd)
            self.C1_ts.append(c1ts)
            self.C2_ts.append(c2ts)
            self.C1_st.append(c1st)
            mcol = t("mcol_%d" % d, [128, 1])
            self.tt("dve", mcol[:], plt32[:], pge64[:], ALU.add, [plt32, pge64], [mcol])
            self.tt("dve", mcol[:], mcol[:], pge96[:], ALU.subtract, [mcol, pge96], [mcol])
            if d == 1:
                self.ts("dve", mcol[:], mcol[:], -1.0, ALU.mult, [mcol], [mcol], s2=1.0, op1=ALU.add)
            midh = t("MIDH_%d" % d, [128, 128])
            self.ts("dve", midh[:], BD[64][:], mcol[:, 0:1], ALU.mult, [BD[64], mcol], [midh])
            ib = t("IB64_%d" % d, [128, 128])
            self.tt("dve", ib[:], self.incl_st[d][:], BD[64][:], ALU.mult, [self.incl_st[d], BD[64]], [ib])
            hd = t("HD_%d" % d, [128, 128])
            self.tt("dve", hd[:], ib[:], midh[:], ALU.subtract, [ib, midh], [hd])
            rem = t("REM64_%d" % d, [128, 128])
            self.tt("dve", rem[:], self.strict_ts[d][:], BD[64][:], ALU.mult, [self.strict_ts[d], BD[64]], [rem])
            om = t("OM_%d" % d, [128, 4])
            self.cp("dve", om[:, 0:1], BD[64][:, 0:1], [BD[64]], [om])
            self.cp("dve", om[:, 1:2], BD[64][:, 127:128], [BD[64]], [om])
            self.tt("dve", om[:, 2:3], om[:, 0:1], mcol[:], ALU.mult, [om, mcol], [om])
            self.tt("dve", om[:, 3:4], om[:, 1:2], mcol[:], ALU.mult, [om, mcol], [om])
            self.HD.append(hd)
            self.OM.append(om)
            self.IB64.append(ib)
            self.REM64.append(rem)
        self.sel2 = []
        for r in range(2):
            s = t("sel2_%d" % r, [2, 128])
            self.memset("pool", s[:], float(r), [s])
            self.memset("pool", s[0:1, :], float(1 - r), [s])
            self.sel2.append(s)
        self.scT = t("scT", [128, 8, 2])
        crow = tt_("crow", [2, D])
        self.ld(crow, crow[0:1, :], self.I["c"].unsqueeze(0))
        self.ld(crow, crow[1:2, :], self.I["c_ctx"].unsqueeze(0))
        self.act(crow[:], crow[:], AF.Silu, [crow], [crow])
        pb = self.big()
        for c in range(8):
            self.tr(pb[:, c * 2:c * 2 + 2], crow[:, c * 128:(c + 1) * 128], [crow], [pb])
        self.cp("dve", self.scT[:].rearrange("p c r -> p (c r)"), pb[:, 0:16], [pb], [self.scT])
        self.lbexp = tt_("lbexp", [1, 4, DI])
        self.ld(self.lbexp, self.lbexp[:].rearrange("p l c -> p (l c)"),
                self.I["hg_lb"].rearrange("l c -> (l c)").unsqueeze(0))
        self.act(self.lbexp[:], self.lbexp[:], AF.Exp, [self.lbexp], [self.lbexp])
        self.lbtot = tt_("lbtot", [1, DI])
        L = self.lbexp
        self.tt("dve", self.lbtot[:], L[:, 0, :], L[:, 1, :], ALU.add, [L], [self.lbtot])
        self.tt("dve", self.lbtot[:], self.lbtot[:], L[:, 2, :], ALU.add, [L, self.lbtot], [self.lbtot])
        self.tt("dve", self.lbtot[:], self.lbtot[:], L[:, 3, :], ALU.add, [L, self.lbtot], [self.lbtot])
        self.p.op("dve", lambda e: e.reciprocal(out=self.lbtot[:], in_=self.lbtot[:]), reads=[self.lbtot], writes=[self.lbtot])
        self.lbrow = tt_("lbrow", [1, 2, DI])
        self.tt("dve", self.lbrow[:, 0, :], L[:, 1, :], self.lbtot[:], ALU.mult, [L, self.lbtot], [self.lbrow])
        self.tt("dve", self.lbrow[:, 1, :], L[:, 1, :], L[:, 2, :], ALU.add, [L], [self.lbrow])
        self.tt("dve", self.lbrow[:, 1, :], self.lbrow[:, 1, :], L[:, 3, :], ALU.add, [L, self.lbrow], [self.lbrow])
        self.tt("dve", self.lbrow[:, 1, :], self.lbrow[:, 1, :], self.lbtot[:], ALU.mult, [self.lbrow, self.lbtot], [self.lbrow])
        self.st_(self.lbrow, self.S["lb"].unsqueeze(0), self.lbrow[:], W=[self.dbuf("lb")])

    def x_ap(self, li, b):
        if li == 0:
            if b < self.NCB:
                return self.I["ctx"][b * 128:(b + 1) * 128, :], self.dbuf("in_ctx", b)
            bb = b - self.NCB
            return self.I["x"][bb * 128:(bb + 1) * 128, :], self.dbuf("in_x", bb)
        return self.S["xs"][b * 128:(b + 1) * 128, :], self.dbuf("xs", b)

    def x_out_ap(self, li, b):
        if li == self.depth - 1 and b >= self.NCB:
            bb = b - self.NCB
            return self.out[bb * 128:(bb + 1) * 128, :], self.dbuf("out", bb)
        return self.S["xs"][b * 128:(b + 1) * 128, :], self.dbuf("xs", b)

    def bcast_rows(self, st, rows_buf, col0, ncol, name):
        outs = []
        for r in range(2):
            tl = self.tile(st, "%s_%d" % (name, r), [128, ncol])
            for j in range(0, ncol, 512):
                pb = self.big()
                self.mm(pb[:, 0:512], self.sel2[r][:], rows_buf[:, col0 + j:col0 + j + 512], True, True,
                        [self.sel2[r], rows_buf], [pb])
                self.cp("act", tl[:, j:j + 512], pb[:, 0:512], [pb], [tl])
            outs.append(tl)
        return outs

    def pass_H(self, li):
        with contextlib.ExitStack() as st:
            I = self.I
            mw = self.rot(st, "mw", [128, 3 * D], 2)
            banks = [self.ps[i] for i in range(6)]
            for c in range(8):
                w = mw.next()
                self.ld(w, w[:], I["mod_w"][li, c * 128:(c + 1) * 128, :])
                for j in range(6):
                    self.mm(banks[j][0:2, :], self.scT[:, c, :], w[:, j * 512:(j + 1) * 512], c == 0, c == 7,
                            [self.scT, w], [banks[j]])
            mb = self.tile(st, "mb", [2, 3 * D])
            self.ld(mb, mb[:], I["mod_b"][li].partition_broadcast(2))
            self.modrow = self.tile(st, "modrow", [2, 3 * D])
            for j in range(6):
                self.tt("dve", self.modrow[:, j * 512:(j + 1) * 512], banks[j][0:2, :], mb[:, j * 512:(j + 1) * 512],
                        ALU.add, [banks[j], mb], [self.modrow])
            self.st_(self.modrow, self.S["mod"], self.modrow[:], W=[self.dbuf("mod")])
            pg = self.tile(st, "pg", [2, D])
            self.ld(pg, pg[:], I["pre_g"][li].partition_broadcast(2))
            grow = self.tile(st, "grow", [2, D])
            self.stt(grow[:], self.modrow[:, D:2 * D], 1.0, pg[:], ALU.add, ALU.mult, [self.modrow, pg], [grow])
            G = self.bcast_rows(st, grow, 0, D, "G")
            Sh = self.bcast_rows(st, self.modrow, 0, D, "Sh")
            xt = self.rot(st, "xt", [128, D], 2)
            ht = self.rot(st, "ht", [128, D], 2)
            hTt = self.rot(st, "hTt", [128, 8, 128], 2)
            sst = self.rot(st, "ss", [128, 2], 2)
            junk = self.tile(st, "junk", [128, D])
            for b in range(self.NB):
                seg = 1 if b < self.NCB else 0
                xap, xb = self.x_ap(li, b)
                x = xt.next()
                self.ld(x, x[:], xap, R=[xb])
                ss = sst.next()
                self.act(junk[:], x[:], AF.Square, [x], [junk, ss], accum=ss[:, 0:1])
                self.act(ss[:, 1:2], ss[:, 0:1], AF.Ln, [ss], [ss], scale=1.0 / D, bias=NORM_EPS)
                self.act(ss[:, 1:2], ss[:, 1:2], AF.Exp, [ss], [ss], scale=-0.5)
                h = ht.next()
                self.stt(h[:], x[:], ss[:, 1:2], G[seg][:], ALU.mult, ALU.mult, [x, ss, G[seg]], [h])
                self.tt("pool", h[:], h[:], Sh[seg][:], ALU.add, [h, Sh[seg]], [h])
                hT = hTt.next()
                for g in range(2):
                    pb = self.big()
                    for c in range(4):
                        cc = g * 4 + c
                        self.tr(pb[:, c * 128:(c + 1) * 128], h[:, cc * 128:(cc + 1) * 128], [h], [pb])
                    self.cp("act" if g == 0 else "dve", hT[:, g * 4:(g + 1) * 4, :].rearrange("p c t -> p (c t)"),
                            pb[:, :], [pb], [hT])
                self.st_(hT, self.S["hT"].rearrange("(c p) t -> p c t", p=128)[:, :, b * 128:(b + 1) * 128], hT[:],
                         W=[self.dbuf("hT", b)])
            self.p.flush()

    def rw_pass_A(self, li):
        j = li // 2
        I, S = self.I, self.S
        NCB = self.NCB
        with contextlib.ExitStack() as st:
            t = lambda n, s: self.tile(st, n, s)
            mixrow = t("mixrow", [6, D])
            self.ld(mixrow, mixrow[:], I["rw_mix"][j])
            MIX = t("MIX", [128, 8, 6])
            pb = self.big()
            for c in range(8):
                self.tr(pb[:, c * 6:c * 6 + 6], mixrow[:, c * 128:(c + 1) * 128], [mixrow], [pb])
            self.cp("dve", MIX[:].rearrange("p c r -> p (c r)"), pb[:, 0:48], [pb], [MIX])
            W1 = t("W1", [128, 8, 128])
            A1 = t("A1", [128, 8, 128])
            for d in range(2):
                self.ld(W1, W1[:, :, d * 64:(d + 1) * 64], I["rw_w1"][j, d].rearrange("(c p) r -> p c r", p=128))
                self.ld(A1, A1[:, :, d * 64:(d + 1) * 64], I["rw_a1"][j, d].rearrange("(c p) r -> p c r", p=128))
            if j > 0:
                V1 = t("V1", [128, 8, 32])
                self.ld(V1, V1[:], I["rw_v1"][j - 1].rearrange("(c p) r -> p c r", p=128))
            TW = [t("TW%d" % d, [65, 128]) for d in range(2)]
            TA = [t("TA%d" % d, [65, 128]) for d in range(2)]
            TV = t("TV", [33, 128])
            for x_ in TW + TA:
                self.memset("pool", x_[64:65, :], 1.0, [x_])
            self.memset("pool", TV[32:33, :], 1.0, [TV])
            HW = t("HW", [128, 8, 256])
            XX = t("XX", [128, 8, 128])
            XN = [t("XN%d" % n, [128, 8, 128]) for n in range(6)]
            WT = self.rot(st, "WT", [128, 8, 512], 2)
            LR = self.rot(st, "LR", [65, 5, 512], 2)
            VP = t("VP", [128, 3, 512])
            wk = {n: t("wk_" + n, [128, 512]) for n in
                  ("r", "k", "v", "z", "kkr", "sq", "kkn", "rk", "vf", "sigv", "sg", "a", "t1", "kd", "bb",
                   "EI", "EnI", "ES", "ER", "o1", "o2", "o3", "o4", "o5", "o6", "prod")}
            sm = {n: t("sm_" + n, [128, 8]) for n in ("ss", "rn", "bon0", "bon1", "bon")}
            TT = {n: t("TT_" + n, [64, 8, 128]) for n in ("kT", "bT", "ktT", "rT")}
            PCt = t("PCt", [64, 8])

            def v3(ap):
                return ap.rearrange("p (h k) -> p h k", k=64)

            wbuf, lrbuf = {}, {}
            npieces = self.NB * 16

            def wload(k):
                if k >= npieces or k in wbuf:
                    return
                cg_, n_ = (k // 4) % 4, k % 4
                w = WT.next()
                self.ld(w, w[:], I["rw_proj"][j, n_, :, cg_ * 512:(cg_ + 1) * 512].rearrange("(c p) n -> p c n", p=128))
                wbuf[k] = w

            def lrload(k):
                if k >= self.NB * 4 or k in lrbuf:
                    return
                cs_ = slice((k % 4) * 512, (k % 4 + 1) * 512)
                lr = LR.next()
                for d in range(2):
                    self.ld(lr, lr[0:64, d, :], I["rw_w2"][j, d, :, cs_])
                    self.ld(lr, lr[64:65, d, :], I["rw_w0"][j, d, cs_].unsqueeze(0))
                    self.ld(lr, lr[0:64, 2 + d, :], I["rw_a2"][j, d, :, cs_])
                    self.ld(lr, lr[64:65, 2 + d, :], I["rw_a0"][j, d, cs_].unsqueeze(0))
                if j > 0:
                    self.ld(lr, lr[0:32, 4, :], I["rw_v2"][j - 1, :, cs_])
                    self.ld(lr, lr[32:33, 4, :], I["rw_v0"][j - 1, cs_].unsqueeze(0))
                lrbuf[k] = lr
            wload(0)
            wload(1)
            lrload(0)

            for b in range(self.NB):
                ctx = b < NCB
                s0, s1 = (0, NCB * 128) if ctx else (NCB * 128, self.T)
                w0, w1 = b * 128 - 64, b * 128 + 192
                v0, v1_ = max(w0, s0), min(w1, s1)
                if v0 > w0:
                    self.memset("pool", HW[:, :, 0:v0 - w0], 0.0, [HW])
                if v1_ < w1:
                    self.memset("pool", HW[:, :, v1_ - w0:256], 0.0, [HW])
                nb_lo, nb_hi = v0 // 128, (v1_ - 1) // 128
                self.ld(HW, HW[:, :, v0 - w0:v1_ - w0],
                        S["hT"].rearrange("(c p) t -> p c t", p=128)[:, :, v0:v1_],
                        R=[self.dbuf("hT", q) for q in range(nb_lo, nb_hi + 1)])
                hc = HW[:, :, 64:192]
                if ctx:
                    self.tt("dve", XX[:, 0:4, :], HW[:, 0:4, 63:191], HW[:, 0:4, 64:192], ALU.subtract, [HW], [XX])
                    self.tt("pool", XX[:, 4:8, :], HW[:, 4:8, 65:193], HW[:, 4:8, 64:192], ALU.subtract, [HW], [XX])
                else:
                    self.tt("dve", XX[:, 0:2, :], HW[:, 0:2, 63:191], HW[:, 0:2, 64:192], ALU.subtract, [HW], [XX])
                    self.tt("pool", XX[:, 2:4, :], HW[:, 2:4, 65:193], HW[:, 2:4, 64:192], ALU.subtract, [HW], [XX])
                    self.tt("dve", XX[:, 4:6, :], HW[:, 4:6, 0:128], HW[:, 4:6, 64:192], ALU.subtract, [HW], [XX])
                    self.tt("pool", XX[:, 6:8, :], HW[:, 6:8, 128:256], HW[:, 6:8, 64:192], ALU.subtract, [HW], [XX])
                    self.ts("dve", XX[:, 0:2, 0:128:64], HW[:, 0:2, 64:192:64], -1.0, ALU.mult, [HW], [XX])
                    self.ts("dve", XX[:, 2:4, 63:128:64], HW[:, 2:4, 127:192:64], -1.0, ALU.mult, [HW], [XX])
                for n in range(6):
                    for c in range(8):
                        self.stt(XN[n][:, c, :], XX[:, c, :], MIX[:, c, n:n + 1], HW[:, c, 64:192],
                                 ALU.mult, ALU.add, [XX, MIX, HW], [XN[n]])
                xr, xw, xk, xv, xa, xg = XN
                for d in range(2):
                    ph = self.half()
                    for c in range(8):
                        self.mm(ph[0:64, 0:128], W1[:, c, d * 64:(d + 1) * 64], xw[:, c, :], c == 0, c == 7, [W1, xw], [ph])
                    self.act(TW[d][0:64, :], ph[0:64, 0:128], AF.Tanh, [ph], [TW[d]])
                    ph = self.half()
                    for c in range(8):
                        self.mm(ph[0:64, 0:128], A1[:, c, d * 64:(d + 1) * 64], xa[:, c, :], c == 0, c == 7, [A1, xa], [ph])
                    self.cp("dve", TA[d][0:64, :], ph[0:64, 0:128], [ph], [TA[d]])
                if j > 0:
                    ph = self.half()
                    for c in range(8):
                        self.mm(ph[0:32, 0:128], V1[:, c, :], xv[:, c, :], c == 0, c == 7, [V1, xv], [ph])
                    self.cp("dve", TV[0:32, :], ph[0:32, 0:128], [ph], [TV])
                tok = slice(b * 128, (b + 1) * 128)
                for cg in range(4):
                    cs = slice(cg * 512, (cg + 1) * 512)
                    for i_, nm in enumerate(("rw_kk", "rw_ka", "rw_rk")):
                        self.ld(VP, VP[:, i_, :], I[nm][j, cs].partition_broadcast(128))
                    lrload(b * 4 + cg + 1)
                    lr = lrbuf.pop(b * 4 + cg)
                    pbs = []
                    for n_, xin in enumerate((xr, xk, xv, xg)):
                        kpiece = (b * 4 + cg) * 4 + n_
                        w = wbuf.pop(kpiece)
                        pb = self.big()
                        for c in range(8):
                            self.mm(pb[:, :], xin[:, c, :], w[:, c, :], c == 0, c == 7, [xin, w], [pb])
                        wload(kpiece + 2)
                        pbs.append(pb)
                        dst = wk[("r", "k", "v", "z")[n_]]
                        if n_ == 3:
                            self.act(dst[:], pb[:, :], AF.Silu, [pb], [dst])
                        else:
                            self.cp("act", dst[:], pb[:, :], [pb], [dst])
                    self.st_(wk["z"], S["z"][tok, cs], wk["z"][:], W=[self.dbuf("z", b, cg)])
                    if j == 0:
                        self.st_(wk["v"], S["vfirst"][tok, cs], wk["v"][:], W=[self.dbuf("vfirst", b, cg)])
                    else:
                        self.ld(wk["vf"], wk["vf"][:], S["vfirst"][tok, cs], R=[self.dbuf("vfirst", b, cg)])
                        pb = self.big()
                        self.mm(pb[:, :], TV[0:33, :], lr[0:33, 4, :], True, True, [TV, lr], [pb])
                        self.act(wk["sigv"][:], pb[:, :], AF.Sigmoid, [pb], [wk["sigv"]])
                        self.tt("pool", wk["vf"][:], wk["vf"][:], wk["v"][:], ALU.subtract, [wk["vf"], wk["v"]], [wk["vf"]])
                        self.tt("dve", wk["vf"][:], wk["vf"][:], wk["sigv"][:], ALU.mult, [wk["vf"], wk["sigv"]], [wk["vf"]])
                        self.tt("pool", wk["v"][:], wk["v"][:], wk["vf"][:], ALU.add, [wk["v"], wk["vf"]], [wk["v"]])
                    self.st_(wk["v"], S["v"][tok, cs], wk["v"][:], W=[self.dbuf("v", b, cg)])
                    self.tt("pool", wk["kkr"][:], wk["k"][:], VP[:, 0, :], ALU.mult, [wk["k"], VP], [wk["kkr"]])
                    self.tt("pool", wk["sq"][:], wk["kkr"][:], wk["kkr"][:], ALU.mult, [wk["kkr"]], [wk["sq"]])
                    self.red(sm["ss"][:], v3(wk["sq"][:]), [wk["sq"]], [sm["ss"]])
                    self.ts("dve", sm["ss"][:], sm["ss"][:], 1e-24, ALU.max, [sm["ss"]], [sm["ss"]])
                    self.act(sm["rn"][:], sm["ss"][:], AF.Ln, [sm["ss"]], [sm["rn"]])
                    self.act(sm["rn"][:], sm["rn"][:], AF.Exp, [sm["rn"]], [sm["rn"]], scale=-0.5)
                    self.tt("dve", v3(wk["kkn"][:]), v3(wk["kkr"][:]), sm["rn"][:].unsqueeze(2).broadcast_to([128, 8, 64]),
                            ALU.mult, [wk["kkr"], sm["rn"]], [wk["kkn"]])
                    self.tt("pool", wk["rk"][:], wk["r"][:], VP[:, 2, :], ALU.mult, [wk["r"], VP], [wk["rk"]])
                    for d in range(2):
                        pu = self.big()
                        self.mm(pu[:, :], TW[d][0:65, :], lr[0:65, d, :], True, True, [TW[d], lr], [pu])
                        self.act(wk["sg"][:], pu[:, :], AF.Sigmoid, [pu], [wk["sg"]])
                        pa = self.big()
                        self.mm(pa[:, :], TA[d][0:65, :], lr[0:65, 2 + d, :], True, True, [TA[d], lr], [pa])
                        self.act(wk["a"][:], pa[:, :], AF.Sigmoid, [pa], [wk["a"]])
                        self.stt(wk["t1"][:], wk["a"][:], -1.0, VP[:, 1, :], ALU.add, ALU.mult, [wk["a"], VP], [wk["t1"]])
                        self.stt(wk["kd"][:], wk["t1"][:], 1.0, wk["k"][:], ALU.add, ALU.mult, [wk["t1"], wk["k"]], [wk["kd"]])
                        self.tt("pool", wk["prod"][:], wk["rk"][:], wk["kd"][:], ALU.mult, [wk["rk"], wk["kd"]], [wk["prod"]])
                        bon = sm["bon%d" % d]
                        self.red(bon[:], v3(wk["prod"][:]), [wk["prod"]], [bon])
                        self.tt("pool", wk["bb"][:], wk["a"][:], wk["kkn"][:], ALU.mult, [wk["a"], wk["kkn"]], [wk["bb"]])
                        sg = wk["sg"]
                        pI = self.big()
                        self.mm(pI[:, :], self.incl_st[d][:], sg[:], True, True, [self.incl_st[d], sg], [pI])
                        self.act(wk["EI"][:], pI[:, :], AF.Exp, [pI], [wk["EI"]], scale=WSCALE)
                        self.act(wk["EnI"][:], pI[:, :], AF.Exp, [pI], [wk["EnI"]], scale=-WSCALE)
                        pS = self.big()
                        self.mm(pS[:, :], self.strict_st[d][:], sg[:], True, True, [self.strict_st[d], sg], [pS])
                        self.act(wk["ES"][:], pS[:, :], AF.Exp, [pS], [wk["ES"]], scale=WSCALE)
                        pR = self.big()
                        self.mm(pR[:, :], self.strict_ts[d][:], sg[:], True, True, [self.strict_ts[d], sg], [pR])
                        self.act(wk["ER"][:], pR[:, :], AF.Exp, [pR], [wk["ER"]], scale=WSCALE)
                        self.tt("dve", wk["o1"][:], wk["kkn"][:], wk["ES"][:], ALU.mult, [wk["kkn"], wk["ES"]], [wk["o1"]])
                        self.tt("pool", wk["o2"][:], wk["bb"][:], wk["EnI"][:], ALU.mult, [wk["bb"], wk["EnI"]], [wk["o2"]])
                        self.tt("dve", wk["o3"][:], wk["kd"][:], wk["EnI"][:], ALU.mult, [wk["kd"], wk["EnI"]], [wk["o3"]])
                        self.tt("pool", wk["o4"][:], wk["r"][:], wk["EI"][:], ALU.mult, [wk["r"], wk["EI"]], [wk["o4"]])
                        self.tt("dve", wk["o5"][:], wk["bb"][:], wk["ER"][:], ALU.mult, [wk["bb"], wk["ER"]], [wk["o5"]])
                        self.tt("pool", wk["o6"][:], wk["kd"][:], wk["ER"][:], ALU.mult, [wk["kd"], wk["ER"]], [wk["o6"]])
                        self.st_(wk["o5"], S["bh"][d, tok, cs], wk["o5"][:], W=[self.dbuf("bh", d, b, cg)])
                        self.st_(wk["o6"], S["kh"][d, tok, cs], wk["o6"][:], W=[self.dbuf("kh", d, b, cg)])
                        for src, nm in ((wk["o1"], "kT"), (wk["o2"], "bT"), (wk["o3"], "ktT"), (wk["o4"], "rT")):
                            dst = TT[nm]
                            for g in range(2):
                                pb = self.big()
                                for hh in range(4):
                                    h8 = g * 4 + hh
                                    self.tr(pb[0:64, hh * 128:(hh + 1) * 128], src[:, h8 * 64:(h8 + 1) * 64], [src], [pb])
                                self.cp("act" if g == 0 else "dve", dst[:, g * 4:(g + 1) * 4, :].rearrange("p h t -> p (h t)"),
                                        pb[0:64, :], [pb], [dst])
                            self.st_(dst, S[nm][d, b, cg * 8:(cg + 1) * 8].rearrange("h k t -> k h t"), dst[:],
                                     W=[self.dbuf(nm, d, b, cg)])
                        ph = self.half()
                        for h8 in range(8):
                            self.mm(ph[0:64, h8:h8 + 1], sg[:, h8 * 64:(h8 + 1) * 64], self.ones[:, 0:1], True, True,
                                    [sg, self.ones], [ph])
                        self.act(PCt[:], ph[0:64, 0:8], AF.Exp, [ph], [PCt], scale=WSCALE)
                        self.st_(PCt, S["pc"][d, b, :, cg * 8:(cg + 1) * 8], PCt[:], W=[self.dbuf("pc", d, b, cg)])
                    self.tt("dve", sm["bon"][:], sm["bon0"][:], sm["bon1"][:], ALU.add, [sm["bon0"], sm["bon1"]], [sm["bon"]])
                    self.st_(sm["bon"], S["bon"][tok, cg * 8:(cg + 1) * 8], sm["bon"][:], W=[self.dbuf("bon", b, cg)])
            self.p.flush()

    def block_order(self, d):
        NCB, NB = self.NCB, self.NB
        if d == 0:
            return list(range(NB))
        return list(range(NCB - 1, -1, -1)) + list(range(NB - 1, NCB - 1, -1))

    def rw_pass_B(self, li):
        S = self.S
        with contextlib.ExitStack() as st:
            t = lambda n, s: self.tile(st, n, s)
            allps = Rot(self.ps)
            ps = allps.next
            H = [[t("H%d_%d" % (d, g), [64, 8, 64]) for g in range(4)] for d in range(2)]
            for d in range(2):
                for g in range(4):
                    self.memset("pool", H[d][g][:], 0.0, [H[d][g]])
            KR = self.rot(st, "KR", [64, 8, 2, 128], 2)
            BT = self.rot(st, "BT", [64, 8, 128], 2)
            KtT = self.rot(st, "KtT", [64, 8, 128], 2)
            BH = self.rot(st, "BH", [128, 512], 2)
            KH = self.rot(st, "KH", [128, 512], 2)
            V = self.rot(st, "V", [128, 512], 2)
            PC = self.rot(st, "PC", [64, 8], 2)
            MF = self.rot(st, "MF", [128, 8, 128], 2)
            LKA = self.rot(st, "LKA", [128, 8, 256], 2)
            ARB = self.rot(st, "ARB", [128, 8, 128], 2)
            Rsb = self.rot(st, "Rsb", [128, 512], 2)
            Usb = self.rot(st, "Usb", [128, 512], 2)
            Y = self.rot(st, "Y", [128, 512], 2)
            Htmp = self.rot(st, "Htmp", [64, 512], 2)
            NSLOT = 3
            GL = [[t("GL%d_%d" % (g, i), [128, 4, 128]) for i in range(3)] for g in range(NSLOT)]
            GR = [self.rot(st, "GR%d_" % g, [128, 4, 128], 8) for g in range(NSLOT)]

            def flat(b_):
                return b_[:].rearrange("p h c -> p (h c)")

            def b4(m_, n=4):
                return m_[:].unsqueeze(1).broadcast_to([128, n, 128])

            def group_gen(d, kr, bt, ktt, lka, arb, mf, g4, slot):
                c1t, c1, c2 = GL[slot]
                gr = GR[slot]
                heads = [g4 * 4 + q for q in range(4)]
                h0 = heads[0]
                p1 = [ps(), ps()]
                for q, h in enumerate(heads):
                    krh = kr[:, h, :, :].rearrange("k a t -> k (a t)")
                    self.mm(p1[q // 2][:, (q % 2) * 256:(q % 2 + 1) * 256], bt[:, h, :], krh, True, True, [bt, kr], [p1[q // 2]])
                qT = gr.next()
                for k in range(2):
                    pv = p1[k][:, :].rearrange("p (h c) -> p h c", c=256)
                    self.tt("dve", qT[:, 2 * k:2 * k + 2, :], pv[:, :, 0:128], b4(self.NBD_st[d], 2), ALU.mult,
                            [p1[k], self.NBD_st[d]], [qT])
                    self.tt("dve", arb[:, h0 + 2 * k:h0 + 2 * k + 2, :], pv[:, :, 128:256], b4(self.incl_st[d], 2), ALU.mult,
                            [p1[k], self.incl_st[d]], [arb])
                    self.tt("dve", c1t[:, 2 * k:2 * k + 2, :], pv[:, :, 0:128], b4(self.C1_st[d], 2), ALU.mult,
                            [p1[k], self.C1_st[d]], [c1t])
                yield
                p2 = [ps(), ps()]
                for q, h in enumerate(heads):
                    krh = kr[:, h, :, :].rearrange("k a t -> k (a t)")
                    self.mm(p2[q // 2][:, (q % 2) * 256:(q % 2 + 1) * 256], ktt[:, h, :], krh, True, True, [ktt, kr], [p2[q // 2]])
                for k in range(2):
                    self.tt("dve", lka[:, h0 + 2 * k:h0 + 2 * k + 2, :], p2[k][:, :].rearrange("p (h c) -> p h c", c=256),
                            self.CM2[d][:].unsqueeze(1).broadcast_to([128, 2, 256]), ALU.mult, [p2[k], self.CM2[d]], [lka])
                yield
                p3 = ps()
                for q, h in enumerate(heads):
                    self.mm(p3[:, q * 128:(q + 1) * 128], kr[:, h, 0, :], bt[:, h, :], True, True, [kr, bt], [p3])
                p3v = p3[:, :].rearrange("p (h c) -> p h c", c=128)
                q0 = gr.next()
                self.tt("dve", q0[:], p3v, b4(self.NBD_ts[d]), ALU.mult, [p3, self.NBD_ts[d]], [q0])
                self.tt("dve", c1[:], p3v, b4(self.C1_ts[d]), ALU.mult, [p3, self.C1_ts[d]], [c1])
                self.tt("dve", c2[:], p3v, b4(self.C2_ts[d]), ALU.mult, [p3, self.C2_ts[d]], [c2])
                m = gr.next()
                self.tt("pool", m[:], qT[:], b4(self.ident), ALU.add, [qT, self.ident], [m])
                yield
                Q, QT = q0, qT
                for lvl in range(1, 5):
                    pq = ps()
                    for q in range(4):
                        self.mm(pq[:, q * 128:(q + 1) * 128], QT[:, q, :], Q[:, q, :], True, True, [QT, Q], [pq])
                    qn = gr.next()
                    self.cp("act", flat(qn), pq[:, :], [pq], [qn])
                    qtn = None
                    if lvl < 4:
                        pqt = ps()
                        for q in range(4):
                            self.mm(pqt[:, q * 128:(q + 1) * 128], Q[:, q, :], QT[:, q, :], True, True, [QT, Q], [pqt])
                        qtn = gr.next()
                        self.cp("act", flat(qtn), pqt[:, :], [pqt], [qtn])
                    yield
                    pm = ps()
                    for q in range(4):
                        self.mm(pm[:, q * 128:(q + 1) * 128], qn[:, q, :], m[:, q, :], True, True, [qn, m], [pm])
                    mn = gr.next()
                    self.tt("dve", flat(mn), pm[:, :], flat(m), ALU.add, [pm, m], [mn])
                    m = mn
                    Q, QT = qn, qtn
                    yield
                Tt = m
                pt = ps()
                for q in range(4):
                    self.tr(pt[:, q * 128:(q + 1) * 128], Tt[:, q, :], [Tt], [pt])
                Tn = gr.next()
                self.cp("act", flat(Tn), pt[:, :], [pt], [Tn])
                yield

                def step(lhs, rhs):
                    pp = ps()
                    for q in range(4):
                        self.mm(pp[:, q * 128:(q + 1) * 128], lhs[:, q, :], rhs[:, q, :], True, True, [lhs, rhs], [pp])
                    return pp
                py1 = step(c1t, Tn)
                y1 = gr.next()
                self.cp("act", flat(y1), py1[:, :], [py1], [y1])
                yield
                pz1 = step(Tt, y1)
                T64 = gr.next()
                self.tt("dve", flat(T64), flat(Tn), pz1[:, :], ALU.subtract, [Tn, pz1], [T64])
                yield
                py2 = step(c1, Tt)
                y2 = gr.next()
                self.cp("act", flat(y2), py2[:, :], [py2], [y2])
                yield
                pz2 = step(Tn, y2)
                Tt64 = gr.next()
                self.tt("dve", flat(Tt64), flat(Tt), pz2[:, :], ALU.subtract, [Tt, pz2], [Tt64])
                yield
                py3 = step(c2, Tt64)
                y3 = gr.next()
                self.cp("act", flat(y3), py3[:, :], [py3], [y3])
                yield
                pz3 = step(T64, y3)
                self.tt("dve", mf[:, h0:h0 + 4, :].rearrange("p h c -> p (h c)"), flat(Tt64), pz3[:, :], ALU.subtract,
                        [Tt64, pz3], [mf])
                yield

            def seq_gen(d, b, cg, kr, bh, kh, v, pc, lka, arb, mf):
                tok = slice(b * 128, (b + 1) * 128)
                cs = slice(cg * 512, (cg + 1) * 512)
                Hd = H[d][cg]
                pr = ps()
                for h8 in range(8):
                    o = pr[:, h8 * 64:(h8 + 1) * 64]
                    self.mm(o, kr[:, h8, 0, :], Hd[:, h8, :], True, False, [kr, Hd], [pr])
                    self.mm(o, lka[:, h8, 0:128], v[:, h8 * 64:(h8 + 1) * 64], False, True, [lka, v], [pr])
                rsb = Rsb.next()
                self.act(rsb[:], pr[:, :], AF.Copy, [pr], [rsb], scale=-1.0)
                yield
                pu = ps()
                for h8 in range(8):
                    self.mm(pu[:, h8 * 64:(h8 + 1) * 64], mf[:, h8, :], rsb[:, h8 * 64:(h8 + 1) * 64], True, True,
                            [mf, rsb], [pu])
                usb = Usb.next()
                self.cp("dve", usb[:], pu[:, :], [pu], [usb])
                yield
                py = ps()
                pn = ps()
                for h8 in range(8):
                    o = py[:, h8 * 64:(h8 + 1) * 64]
                    hsl = slice(h8 * 64, (h8 + 1) * 64)
                    self.mm(o, kr[:, h8, 1, :], Hd[:, h8, :], True, False, [kr, Hd], [py])
                    self.mm(o, arb[:, h8, :], usb[:, hsl], False, False, [arb, usb], [py])
                    self.mm(o, lka[:, h8, 128:256], v[:, hsl], False, True, [lka, v], [py])
                    o2 = pn[0:64, hsl]
                    self.mm(o2, bh[:, hsl], usb[:, hsl], True, False, [bh, usb], [pn])
                    self.mm(o2, kh[:, hsl], v[:, hsl], False, True, [kh, v], [pn])
                y = Y.next()
                self.cp("act", y[:], py[:, :], [py], [y])
                self.st_(y, S["yd"][d, tok, cs], y[:], W=[self.dbuf("yd", d, b, cg)])
                ht_ = Htmp.next()
                self.tt("pool", ht_[:].rearrange("k (h v) -> k h v", v=64), Hd[:, :, :],
                        pc[:].unsqueeze(2).broadcast_to([64, 8, 64]), ALU.mult, [Hd, pc], [ht_])
                self.tt("dve", Hd[:, :, :], ht_[:].rearrange("k (h v) -> k h v", v=64),
                        pn[0:64, :].rearrange("k (h v) -> k h v", v=64), ALU.add, [ht_, pn], [Hd])
                yield

            orders = [self.block_order(0), self.block_order(1)]
            units = []
            for step_i in range(self.NB):
                for cg in range(4):
                    for d in range(2):
                        units.append((d, orders[d][step_i], cg))
            ustate = {}

            def acquire(u):
                d, b, cg = units[u]
                tok = slice(b * 128, (b + 1) * 128)
                cs = slice(cg * 512, (cg + 1) * 512)
                hs = slice(cg * 8, (cg + 1) * 8)
                kr, bt, ktt, bh, kh, v, pc = KR.next(), BT.next(), KtT.next(), BH.next(), KH.next(), V.next(), PC.next()
                self.ld(kr, kr[:, :, 0, :], S["kT"][d, b, hs].rearrange("h k t -> k h t"), R=[self.dbuf("kT", d, b, cg)])
                self.ld(kr, kr[:, :, 1, :], S["rT"][d, b, hs].rearrange("h k t -> k h t"), R=[self.dbuf("rT", d, b, cg)])
                self.ld(bt, bt[:], S["bT"][d, b, hs].rearrange("h k t -> k h t"), R=[self.dbuf("bT", d, b, cg)])
                self.ld(ktt, ktt[:], S["ktT"][d, b, hs].rearrange("h k t -> k h t"), R=[self.dbuf("ktT", d, b, cg)])
                self.ld(bh, bh[:], S["bh"][d, tok, cs], R=[self.dbuf("bh", d, b, cg)])
                self.ld(kh, kh[:], S["kh"][d, tok, cs], R=[self.dbuf("kh", d, b, cg)])
                self.ld(v, v[:], S["v"][tok, cs], R=[self.dbuf("v", b, cg)])
                self.ld(pc, pc[:], S["pc"][d, b, :, hs], R=[self.dbuf("pc", d, b, cg)])
                mf, lka, arb = MF.next(), LKA.next(), ARB.next()
                ustate[u] = dict(t=(kr, bt, ktt, bh, kh, v, pc, mf, lka, arb), rem=2, done=False)

            from collections import deque
            tasks = deque((u, g4) for u in range(len(units)) for g4 in range(2))
            free_slots = list(range(NSLOT))
            running = []

            def try_start():
                if not tasks or not free_slots:
                    return False
                u, g4 = tasks[0]
                if u not in ustate:
                    if u >= 2 and not ustate[u - 2]["done"]:
                        return False
                    acquire(u)
                d, b, cg = units[u]
                kr, bt, ktt, bh, kh, v, pc, mf, lka, arb = ustate[u]["t"]
                slot = free_slots.pop(0)
                running.append([group_gen(d, kr, bt, ktt, lka, arb, mf, g4, slot), u, slot, "g"])
                tasks.popleft()
                return True

            while tasks or running:
                while try_start():
                    pass
                for r in list(running):
                    try:
                        next(r[0])
                    except StopIteration:
                        running.remove(r)
                        u = r[1]
                        if r[3] == "g":
                            free_slots.append(r[2])
                            ustate[u]["rem"] -= 1
                            if ustate[u]["rem"] == 0:
                                d, b, cg = units[u]
                                kr, bt, ktt, bh, kh, v, pc, mf, lka, arb = ustate[u]["t"]
                                running.append([seq_gen(d, b, cg, kr, bh, kh, v, pc, lka, arb, mf), u, None, "s"])
                        else:
                            ustate[u]["done"] = True
            self.p.flush()

    def pass_C(self, li, rw):
        j = li // 2
        I, S = self.I, self.S
        last = li == self.depth - 1
        with contextlib.ExitStack() as st:
            t = lambda n, s: self.tile(st, n, s)
            WO = t("WO", [128, 16, D])
            wsrc = I["rw_wo"][j] if rw else I["hg_wo"][j]
            for c4 in range(4):
                self.ld(WO, WO[:, c4 * 4:(c4 + 1) * 4, :],
                        wsrc[c4 * 512:(c4 + 1) * 512, :].rearrange("(c p) n -> p c n", p=128))
            pg = t("pg", [2, D])
            self.ld(pg, pg[:], I["post_g"][li].partition_broadcast(2))
            gprow = t("gprow", [2, D])
            gate = t("gate", [2, D])
            self.ld(gate, gate[:], S["mod"][:, 2 * D:3 * D], R=[self.dbuf("mod")])
            self.tt("dve", gprow[:], gate[:], pg[:], ALU.mult, [gate, pg], [gprow])
            GP = self.bcast_rows(st, gprow, 0, D, "GP")
            if rw:
                LNW = t("LNW", [128, DI])
                LNB = t("LNB", [128, DI])
                self.ld(LNW, LNW[:], I["rw_lnw"][j].partition_broadcast(128))
                self.ld(LNB, LNB[:], I["rw_lnb"][j].partition_broadcast(128))
            else:
                GN = t("GN", [128, 128])
                self.ld(GN, GN[:], I["hg_gn"][j].partition_broadcast(128))
            yf = self.rot(st, "yf", [128, 512], 2)
            yb = self.rot(st, "yb", [128, 512], 2)
            zt = self.rot(st, "zt", [128, 512], 2)
            vt = self.rot(st, "vt", [128, 512], 2)
            bont = self.rot(st, "bont", [128, 8], 2)
            w1 = t("w1", [128, 512])
            w2 = t("w2", [128, 512])
            w3 = t("w3", [128, 512])
            smA = t("smA", [128, 8])
            smB = t("smB", [128, 8])
            smC = t("smC", [128, 8])
            yzT = self.rot(st, "yzT", [128, 4, 128], 2)
            xt = self.rot(st, "xc", [128, D], 2)
            ym = t("ym", [128, D])
            junk = t("junkc", [128, D])
            ss = t("ssc", [128, 2])
            G_ = 64 if rw else 128
            ng = 512 // G_

            def v3(ap):
                return ap.rearrange("p (h k) -> p h k", k=G_)

            def bc(ap):
                return ap.unsqueeze(2).broadcast_to([128, ng, G_])
            acc = [self.ps[6], self.ps[7]]
            for b in range(self.NB):
                if last and b < self.NCB:
                    continue
                seg = 1 if b < self.NCB else 0
                tok = slice(b * 128, (b + 1) * 128)
                for cg in range(4):
                    cs = slice(cg * 512, (cg + 1) * 512)
                    f, bk, z = yf.next(), yb.next(), zt.next()
                    self.ld(f, f[:], S["yd"][0, tok, cs], R=[self.dbuf("yd", 0, b, cg)])
                    self.ld(bk, bk[:], S["yd"][1, tok, cs], R=[self.dbuf("yd", 1, b, cg)])
                    self.ld(z, z[:], S["z"][tok, cs], R=[self.dbuf("z", b, cg)])
                    self.tt("pool", w1[:], f[:], bk[:], ALU.add, [f, bk], [w1])
                    if rw:
                        v = vt.next()
                        bon = bont.next()
                        self.ld(v, v[:], S["v"][tok, cs], R=[self.dbuf("v", b, cg)])
                        self.ld(bon, bon[:], S["bon"][tok, cg * 8:(cg + 1) * 8], R=[self.dbuf("bon", b, cg)])
                        self.red(smA[:], v3(w1[:]), [w1], [smA])
                        self.ts("dve", smA[:], smA[:], 1.0 / 64, ALU.mult, [smA], [smA])
                        self.tt("dve", v3(w2[:]), v3(w1[:]), bc(smA[:]), ALU.subtract, [w1, smA], [w2])
                        self.tt("pool", w3[:], w2[:], w2[:], ALU.mult, [w2], [w3])
                        self.red(smB[:], v3(w3[:]), [w3], [smB])
                        self.act(smC[:], smB[:], AF.Ln, [smB], [smC], scale=1.0 / 64, bias=LN_X_EPS)
                        self.act(smC[:], smC[:], AF.Exp, [smC], [smC], scale=-0.5)
                        self.tt("dve", v3(w2[:]), v3(w2[:]), bc(smC[:]), ALU.mult, [w2, smC], [w2])
                        self.tt("pool", w2[:], w2[:], LNW[:, cs], ALU.mult, [w2, LNW], [w2])
                        self.tt("pool", w2[:], w2[:], LNB[:, cs], ALU.add, [w2, LNB], [w2])
                        self.tt("dve", v3(w3[:]), v3(v[:]), bc(bon[:]), ALU.mult, [v, bon], [w3])
                        self.tt("pool", w2[:], w2[:], w3[:], ALU.add, [w2, w3], [w2])
                        self.tt("dve", w2[:], w2[:], z[:], ALU.mult, [w2, z], [w2])
                    else:
                        self.tt("pool", w3[:], w1[:], w1[:], ALU.mult, [w1], [w3])
                        self.red(smB[:, 0:ng], v3(w3[:]), [w3], [smB])
                        self.act(smC[:, 0:ng], smB[:, 0:ng], AF.Ln, [smB], [smC], scale=1.0 / 128, bias=NORM_EPS)
                        self.act(smC[:, 0:ng], smC[:, 0:ng], AF.Exp, [smC], [smC], scale=-0.5)
                        self.tt("dve", v3(w2[:]), v3(w1[:]), bc(smC[:, 0:ng]), ALU.mult, [w1, smC], [w2])
                        self.tt("pool", v3(w2[:]), v3(w2[:]), GN[:].unsqueeze(1).broadcast_to([128, ng, G_]), ALU.mult,
                                [w2, GN], [w2])
                        self.tt("dve", w2[:], w2[:], z[:], ALU.mult, [w2, z], [w2])
                    yz = yzT.next()
                    pb = self.big()
                    for c in range(4):
                        self.tr(pb[:, c * 128:(c + 1) * 128], w2[:, c * 128:(c + 1) * 128], [w2], [pb])
                    self.cp("act", yz[:].rearrange("p c t -> p (c t)"), pb[:, :], [pb], [yz])
                    for c in range(4):
                        cc = cg * 4 + c
                        for hf in range(2):
                            self.mm(acc[hf][:, :], yz[:, c, :], WO[:, cc, hf * 512:(hf + 1) * 512], cc == 0, cc == 15,
                                    [yz, WO], [acc[hf]])
                for hf in range(2):
                    self.cp("act", ym[:, hf * 512:(hf + 1) * 512], acc[hf][:, :], [acc[hf]], [ym])
                self.act(junk[:], ym[:], AF.Square, [ym], [junk, ss], accum=ss[:, 0:1])
                self.act(ss[:, 1:2], ss[:, 0:1], AF.Ln, [ss], [ss], scale=1.0 / D, bias=NORM_EPS)
                self.act(ss[:, 1:2], ss[:, 1:2], AF.Exp, [ss], [ss], scale=-0.5)
                self.stt(ym[:], ym[:], ss[:, 1:2], GP[seg][:], ALU.mult, ALU.mult, [ym, ss, GP[seg]], [ym])
                xap, xb = self.x_ap(li, b)
                x = xt.next()
                self.ld(x, x[:], xap, R=[xb])
                self.tt("dve", x[:], x[:], ym[:], ALU.add, [x, ym], [x])
                oap, ob = self.x_out_ap(li, b)
                self.st_(x, oap, x[:], W=[ob])
            self.p.flush()

    def hg_pass_A(self, li):
        j = li // 2
        I, S = self.I, self.S
        STH = 4
        with contextlib.ExitStack() as st:
            t = lambda n, s: self.tile(st, n, s)
            LB = t("LB", [128, DI])
            OMLB = t("OMLB", [128, DI])
            self.ld(LB, LB[:], S["lb"][j].partition_broadcast(128), R=[self.dbuf("lb")])
            self.ts("dve", OMLB[:], LB[:], -1.0, ALU.mult, [LB], [OMLB], s2=1.0, op1=ALU.add)
            HT = self.rot(st, "HT", [128, 8, 128], 2 * STH)
            WT = self.rot(st, "WTh", [128, 8, 512], 3)
            wkq = [t("hk_q%d" % i, [128, 512]) for i in range(STH)]
            sets = []
            for si in range(2):
                wk = {n: t("hk%d_%s" % (si, n), [128, 512]) for n in
                      ("v", "g", "sg", "f", "k", "lf", "E1", "E2", "E3", "o2", "o3", "o1a", "o1b")}
                wk["TTq"] = t("TTq%d" % si, [128, 4, 128])
                wk["TTqb"] = t("TTqb%d" % si, [128, 4, 128])
                wk["TTk"] = t("TTk%d" % si, [128, 4, 128])
                wk["PCM"] = t("PCM%d" % si, [128, 4, 4])
                self.memset("pool", wk["o1a"][:], 0.0, [wk["o1a"]])
                self.memset("pool", wk["o1b"][:], 0.0, [wk["o1b"]])
                sets.append(wk)
            sbs = [list(range(a, min(a + STH, self.NB))) for a in range(0, self.NB, STH)]
            pieces = []
            for si_, blks in enumerate(sbs):
                for cg in range(4):
                    for kind, col0 in (("q", 0), ("i", 3 * DI), ("g", 4 * DI), ("f0", DI), ("f1", 2 * DI)):
                        pieces.append((si_, cg, kind, col0))
            wbuf = {}

            def wload(k):
                if k >= len(pieces) or k in wbuf:
                    return
                si_, cg, kind, col0 = pieces[k]
                w = WT.next()
                self.ld(w, w[:], I["hg_win"][j, :, col0 + cg * 512:col0 + (cg + 1) * 512].rearrange("(c p) n -> p c n", p=128))
                wbuf[k] = w
            hbuf = {}

            def hload(si_):
                if si_ >= len(sbs) or si_ in hbuf:
                    return
                lst = []
                for b in sbs[si_]:
                    hT = HT.next()
                    self.ld(hT, hT[:], S["hT"].rearrange("(c p) t -> p c t", p=128)[:, :, b * 128:(b + 1) * 128],
                            R=[self.dbuf("hT", b)])
                    lst.append(hT)
                hbuf[si_] = lst
            chains = []

            def run_round():
                for g in list(chains):
                    try:
                        next(g)
                    except StopIteration:
                        chains.remove(g)

            def chain_gen(wk, d, b, cg, q_, pb):
                tok = slice(b * 128, (b + 1) * 128)
                cs = slice(cg * 512, (cg + 1) * 512)
                self.act(wk["sg"][:], pb[:, :], AF.Sigmoid, [pb], [wk["sg"]])
                yield
                self.tt("dve", wk["f"][:], wk["sg"][:], OMLB[:, cs], ALU.mult, [wk["sg"], OMLB], [wk["f"]])
                self.tt("pool", wk["f"][:], wk["f"][:], LB[:, cs], ALU.add, [wk["f"], LB], [wk["f"]])
                yield
                self.ts("dve", wk["k"][:], wk["f"][:], -1.0, ALU.mult, [wk["f"]], [wk["k"]], s2=1.0, op1=ALU.add)
                self.act(wk["lf"][:], wk["f"][:], AF.Ln, [wk["f"]], [wk["lf"]])
                yield
                lf = wk["lf"]
                p1 = self.half()
                self.mm(p1[:, :], self.HD[d][:], lf[:], True, True, [self.HD[d], lf], [p1])
                p2 = self.half()
                self.mm(p2[:, :], self.REM64[d][:], lf[:], True, True, [self.REM64[d], lf], [p2])
                ph = self.half()
                for hh in range(4):
                    self.mm(ph[:, hh * 4:hh * 4 + 4], lf[:, hh * 128:(hh + 1) * 128], self.OM[d][:], True, True,
                            [lf, self.OM[d]], [ph])
                self.act(wk["E1"][:], p1[:, :], AF.Exp, [p1], [wk["E1"]])
                self.act(wk["E2"][:], p1[:, :], AF.Exp, [p1], [wk["E2"]], scale=-1.0)
                self.act(wk["E3"][:], p2[:, :], AF.Exp, [p2], [wk["E3"]])
                PCM = wk["PCM"]
                self.act(PCM[:].rearrange("p h a -> p (h a)"), ph[:, 0:16], AF.Exp, [ph], [PCM])
                self.st_(PCM, S["pcm"][d, b, :, cg * 4:(cg + 1) * 4, :], PCM[:], W=[self.dbuf("pcm", d, b, cg)])
                yield
                o1a, o1b = wk["o1a"], wk["o1b"]
                self.tt("dve", o1a[0:64, :], q_[0:64, :], wk["E1"][0:64, :], ALU.mult, [q_, wk["E1"]], [o1a])
                self.tt("dve", o1b[64:128, :], q_[64:128, :], wk["E1"][64:128, :], ALU.mult, [q_, wk["E1"]], [o1b])
                self.tt("pool", wk["o2"][:], wk["k"][:], wk["E2"][:], ALU.mult, [wk["k"], wk["E2"]], [wk["o2"]])
                self.tt("pool", wk["o3"][:], wk["k"][:], wk["E3"][:], ALU.mult, [wk["k"], wk["E3"]], [wk["o3"]])
                self.st_(wk["o3"], S["kh"][d, tok, cs], wk["o3"][:], W=[self.dbuf("kh", d, b, cg)])
                yield
                for src, dst, nm in ((o1a, wk["TTq"], "qT"), (o1b, wk["TTqb"], "qbT"), (wk["o2"], wk["TTk"], "gkT")):
                    pbt = self.half()
                    for hh in range(4):
                        self.tr(pbt[:, hh * 128:(hh + 1) * 128], src[:, hh * 128:(hh + 1) * 128], [src], [pbt])
                    self.cp("act" if nm != "qbT" else "dve", dst[:].rearrange("p h t -> p (h t)"), pbt[:, :], [pbt], [dst])
                    self.st_(dst, S[nm][d, b, cg * 4:(cg + 1) * 4].rearrange("h k t -> k h t"), dst[:],
                             W=[self.dbuf(nm, d, b, cg)])
                    yield

            hload(0)
            wload(0)
            wload(1)
            nchain = 0
            for k, (si_, cg, kind, col0) in enumerate(pieces):
                wload(k + 2)
                if kind == "q":
                    while chains:
                        run_round()
                if cg == 0 and kind == "q":
                    hload(si_ + 1)
                w = wbuf.pop(k)
                blks = sbs[si_]
                cs = slice(cg * 512, (cg + 1) * 512)
                for bi, b in enumerate(blks):
                    tok = slice(b * 128, (b + 1) * 128)
                    hT = hbuf[si_][bi]
                    pb = self.big()
                    for c in range(8):
                        self.mm(pb[:, :], hT[:, c, :], w[:, c, :], c == 0, c == 7, [hT, w], [pb])
                    if kind == "q":
                        self.act(wkq[bi][:], pb[:, :], AF.Silu, [pb], [wkq[bi]])
                        continue
                    wk = sets[nchain % 2]
                    nchain += 1
                    if kind == "i":
                        self.cp("act", wk["v"][:], pb[:, :], [pb], [wk["v"]])
                        self.st_(wk["v"], S["v"][tok, cs], wk["v"][:], W=[self.dbuf("v", b, cg)])
                        continue
                    if kind == "g":
                        self.act(wk["g"][:], pb[:, :], AF.Silu, [pb], [wk["g"]])
                        self.st_(wk["g"], S["z"][tok, cs], wk["g"][:], W=[self.dbuf("z", b, cg)])
                        continue
                    d = 0 if kind == "f0" else 1
                    chains.append(chain_gen(wk, d, b, cg, wkq[bi], pb))
                    while len(chains) >= 2:
                        run_round()
            while chains:
                run_round()
            self.p.flush()

    def hg_pass_B(self, li):
        S = self.S
        with contextlib.ExitStack() as st:
            t = lambda n, s: self.tile(st, n, s)
            allps = Rot(self.ps)
            ps = allps.next
            ST = [[t("ST%d_%d" % (d, g), [128, 4, 128]) for g in range(4)] for d in range(2)]
            for d in range(2):
                for g in range(4):
                    self.memset("pool", ST[d][g][:], 0.0, [ST[d][g]])
            NP = 4
            QT = self.rot(st, "QT", [128, 4, 128], NP)
            QBT = self.rot(st, "QBT", [128, 4, 128], NP)
            KT = self.rot(st, "KT", [128, 4, 128], NP)
            KH = self.rot(st, "KHh", [128, 512], NP)
            V = self.rot(st, "Vh", [128, 512], NP)
            PCM = self.rot(st, "PCMh", [128, 4, 4], NP)
            AT = self.rot(st, "AT", [128, 4, 128], 4)
            SS = self.rot(st, "SS", [128, 4, 128], 6)
            SM = self.rot(st, "SM", [128, 4, 128], 4)
            TM = self.rot(st, "TM", [128, 4, 128], 4)
            Y = self.rot(st, "Yh", [128, 512], 3)

            def flat(b_):
                return b_[:].rearrange("p h c -> p (h c)")

            def unit_gen(d, b, cg, qt, qbt, kt, kh, v, pcm):
                tok = slice(b * 128, (b + 1) * 128)
                cs = slice(cg * 512, (cg + 1) * 512)
                Sd = ST[d][cg]
                f0, f1 = (0, 1) if d == 0 else (1, 0)
                r0 = slice(f0 * 64, (f0 + 1) * 64)
                r1 = slice(f1 * 64, (f1 + 1) * 64)
                qts = (qt, qbt)

                def pcb(col):
                    return pcm[:, :, col:col + 1].broadcast_to([128, 4, 128])
                pa = ps()
                for hh in range(4):
                    o = pa[:, hh * 128:(hh + 1) * 128]
                    self.mm(o, kt[:, hh, :], qt[:, hh, :], True, False, [kt, qt], [pa])
                    self.mm(o, kt[:, hh, :], qbt[:, hh, :], False, True, [kt, qbt], [pa])
                at = AT.next()
                self.tt("dve", at[:], pa[:, :].rearrange("p (h c) -> p h c", c=128),
                        self.IB64[d][:].unsqueeze(1).broadcast_to([128, 4, 128]), ALU.mult, [pa, self.IB64[d]], [at])
                s0 = SS.next()
                self.tt("pool", s0[:], Sd[:], pcb(2 + f0), ALU.mult, [Sd, pcm], [s0])
                yield
                pn0 = ps()
                for hh in range(4):
                    hsl = slice(hh * 128, (hh + 1) * 128)
                    self.mm(pn0[:, hsl], kh[r0, hsl], v[r0, hsl], True, True, [kh, v], [pn0])
                tm = TM.next()
                self.tt("pool", tm[:], Sd[:], pcb(f0), ALU.mult, [Sd, pcm], [tm])
                smid = SM.next()
                self.tt("dve", flat(smid), flat(tm), pn0[:, :], ALU.add, [tm, pn0], [smid])
                yield
                s1 = SS.next()
                self.tt("pool", s1[:], smid[:], pcb(2 + f1), ALU.mult, [smid, pcm], [s1])
                pn1 = ps()
                for hh in range(4):
                    hsl = slice(hh * 128, (hh + 1) * 128)
                    self.mm(pn1[:, hsl], kh[r1, hsl], v[r1, hsl], True, True, [kh, v], [pn1])
                tm2 = TM.next()
                self.tt("pool", tm2[:], smid[:], pcb(f1), ALU.mult, [smid, pcm], [tm2])
                self.tt("dve", flat(Sd), flat(tm2), pn1[:, :], ALU.add, [tm2, pn1], [Sd])
                yield
                py = ps()
                for hh in range(4):
                    hsl = slice(hh * 128, (hh + 1) * 128)
                    self.mm(py[:, hsl], qts[f0][:, hh, :], s0[:, hh, :], True, False, [qts[f0], s0], [py])
                    self.mm(py[:, hsl], qts[f1][:, hh, :], s1[:, hh, :], False, False, [qts[f1], s1], [py])
                    self.mm(py[:, hsl], at[:, hh, :], v[:, hsl], False, True, [at, v], [py])
                y = Y.next()
                self.cp("act", y[:], py[:, :], [py], [y])
                self.st_(y, S["yd"][d, tok, cs], y[:], W=[self.dbuf("yd", d, b, cg)])
                yield

            active = []

            def run_round():
                for g in list(active):
                    try:
                        next(g)
                    except StopIteration:
                        active.remove(g)
            orders = [self.block_order(0), self.block_order(1)]
            for step_i in range(self.NB):
                for cg in range(4):
                    gens = []
                    for d in range(2):
                        b = orders[d][step_i]
                        tok = slice(b * 128, (b + 1) * 128)
                        cs = slice(cg * 512, (cg + 1) * 512)
                        qt, qbt, kt, kh, v, pcm = QT.next(), QBT.next(), KT.next(), KH.next(), V.next(), PCM.next()
                        hsel = slice(cg * 4, (cg + 1) * 4)
                        self.ld(qt, qt[:], S["qT"][d, b, hsel].rearrange("h k t -> k h t"), R=[self.dbuf("qT", d, b, cg)])
                        self.ld(qbt, qbt[:], S["qbT"][d, b, hsel].rearrange("h k t -> k h t"), R=[self.dbuf("qbT", d, b, cg)])
                        self.ld(kt, kt[:], S["gkT"][d, b, hsel].rearrange("h k t -> k h t"), R=[self.dbuf("gkT", d, b, cg)])
                        self.ld(kh, kh[:], S["kh"][d, tok, cs], R=[self.dbuf("kh", d, b, cg)])
                        self.ld(v, v[:], S["v"][tok, cs], R=[self.dbuf("v", b, cg)])
                        self.ld(pcm, pcm[:], S["pcm"][d, b, :, hsel, :], R=[self.dbuf("pcm", d, b, cg)])
                        gens.append(unit_gen(d, b, cg, qt, qbt, kt, kh, v, pcm))
                    active.extend(gens)
                    while active:
                        run_round()
            self.p.flush()


def build_nc(rows=64, ctx_len=256, depth=4, debug=False):
    nc = bass.Bass("TRN2", target_bir_lowering=False)
    bld = Builder(nc, rows=rows, ctx_len=ctx_len, depth=depth, debug=debug)
    bld.build()
    return nc, bld


PARAM_NAMES = ["mod_w", "mod_b", "pre_g", "post_g", "rw_mix", "rw_proj", "rw_wo", "rw_w0", "rw_w1", "rw_w2",
               "rw_a0", "rw_a1", "rw_a2", "rw_v0", "rw_v1", "rw_v2", "rw_kk", "rw_ka", "rw_rk", "rw_lnw", "rw_lnb",
               "hg_win", "hg_wo", "hg_gn", "hg_lb"]


def kernel(**inputs):
    x = np.ascontiguousarray(inputs["x"], dtype=np.float32)
    B, SEQ, _ = x.shape
    ctx = np.ascontiguousarray(inputs["ctx"], dtype=np.float32)
    c = np.ascontiguousarray(inputs["c"], dtype=np.float32)
    shared = {n: np.ascontiguousarray(inputs[n], dtype=np.float32) for n in PARAM_NAMES}
    shared["c_ctx"] = np.ascontiguousarray(inputs["c_ctx"], dtype=np.float32)
    nc, _ = build_nc(rows=SEQ // 64, ctx_len=ctx.shape[1], depth=4)
    in_maps = []
    for b in range(B):
        m = dict(shared)
        m["x"] = x[b]
        m["ctx"] = ctx[b]
        m["c"] = c[b]
        in_maps.append(m)
    res = run_bass_kernel_spmd(nc, in_maps, core_ids=list(range(B)))
    return np.stack([np.asarray(r["out"]) for r in res.results], axis=0).astype(np.float32)
```

```python
import contextlib
import numpy as np
import concourse.bass as bass
import concourse.mybir as mybir
from concourse.bass_utils import run_bass_kernel_spmd

F32 = mybir.dt.float32
AF = mybir.ActivationFunctionType
ALU = mybir.AluOpType
AX = mybir.AxisListType

D = 1024
DI = 2048
NORM_EPS = 1e-6
LN_X_EPS = 64e-5
WSCALE = -0.6065306597126334


class Buf:
    __slots__ = ("name", "t", "last_w", "readers", "dsem", "dcount")

    def __init__(self, name, t=None):
        self.name = name
        self.t = t
        self.last_w = None
        self.readers = []
        self.dsem = None
        self.dcount = 0

    def __getitem__(self, idx):
        return self.t[idx]


class Rot:
    def __init__(self, bufs):
        self.bufs = bufs
        self.i = 0

    def next(self):
        b = self.bufs[self.i % len(self.bufs)]
        self.i += 1
        return b


class Prog:
    ENGS = ("pe", "act", "dve", "pool", "sp")

    def __init__(self, nc, st, ndma=40):
        self.nc = nc
        self.ops = {e: [] for e in self.ENGS}
        self.count = {e: 0 for e in self.ENGS}
        self.seen = {e: {} for e in self.ENGS}
        self.sems = {}
        for e in self.ENGS:
            self.sems[e] = st.enter_context(nc.semaphore("s_" + e))
        self.dma_sems = [st.enter_context(nc.semaphore("s_d%d" % i)) for i in range(ndma)]
        self.dma_counts = [0] * ndma
        self.dma_next = 0
        self.nops = 0
        import os
        self.maxops = int(os.environ.get("K_MAXOPS", "100000000"))

    def _deps(self, eng, reads, writes, is_pe_mm=False):
        need = {}

        def add(tok):
            if tok is None:
                return
            k, v = tok
            if k == eng and is_pe_mm:
                return
            if need.get(k, 0) < v:
                need[k] = v
        for b in reads:
            add(b.last_w)
        for b in writes:
            add(b.last_w)
            for r in b.readers:
                add(r)
        waits = []
        seen = self.seen[eng]
        for k, v in need.items():
            if seen.get(k, 0) >= v:
                continue
            seen[k] = v
            waits.append((k, v))
        return waits

    def _commit(self, tok, reads, writes):
        for b in reads:
            b.readers.append(tok)
        for b in writes:
            b.last_w = tok
            b.readers = []

    def op(self, eng, fn, reads=(), writes=(), mm=False):
        if self.nops >= self.maxops:
            return None
        waits = self._deps(eng, reads, writes, is_pe_mm=mm)
        self.count[eng] += 1
        tok = (eng, self.count[eng])
        self.ops[eng].append((waits, fn, eng, 1))
        self._commit(tok, reads, writes)
        self.nops += 1
        return tok

    def dma(self, eng, fn, sb, reads=(), writes=()):
        if self.nops >= self.maxops:
            return None
        waits = self._deps(eng, reads, writes)
        if sb.dsem is None:
            sb.dsem = self.dma_next % len(self.dma_sems)
            self.dma_next += 1
        i = sb.dsem
        self.dma_counts[i] += 16
        key = ("d", i)
        tok = (key, self.dma_counts[i])
        self.ops[eng].append((waits, fn, key, 16))
        self._commit(tok, reads, writes)
        self.nops += 1
        return tok

    def _sem(self, k):
        return self.sems[k] if isinstance(k, str) else self.dma_sems[k[1]]

    def flush(self):
        nc = self.nc
        final = [(e, self.count[e]) for e in self.ENGS if self.count[e] > 0]
        final += [(("d", i), c) for i, c in enumerate(self.dma_counts) if c > 0]
        with nc.Block() as block:
            engmap = {"pe": block.tensor, "act": block.scalar, "dve": block.vector,
                      "pool": block.gpsimd, "sp": block.sync}

            def make(ename):
                oplist = self.ops[ename]
                seen = self.seen[ename]

                def body(e):
                    for waits, fn, ik, amt in oplist:
                        for k, v in waits:
                            e.wait_ge(self._sem(k), v)
                        fn(e).then_inc(self._sem(ik), amt)
                    for k, v in final:
                        if k == ename:
                            continue
                        if seen.get(k, 0) >= v:
                            continue
                        seen[k] = v
                        e.wait_ge(self._sem(k), v)
                return body
            for ename in self.ENGS:
                engmap[ename](make(ename))
        self.ops = {e: [] for e in self.ENGS}
        self.dma_next = 0


class Builder:
    def __init__(self, nc, rows=64, ctx_len=256, depth=4, debug=False):
        self.nc = nc
        self.rows = rows
        self.LAT = 64 * rows
        self.CTX = ctx_len
        self.NCB = ctx_len // 128
        self.NLB = self.LAT // 128
        self.NB = self.NCB + self.NLB
        self.T = self.NB * 128
        self.depth = depth
        self.debug = debug
        self.dbufs = {}

    def dram(self, name, shape, kind="Internal"):
        if self.debug and kind == "Internal":
            kind = "ExternalOutput"
        return self.nc.dram_tensor(name, list(shape), F32, kind=kind).ap()

    def dbuf(self, *key):
        b = self.dbufs.get(key)
        if b is None:
            b = Buf(str(key))
            self.dbufs[key] = b
        return b

    def tile(self, st, name, shape):
        self.uid = getattr(self, "uid", 0) + 1
        name = "%s_u%d" % (name, self.uid)
        return Buf(name, st.enter_context(self.nc.sbuf_tensor(name, list(shape), F32)))

    def rot(self, st, name, shape, n):
        return Rot([self.tile(st, "%s%d" % (name, i), shape) for i in range(n)])

    def mm(self, out, lhsT, rhs, start, stop, R, W):
        self.p.op("pe", lambda e: e.matmul(out, lhsT=lhsT, rhs=rhs, start=start, stop=stop),
                  reads=R, writes=W, mm=True)

    def tr(self, out, in_, R, W):
        k = in_.shape[0]
        ident = self.ident[0:k, 0:k]
        self.p.op("pe", lambda e: e.transpose(out, in_, ident), reads=list(R) + [self.ident], writes=W, mm=True)

    def act(self, out, in_, func, R, W, scale=None, bias=None, accum=None):
        kw = {}
        if scale is not None:
            kw["scale"] = scale
        if bias is not None:
            kw["bias"] = bias
        if accum is not None:
            kw["accum_out"] = accum
        self.p.op("act", lambda e: e.activation(out=out, in_=in_, func=func, **kw), reads=R, writes=W)

    def tt(self, eng, out, a, b, op, R, W):
        self.p.op(eng, lambda e: e.tensor_tensor(out=out, in0=a, in1=b, op=op), reads=R, writes=W)

    def ts(self, eng, out, a, s1, op0, R, W, s2=None, op1=None):
        if op1 is None:
            self.p.op(eng, lambda e: e.tensor_scalar(out=out, in0=a, scalar1=s1, scalar2=None, op0=op0),
                      reads=R, writes=W)
        else:
            self.p.op(eng, lambda e: e.tensor_scalar(out=out, in0=a, scalar1=s1, scalar2=s2, op0=op0, op1=op1),
                      reads=R, writes=W)

    def stt(self, out, a, scalar, b, op0, op1, R, W):
        self.p.op("dve", lambda e: e.scalar_tensor_tensor(out=out, in0=a, scalar=scalar, in1=b, op0=op0, op1=op1),
                  reads=R, writes=W)

    def cp(self, eng, out, in_, R, W):
        if eng == "act":
            self.act(out, in_, AF.Copy, R, W)
        else:
            self.p.op(eng, lambda e: e.tensor_copy(out=out, in_=in_), reads=R, writes=W)

    def memset(self, eng, out, val, W):
        self.p.op(eng, lambda e: e.memset(out, val), writes=W)

    def red(self, out, in_, R, W):
        self.p.op("dve", lambda e: e.tensor_reduce(out=out, in_=in_, axis=AX.X, op=ALU.add), reads=R, writes=W)

    def ld(self, buf, out_ap, in_ap, R=(), q="sp"):
        return self.p.dma(q, lambda e: e.dma_start(out=out_ap, in_=in_ap), buf, reads=R, writes=[buf])

    def st_(self, buf, out_ap, in_ap, W=()):
        return self.p.dma("sp", lambda e: e.dma_start(out=out_ap, in_=in_ap), buf, reads=[buf], writes=W)

    def big(self):
        return self.psbig.next()

    def half(self):
        return self.pshalf.next()

    def build(self):
        nc = self.nc
        NB, T = self.NB, self.T
        inp = lambda name, shape: nc.dram_tensor(name, list(shape), F32, kind="ExternalInput").ap()
        I = self.I = {}
        I["x"] = inp("x", [self.LAT, D])
        I["c"] = inp("c", [D])
        I["ctx"] = inp("ctx", [self.CTX, D])
        I["c_ctx"] = inp("c_ctx", [D])
        I["mod_w"] = inp("mod_w", [4, D, 3 * D])
        I["mod_b"] = inp("mod_b", [4, 3 * D])
        I["pre_g"] = inp("pre_g", [4, D])
        I["post_g"] = inp("post_g", [4, D])
        I["rw_mix"] = inp("rw_mix", [2, 6, D])
        I["rw_proj"] = inp("rw_proj", [2, 4, D, DI])
        I["rw_wo"] = inp("rw_wo", [2, DI, D])
        I["rw_w0"] = inp("rw_w0", [2, 2, DI])
        I["rw_w1"] = inp("rw_w1", [2, 2, D, 64])
        I["rw_w2"] = inp("rw_w2", [2, 2, 64, DI])
        I["rw_a0"] = inp("rw_a0", [2, 2, DI])
        I["rw_a1"] = inp("rw_a1", [2, 2, D, 64])
        I["rw_a2"] = inp("rw_a2", [2, 2, 64, DI])
        I["rw_v0"] = inp("rw_v0", [1, DI])
        I["rw_v1"] = inp("rw_v1", [1, D, 32])
        I["rw_v2"] = inp("rw_v2", [1, 32, DI])
        for n in ("rw_kk", "rw_ka", "rw_rk", "rw_lnw", "rw_lnb"):
            I[n] = inp(n, [2, DI])
        I["hg_win"] = inp("hg_win", [2, D, 5 * DI])
        I["hg_wo"] = inp("hg_wo", [2, DI, D])
        I["hg_gn"] = inp("hg_gn", [2, 128])
        I["hg_lb"] = inp("hg_lb", [4, DI])
        self.out = nc.dram_tensor("out", [self.LAT, D], F32, kind="ExternalOutput").ap()

        S = self.S = {}
        S["xs"] = self.dram("xs", [T, D])
        S["hT"] = self.dram("hT", [D, T])
        S["vfirst"] = self.dram("vfirst", [T, DI])
        S["v"] = self.dram("sv", [T, DI])
        S["z"] = self.dram("sz", [T, DI])
        S["bon"] = self.dram("sbon", [T, 32])
        S["yd"] = self.dram("syd", [2, T, DI])
        for n in ("kT", "rT", "bT", "ktT"):
            S[n] = self.dram("s" + n, [2, NB, 32, 64, 128])
        for n in ("bh", "kh"):
            S[n] = self.dram("s" + n, [2, T, DI])
        S["pc"] = self.dram("spc", [2, NB, 64, 32])
        for n in ("qT", "gkT"):
            S[n] = self.dram("s" + n, [2, NB, 16, 128, 128])
        S["pcm"] = self.dram("spcm", [2, NB, 128, 16, 4])
        S["qbT"] = self.dram("sqbT", [2, NB, 16, 128, 128])
        S["mod"] = self.dram("smod", [2, 3 * D])
        S["lb"] = self.dram("slb", [2, DI])

        with contextlib.ExitStack() as gst:
            self.p = Prog(nc, gst)
            self.ps = [Buf("ps%d" % i, gst.enter_context(nc.psum_tensor("ps%d" % i, [128, 512], F32))) for i in range(8)]
            self.psbig = Rot(self.ps[0:4])
            self.pshalf = Rot(self.ps[4:8])
            with contextlib.ExitStack() as tst:
                self.setup_consts(gst, tst)
                self.p.flush()
            import os
            stop = int(os.environ.get("K_STOP", "999"))
            npass = 0
            for li in range(self.depth):
                seq = [lambda: self.pass_H(li)]
                if li % 2 == 0:
                    seq += [lambda: self.rw_pass_A(li), lambda: self.rw_pass_B(li), lambda: self.pass_C(li, rw=True)]
                else:
                    seq += [lambda: self.hg_pass_A(li), lambda: self.hg_pass_B(li), lambda: self.pass_C(li, rw=False)]
                for f in seq:
                    if npass < stop:
                        f()
                    npass += 1
        return nc

    def setup_consts(self, st, tst):
        t = lambda n, s: self.tile(st, n, s)
        tt_ = lambda n, s: self.tile(tst, n, s)
        self.ident = t("ident", [128, 128])
        self.ones = t("ones", [128, 128])
        UI, US, LI, LS = t("UI", [128, 128]), t("US", [128, 128]), t("LI", [128, 128]), t("LS", [128, 128])
        self.memset("pool", self.ones[:], 1.0, [self.ones])

        def sel(dst, step, cm, cmp):
            self.p.op("pool", lambda e: e.affine_select(out=dst[:], in_=self.ones[:], pattern=[[step, 128]],
                                                        compare_op=cmp, fill=0.0, base=0, channel_multiplier=cm),
                      reads=[self.ones], writes=[dst])
        sel(self.ident, -1, 1, ALU.is_equal)
        sel(UI, 1, -1, ALU.is_ge)
        sel(US, 1, -1, ALU.is_gt)
        sel(LI, -1, 1, ALU.is_ge)
        sel(LS, -1, 1, ALU.is_gt)
        self.incl_st = [UI, LI]
        self.strict_st = [US, LS]
        self.strict_ts = [LS, US]
        BD = {}
        for nb_, bs in ((4, 32), (2, 64)):
            E = t("E%d" % bs, [nb_, 128])
            self.p.op("pool", lambda e, E=E, nb_=nb_, bs=bs: e.affine_select(
                out=E[:], in_=self.ones[0:nb_, :], pattern=[[1, 128]], compare_op=ALU.is_ge, fill=0.0, base=0,
                channel_multiplier=-bs), reads=[self.ones], writes=[E])
            self.p.op("pool", lambda e, E=E, nb_=nb_, bs=bs: e.affine_select(
                out=E[:], in_=E[:], pattern=[[-1, 128]], compare_op=ALU.is_ge, fill=0.0, base=bs - 1,
                channel_multiplier=bs), reads=[E], writes=[E])
            pb = self.big()
            self.mm(pb[:, 0:128], E[:], E[:], True, True, [E], [pb])
            bd = t("BD%d" % bs, [128, 128])
            self.cp("dve", bd[:], pb[:, 0:128], [pb], [bd])
            BD[bs] = bd
        D64m32 = t("D64m32", [128, 128])
        self.tt("dve", D64m32[:], BD[64][:], BD[32][:], ALU.subtract, [BD[64], BD[32]], [D64m32])
        N64 = t("N64", [128, 128])
        self.ts("dve", N64[:], BD[64][:], -1.0, ALU.mult, [BD[64]], [N64], s2=1.0, op1=ALU.add)
        self.CM1, self.CM2, self.HD, self.OM, self.IB64, self.REM64 = [], [], [], [], [], []
        plt32, pge64, pge96 = t("plt32", [128, 1]), t("pge64", [128, 1]), t("pge96", [128, 1])
        for dst_, base_, cm_ in ((plt32, 31, -1), (pge64, -64, 1), (pge96, -96, 1)):
            self.p.op("pool", lambda e, dst_=dst_, base_=base_, cm_=cm_: e.affine_select(
                out=dst_[:], in_=self.ones[:, 0:1], pattern=[[0, 1]], compare_op=ALU.is_ge, fill=0.0, base=base_,
                channel_multiplier=cm_), reads=[self.ones], writes=[dst_])
        self.NBD_ts, self.C1_ts, self.C2_ts, self.C1_st, self.NBD_st = [], [], [], [], []
        for d in range(2):
            cm1 = t("CM1_%d" % d, [128, 256])
            cm2 = t("CM2_%d" % d, [128, 256])
            self.stt(cm1[:, 0:128], self.strict_st[d][:], -1.0, BD[32][:], ALU.mult, ALU.mult, [self.strict_st[d], BD[32]], [cm1])
            self.cp("dve", cm1[:, 128:256], self.incl_st[d][:], [self.incl_st[d]], [cm1])
            self.cp("dve", cm2[:, 0:128], self.strict_st[d][:], [self.strict_st[d]], [cm2])
            self.cp("dve", cm2[:, 128:256], self.incl_st[d][:], [self.incl_st[d]], [cm2])
            nbd = t("NBDts_%d" % d, [128, 128])
            self.stt(nbd[:], self.strict_ts[d][:], -1.0, BD[32][:], ALU.mult, ALU.mult, [self.strict_ts[d], BD[32]], [nbd])
            c1ts = t("C1ts_%d" % d, [128, 128])
            self.tt("dve", c1ts[:], self.strict_ts[d][:], D64m32[:], ALU.mult, [self.strict_ts[d], D64m32], [c1ts])
            c2ts = t("C2ts_%d" % d, [128, 128])
            self.tt("dve", c2ts[:], self.strict_ts[d][:], N64[:], ALU.mult, [self.strict_ts[d], N64], [c2ts])
            nbdst = t("NBDst_%d" % d, [128, 128])
            self.cp("dve", nbdst[:], cm1[:, 0:128], [cm1], [nbdst])
            self.NBD_st.append(nbdst)
            c1st = t("C1st_%d" % d, [128, 128])
            self.tt("dve", c1st[:], self.strict_st[d][:], D64m32[:], ALU.mult, [self.strict_st[d], D64m32], [c1st])
            self.CM1.append(cm1)
            self.CM2.append(cm2)
            self.NBD_ts.append(nbd)
            self.C1_ts.append(c1ts)
            self.C2_ts.append(c2ts)
            self.C1_st.append(c1st)
            mcol = t("mcol_%d" % d, [128, 1])
            self.tt("dve", mcol[:], plt32[:], pge64[:], ALU.add, [plt32, pge64], [mcol])
            self.tt("dve", mcol[:], mcol[:], pge96[:], ALU.subtract, [mcol, pge96], [mcol])
            if d == 1:
                self.ts("dve", mcol[:], mcol[:], -1.0, ALU.mult, [mcol], [mcol], s2=1.0, op1=ALU.add)
            midh = t("MIDH_%d" % d, [128, 128])
            self.ts("dve", midh[:], BD[64][:], mcol[:, 0:1], ALU.mult, [BD[64], mcol], [midh])
            ib = t("IB64_%d" % d, [128, 128])
            self.tt("dve", ib[:], self.incl_st[d][:], BD[64][:], ALU.mult, [self.incl_st[d], BD[64]], [ib])
            hd = t("HD_%d" % d, [128, 128])
            self.tt("dve", hd[:], ib[:], midh[:], ALU.subtract, [ib, midh], [hd])
            rem = t("REM64_%d" % d, [128, 128])
            self.tt("dve", rem[:], self.strict_ts[d][:], BD[64][:], ALU.mult, [self.strict_ts[d], BD[64]], [rem])
            om = t("OM_%d" % d, [128, 4])
            self.cp("dve", om[:, 0:1], BD[64][:, 0:1], [BD[64]], [om])
            self.cp("dve", om[:, 1:2], BD[64][:, 127:128], [BD[64]], [om])
            self.tt("dve", om[:, 2:3], om[:, 0:1], mcol[:], ALU.mult, [om, mcol], [om])
            self.tt("dve", om[:, 3:4], om[:, 1:2], mcol[:], ALU.mult, [om, mcol], [om])
            self.HD.append(hd)
            self.OM.append(om)
            self.IB64.append(ib)
            self.REM64.append(rem)
        self.sel2 = []
        for r in range(2):
            s = t("sel2_%d" % r, [2, 128])
            self.memset("pool", s[:], float(r), [s])
            self.memset("pool", s[0:1, :], float(1 - r), [s])
            self.sel2.append(s)
        self.scT = t("scT", [128, 8, 2])
        crow = tt_("crow", [2, D])
        self.ld(crow, crow[0:1, :], self.I["c"].unsqueeze(0))
        self.ld(crow, crow[1:2, :], self.I["c_ctx"].unsqueeze(0))
        self.act(crow[:], crow[:], AF.Silu, [crow], [crow])
        pb = self.big()
        for c in range(8):
            self.tr(pb[:, c * 2:c * 2 + 2], crow[:, c * 128:(c + 1) * 128], [crow], [pb])
        self.cp("dve", self.scT[:].rearrange("p c r -> p (c r)"), pb[:, 0:16], [pb], [self.scT])
        self.lbexp = tt_("lbexp", [1, 4, DI])
        self.ld(self.lbexp, self.lbexp[:].rearrange("p l c -> p (l c)"),
                self.I["hg_lb"].rearrange("l c -> (l c)").unsqueeze(0))
        self.act(self.lbexp[:], self.lbexp[:], AF.Exp, [self.lbexp], [self.lbexp])
        self.lbtot = tt_("lbtot", [1, DI])
        L = self.lbexp
        self.tt("dve", self.lbtot[:], L[:, 0, :], L[:, 1, :], ALU.add, [L], [self.lbtot])
        self.tt("dve", self.lbtot[:], self.lbtot[:], L[:, 2, :], ALU.add, [L, self.lbtot], [self.lbtot])
        self.tt("dve", self.lbtot[:], self.lbtot[:], L[:, 3, :], ALU.add, [L, self.lbtot], [self.lbtot])
        self.p.op("dve", lambda e: e.reciprocal(out=self.lbtot[:], in_=self.lbtot[:]), reads=[self.lbtot], writes=[self.lbtot])
        self.lbrow = tt_("lbrow", [1, 2, DI])
        self.tt("dve", self.lbrow[:, 0, :], L[:, 1, :], self.lbtot[:], ALU.mult, [L, self.lbtot], [self.lbrow])
        self.tt("dve", self.lbrow[:, 1, :], L[:, 1, :], L[:, 2, :], ALU.add, [L], [self.lbrow])
        self.tt("dve", self.lbrow[:, 1, :], self.lbrow[:, 1, :], L[:, 3, :], ALU.add, [L, self.lbrow], [self.lbrow])
        self.tt("dve", self.lbrow[:, 1, :], self.lbrow[:, 1, :], self.lbtot[:], ALU.mult, [self.lbrow, self.lbtot], [self.lbrow])
        self.st_(self.lbrow, self.S["lb"].unsqueeze(0), self.lbrow[:], W=[self.dbuf("lb")])

    def x_ap(self, li, b):
        if li == 0:
            if b < self.NCB:
                return self.I["ctx"][b * 128:(b + 1) * 128, :], self.dbuf("in_ctx", b)
            bb = b - self.NCB
            return self.I["x"][bb * 128:(bb + 1) * 128, :], self.dbuf("in_x", bb)
        return self.S["xs"][b * 128:(b + 1) * 128, :], self.dbuf("xs", b)

    def x_out_ap(self, li, b):
        if li == self.depth - 1 and b >= self.NCB:
            bb = b - self.NCB
            return self.out[bb * 128:(bb + 1) * 128, :], self.dbuf("out", bb)
        return self.S["xs"][b * 128:(b + 1) * 128, :], self.dbuf("xs", b)

    def bcast_rows(self, st, rows_buf, col0, ncol, name):
        outs = []
        for r in range(2):
            tl = self.tile(st, "%s_%d" % (name, r), [128, ncol])
            for j in range(0, ncol, 512):
                pb = self.big()
                self.mm(pb[:, 0:512], self.sel2[r][:], rows_buf[:, col0 + j:col0 + j + 512], True, True,
                        [self.sel2[r], rows_buf], [pb])
                self.cp("act", tl[:, j:j + 512], pb[:, 0:512], [pb], [tl])
            outs.append(tl)
        return outs

    def pass_H(self, li):
        with contextlib.ExitStack() as st:
            I = self.I
            mw = self.rot(st, "mw", [128, 3 * D], 2)
            banks = [self.ps[i] for i in range(6)]
            for c in range(8):
                w = mw.next()
                self.ld(w, w[:], I["mod_w"][li, c * 128:(c + 1) * 128, :])
                for j in range(6):
                    self.mm(banks[j][0:2, :], self.scT[:, c, :], w[:, j * 512:(j + 1) * 512], c == 0, c == 7,
                            [self.scT, w], [banks[j]])
            mb = self.tile(st, "mb", [2, 3 * D])
            self.ld(mb, mb[:], I["mod_b"][li].partition_broadcast(2))
            self.modrow = self.tile(st, "modrow", [2, 3 * D])
            for j in range(6):
                self.tt("dve", self.modrow[:, j * 512:(j + 1) * 512], banks[j][0:2, :], mb[:, j * 512:(j + 1) * 512],
                        ALU.add, [banks[j], mb], [self.modrow])
            self.st_(self.modrow, self.S["mod"], self.modrow[:], W=[self.dbuf("mod")])
            pg = self.tile(st, "pg", [2, D])
            self.ld(pg, pg[:], I["pre_g"][li].partition_broadcast(2))
            grow = self.tile(st, "grow", [2, D])
            self.stt(grow[:], self.modrow[:, D:2 * D], 1.0, pg[:], ALU.add, ALU.mult, [self.modrow, pg], [grow])
            G = self.bcast_rows(st, grow, 0, D, "G")
            Sh = self.bcast_rows(st, self.modrow, 0, D, "Sh")
            xt = self.rot(st, "xt", [128, D], 2)
            ht = self.rot(st, "ht", [128, D], 2)
            hTt = self.rot(st, "hTt", [128, 8, 128], 2)
            sst = self.rot(st, "ss", [128, 2], 2)
            junk = self.tile(st, "junk", [128, D])
            for b in range(self.NB):
                seg = 1 if b < self.NCB else 0
                xap, xb = self.x_ap(li, b)
                x = xt.next()
                self.ld(x, x[:], xap, R=[xb])
                ss = sst.next()
                self.act(junk[:], x[:], AF.Square, [x], [junk, ss], accum=ss[:, 0:1])
                self.act(ss[:, 1:2], ss[:, 0:1], AF.Ln, [ss], [ss], scale=1.0 / D, bias=NORM_EPS)
                self.act(ss[:, 1:2], ss[:, 1:2], AF.Exp, [ss], [ss], scale=-0.5)
                h = ht.next()
                self.stt(h[:], x[:], ss[:, 1:2], G[seg][:], ALU.mult, ALU.mult, [x, ss, G[seg]], [h])
                self.tt("pool", h[:], h[:], Sh[seg][:], ALU.add, [h, Sh[seg]], [h])
                hT = hTt.next()
                for g in range(2):
                    pb = self.big()
                    for c in range(4):
                        cc = g * 4 + c
                        self.tr(pb[:, c * 128:(c + 1) * 128], h[:, cc * 128:(cc + 1) * 128], [h], [pb])
                    self.cp("act" if g == 0 else "dve", hT[:, g * 4:(g + 1) * 4, :].rearrange("p c t -> p (c t)"),
                            pb[:, :], [pb], [hT])
                self.st_(hT, self.S["hT"].rearrange("(c p) t -> p c t", p=128)[:, :, b * 128:(b + 1) * 128], hT[:],
                         W=[self.dbuf("hT", b)])
            self.p.flush()

    def rw_pass_A(self, li):
        j = li // 2
        I, S = self.I, self.S
        NCB = self.NCB
        with contextlib.ExitStack() as st:
            t = lambda n, s: self.tile(st, n, s)
            mixrow = t("mixrow", [6, D])
            self.ld(mixrow, mixrow[:], I["rw_mix"][j])
            MIX = t("MIX", [128, 8, 6])
            pb = self.big()
            for c in range(8):
                self.tr(pb[:, c * 6:c * 6 + 6], mixrow[:, c * 128:(c + 1) * 128], [mixrow], [pb])
            self.cp("dve", MIX[:].rearrange("p c r -> p (c r)"), pb[:, 0:48], [pb], [MIX])
            W1 = t("W1", [128, 8, 128])
            A1 = t("A1", [128, 8, 128])
            for d in range(2):
                self.ld(W1, W1[:, :, d * 64:(d + 1) * 64], I["rw_w1"][j, d].rearrange("(c p) r -> p c r", p=128))
                self.ld(A1, A1[:, :, d * 64:(d + 1) * 64], I["rw_a1"][j, d].rearrange("(c p) r -> p c r", p=128))
            if j > 0:
                V1 = t("V1", [128, 8, 32])
                self.ld(V1, V1[:], I["rw_v1"][j - 1].rearrange("(c p) r -> p c r", p=128))
            TW = [t("TW%d" % d, [65, 128]) for d in range(2)]
            TA = [t("TA%d" % d, [65, 128]) for d in range(2)]
            TV = t("TV", [33, 128])
            for x_ in TW + TA:
                self.memset("pool", x_[64:65, :], 1.0, [x_])
            self.memset("pool", TV[32:33, :], 1.0, [TV])
            HW = t("HW", [128, 8, 256])
            XX = t("XX", [128, 8, 128])
            XN = [t("XN%d" % n, [128, 8, 128]) for n in range(6)]
            WT = self.rot(st, "WT", [128, 8, 512], 2)
            LR = self.rot(st, "LR", [65, 5, 512], 2)
            VPr = self.rot(st, "VP", [128, 3, 512], 2)
            wk = {n: t("wk_" + n, [128, 512]) for n in
                  ("r", "k", "v", "z", "kkr", "sq", "kkn", "rk", "vf", "sigv", "sg", "a", "t1", "kd", "bb",
                   "EI", "EnI", "ES", "ER", "o1", "o2", "o3", "o4", "o5", "o6", "prod")}
            sm = {n: t("sm_" + n, [128, 8]) for n in ("ss", "rn", "bon0", "bon1", "bon")}
            TT = {n: t("TT_" + n, [64, 8, 128]) for n in ("kT", "bT", "ktT", "rT")}
            PCt = t("PCt", [64, 8])

            def v3(ap):
                return ap.rearrange("p (h k) -> p h k", k=64)

            wbuf, lrbuf = {}, {}
            npieces = self.NB * 16

            def wload(k):
                if k >= npieces or k in wbuf:
                    return
                cg_, n_ = (k // 4) % 4, k % 4
                w = WT.next()
                self.ld(w, w[:], I["rw_proj"][j, n_, :, cg_ * 512:(cg_ + 1) * 512].rearrange("(c p) n -> p c n", p=128),
                        q="sp" if k < 2 else "act")
                wbuf[k] = w

            def lrload(k):
                if k >= self.NB * 4 or k in lrbuf:
                    return
                cs_ = slice((k % 4) * 512, (k % 4 + 1) * 512)
                lr = LR.next()
                for d in range(2):
                    self.ld(lr, lr[0:64, d, :], I["rw_w2"][j, d, :, cs_])
                    self.ld(lr, lr[64:65, d, :], I["rw_w0"][j, d, cs_].unsqueeze(0))
                    self.ld(lr, lr[0:64, 2 + d, :], I["rw_a2"][j, d, :, cs_])
                    self.ld(lr, lr[64:65, 2 + d, :], I["rw_a0"][j, d, cs_].unsqueeze(0))
                if j > 0:
                    self.ld(lr, lr[0:32, 4, :], I["rw_v2"][j - 1, :, cs_])
                    self.ld(lr, lr[32:33, 4, :], I["rw_v0"][j - 1, cs_].unsqueeze(0))
                lrbuf[k] = lr
            vpbuf = {}

            def vpload(k):
                if k >= self.NB * 4 or k in vpbuf:
                    return
                cs_ = slice((k % 4) * 512, (k % 4 + 1) * 512)
                vp = VPr.next()
                for i_, nm in enumerate(("rw_kk", "rw_ka", "rw_rk")):
                    self.ld(vp, vp[:, i_, :], I[nm][j, cs_].partition_broadcast(128))
                vpbuf[k] = vp
            wload(0)
            wload(1)
            lrload(0)
            vpload(0)

            for b in range(self.NB):
                ctx = b < NCB
                s0, s1 = (0, NCB * 128) if ctx else (NCB * 128, self.T)
                w0, w1 = b * 128 - 64, b * 128 + 192
                v0, v1_ = max(w0, s0), min(w1, s1)
                if v0 > w0:
                    self.memset("pool", HW[:, :, 0:v0 - w0], 0.0, [HW])
                if v1_ < w1:
                    self.memset("pool", HW[:, :, v1_ - w0:256], 0.0, [HW])
                nb_lo, nb_hi = v0 // 128, (v1_ - 1) // 128
                self.ld(HW, HW[:, :, v0 - w0:v1_ - w0],
                        S["hT"].rearrange("(c p) t -> p c t", p=128)[:, :, v0:v1_],
                        R=[self.dbuf("hT", q) for q in range(nb_lo, nb_hi + 1)])
                hc = HW[:, :, 64:192]
                if ctx:
                    self.tt("dve", XX[:, 0:4, :], HW[:, 0:4, 63:191], HW[:, 0:4, 64:192], ALU.subtract, [HW], [XX])
                    self.tt("pool", XX[:, 4:8, :], HW[:, 4:8, 65:193], HW[:, 4:8, 64:192], ALU.subtract, [HW], [XX])
                else:
                    self.tt("dve", XX[:, 0:2, :], HW[:, 0:2, 63:191], HW[:, 0:2, 64:192], ALU.subtract, [HW], [XX])
                    self.tt("pool", XX[:, 2:4, :], HW[:, 2:4, 65:193], HW[:, 2:4, 64:192], ALU.subtract, [HW], [XX])
                    self.tt("dve", XX[:, 4:6, :], HW[:, 4:6, 0:128], HW[:, 4:6, 64:192], ALU.subtract, [HW], [XX])
                    self.tt("pool", XX[:, 6:8, :], HW[:, 6:8, 128:256], HW[:, 6:8, 64:192], ALU.subtract, [HW], [XX])
                    self.ts("dve", XX[:, 0:2, 0:128:64], HW[:, 0:2, 64:192:64], -1.0, ALU.mult, [HW], [XX])
                    self.ts("dve", XX[:, 2:4, 63:128:64], HW[:, 2:4, 127:192:64], -1.0, ALU.mult, [HW], [XX])
                for n in range(6):
                    for c in range(8):
                        self.stt(XN[n][:, c, :], XX[:, c, :], MIX[:, c, n:n + 1], HW[:, c, 64:192],
                                 ALU.mult, ALU.add, [XX, MIX, HW], [XN[n]])
                xr, xw, xk, xv, xa, xg = XN
                for d in range(2):
                    ph = self.half()
                    for c in range(8):
                        self.mm(ph[0:64, 0:128], W1[:, c, d * 64:(d + 1) * 64], xw[:, c, :], c == 0, c == 7, [W1, xw], [ph])
                    self.act(TW[d][0:64, :], ph[0:64, 0:128], AF.Tanh, [ph], [TW[d]])
                    ph = self.half()
                    for c in range(8):
                        self.mm(ph[0:64, 0:128], A1[:, c, d * 64:(d + 1) * 64], xa[:, c, :], c == 0, c == 7, [A1, xa], [ph])
                    self.cp("dve", TA[d][0:64, :], ph[0:64, 0:128], [ph], [TA[d]])
                if j > 0:
                    ph = self.half()
                    for c in range(8):
                        self.mm(ph[0:32, 0:128], V1[:, c, :], xv[:, c, :], c == 0, c == 7, [V1, xv], [ph])
                    self.cp("dve", TV[0:32, :], ph[0:32, 0:128], [ph], [TV])
                tok = slice(b * 128, (b + 1) * 128)
                for cg in range(4):
                    cs = slice(cg * 512, (cg + 1) * 512)
                    vpload(b * 4 + cg + 1)
                    VP = vpbuf.pop(b * 4 + cg)
                    lrload(b * 4 + cg + 1)
                    lr = lrbuf.pop(b * 4 + cg)
                    pbs = []
                    for n_, xin in enumerate((xr, xk, xv, xg)):
                        kpiece = (b * 4 + cg) * 4 + n_
                        w = wbuf.pop(kpiece)
                        pb = self.big()
                        for c in range(8):
                            self.mm(pb[:, :], xin[:, c, :], w[:, c, :], c == 0, c == 7, [xin, w], [pb])
                        pbs.append(pb)
                        dst = wk[("r", "k", "v", "z")[n_]]
                        if n_ == 3:
                            self.act(dst[:], pb[:, :], AF.Silu, [pb], [dst])
                        else:
                            self.cp("act", dst[:], pb[:, :], [pb], [dst])
                        wload(kpiece + 2)
                    self.st_(wk["z"], S["z"][tok, cs], wk["z"][:], W=[self.dbuf("z", b, cg)])
                    if j == 0:
                        self.st_(wk["v"], S["vfirst"][tok, cs], wk["v"][:], W=[self.dbuf("vfirst", b, cg)])
                    else:
                        self.ld(wk["vf"], wk["vf"][:], S["vfirst"][tok, cs], R=[self.dbuf("vfirst", b, cg)])
                        pb = self.big()
                        self.mm(pb[:, :], TV[0:33, :], lr[0:33, 4, :], True, True, [TV, lr], [pb])
                        self.act(wk["sigv"][:], pb[:, :], AF.Sigmoid, [pb], [wk["sigv"]])
                        self.tt("pool", wk["vf"][:], wk["vf"][:], wk["v"][:], ALU.subtract, [wk["vf"], wk["v"]], [wk["vf"]])
                        self.tt("dve", wk["vf"][:], wk["vf"][:], wk["sigv"][:], ALU.mult, [wk["vf"], wk["sigv"]], [wk["vf"]])
                        self.tt("pool", wk["v"][:], wk["v"][:], wk["vf"][:], ALU.add, [wk["v"], wk["vf"]], [wk["v"]])
                    self.st_(wk["v"], S["v"][tok, cs], wk["v"][:], W=[self.dbuf("v", b, cg)])
                    self.tt("pool", wk["kkr"][:], wk["k"][:], VP[:, 0, :], ALU.mult, [wk["k"], VP], [wk["kkr"]])
                    self.tt("pool", wk["sq"][:], wk["kkr"][:], wk["kkr"][:], ALU.mult, [wk["kkr"]], [wk["sq"]])
                    self.red(sm["ss"][:], v3(wk["sq"][:]), [wk["sq"]], [sm["ss"]])
                    self.ts("dve", sm["ss"][:], sm["ss"][:], 1e-24, ALU.max, [sm["ss"]], [sm["ss"]])
                    self.act(sm["rn"][:], sm["ss"][:], AF.Ln, [sm["ss"]], [sm["rn"]])
                    self.act(sm["rn"][:], sm["rn"][:], AF.Exp, [sm["rn"]], [sm["rn"]], scale=-0.5)
                    self.tt("dve", v3(wk["kkn"][:]), v3(wk["kkr"][:]), sm["rn"][:].unsqueeze(2).broadcast_to([128, 8, 64]),
                            ALU.mult, [wk["kkr"], sm["rn"]], [wk["kkn"]])
                    self.tt("pool", wk["rk"][:], wk["r"][:], VP[:, 2, :], ALU.mult, [wk["r"], VP], [wk["rk"]])
                    for d in range(2):
                        pu = self.big()
                        self.mm(pu[:, :], TW[d][0:65, :], lr[0:65, d, :], True, True, [TW[d], lr], [pu])
                        self.act(wk["sg"][:], pu[:, :], AF.Sigmoid, [pu], [wk["sg"]])
                        pa = self.big()
                        self.mm(pa[:, :], TA[d][0:65, :], lr[0:65, 2 + d, :], True, True, [TA[d], lr], [pa])
                        self.act(wk["a"][:], pa[:, :], AF.Sigmoid, [pa], [wk["a"]])
                        self.stt(wk["t1"][:], wk["a"][:], -1.0, VP[:, 1, :], ALU.add, ALU.mult, [wk["a"], VP], [wk["t1"]])
                        self.stt(wk["kd"][:], wk["t1"][:], 1.0, wk["k"][:], ALU.add, ALU.mult, [wk["t1"], wk["k"]], [wk["kd"]])
                        self.tt("pool", wk["prod"][:], wk["rk"][:], wk["kd"][:], ALU.mult, [wk["rk"], wk["kd"]], [wk["prod"]])
                        bon = sm["bon%d" % d]
                        self.red(bon[:], v3(wk["prod"][:]), [wk["prod"]], [bon])
                        self.tt("pool", wk["bb"][:], wk["a"][:], wk["kkn"][:], ALU.mult, [wk["a"], wk["kkn"]], [wk["bb"]])
                        sg = wk["sg"]
                        pI = self.big()
                        self.mm(pI[:, :], self.incl_st[d][:], sg[:], True, True, [self.incl_st[d], sg], [pI])
                        self.act(wk["EI"][:], pI[:, :], AF.Exp, [pI], [wk["EI"]], scale=WSCALE)
                        self.act(wk["EnI"][:], pI[:, :], AF.Exp, [pI], [wk["EnI"]], scale=-WSCALE)
                        pS = self.big()
                        self.mm(pS[:, :], self.strict_st[d][:], sg[:], True, True, [self.strict_st[d], sg], [pS])
                        self.act(wk["ES"][:], pS[:, :], AF.Exp, [pS], [wk["ES"]], scale=WSCALE)
                        pR = self.big()
                        self.mm(pR[:, :], self.strict_ts[d][:], sg[:], True, True, [self.strict_ts[d], sg], [pR])
                        self.act(wk["ER"][:], pR[:, :], AF.Exp, [pR], [wk["ER"]], scale=WSCALE)
                        self.tt("dve", wk["o1"][:], wk["kkn"][:], wk["ES"][:], ALU.mult, [wk["kkn"], wk["ES"]], [wk["o1"]])
                        self.tt("pool", wk["o2"][:], wk["bb"][:], wk["EnI"][:], ALU.mult, [wk["bb"], wk["EnI"]], [wk["o2"]])
                        self.tt("dve", wk["o3"][:], wk["kd"][:], wk["EnI"][:], ALU.mult, [wk["kd"], wk["EnI"]], [wk["o3"]])
                        self.tt("pool", wk["o4"][:], wk["r"][:], wk["EI"][:], ALU.mult, [wk["r"], wk["EI"]], [wk["o4"]])
                        self.tt("dve", wk["o5"][:], wk["bb"][:], wk["ER"][:], ALU.mult, [wk["bb"], wk["ER"]], [wk["o5"]])
                        self.tt("pool", wk["o6"][:], wk["kd"][:], wk["ER"][:], ALU.mult, [wk["kd"], wk["ER"]], [wk["o6"]])
                        self.st_(wk["o5"], S["bh"][d, tok, cs], wk["o5"][:], W=[self.dbuf("bh", d, b, cg)])
                        self.st_(wk["o6"], S["kh"][d, tok, cs], wk["o6"][:], W=[self.dbuf("kh", d, b, cg)])
                        for src, nm in ((wk["o1"], "kT"), (wk["o2"], "bT"), (wk["o3"], "ktT"), (wk["o4"], "rT")):
                            dst = TT[nm]
                            for g in range(2):
                                pb = self.big()
                                for hh in range(4):
                                    h8 = g * 4 + hh
                                    self.tr(pb[0:64, hh * 128:(hh + 1) * 128], src[:, h8 * 64:(h8 + 1) * 64], [src], [pb])
                                self.cp("act" if g == 0 else "dve", dst[:, g * 4:(g + 1) * 4, :].rearrange("p h t -> p (h t)"),
                                        pb[0:64, :], [pb], [dst])
                            self.st_(dst, S[nm][d, b, cg * 8:(cg + 1) * 8].rearrange("h k t -> k h t"), dst[:],
                                     W=[self.dbuf(nm, d, b, cg)])
                        ph = self.half()
                        for h8 in range(8):
                            self.mm(ph[0:64, h8:h8 + 1], sg[:, h8 * 64:(h8 + 1) * 64], self.ones[:, 0:1], True, True,
                                    [sg, self.ones], [ph])
                        self.act(PCt[:], ph[0:64, 0:8], AF.Exp, [ph], [PCt], scale=WSCALE)
                        self.st_(PCt, S["pc"][d, b, :, cg * 8:(cg + 1) * 8], PCt[:], W=[self.dbuf("pc", d, b, cg)])
                    self.tt("dve", sm["bon"][:], sm["bon0"][:], sm["bon1"][:], ALU.add, [sm["bon0"], sm["bon1"]], [sm["bon"]])
                    self.st_(sm["bon"], S["bon"][tok, cg * 8:(cg + 1) * 8], sm["bon"][:], W=[self.dbuf("bon", b, cg)])
            self.p.flush()

    def block_order(self, d):
        NCB, NB = self.NCB, self.NB
        if d == 0:
            return list(range(NB))
        return list(range(NCB - 1, -1, -1)) + list(range(NB - 1, NCB - 1, -1))

    def rw_pass_B(self, li):
        S = self.S
        with contextlib.ExitStack() as st:
            t = lambda n, s: self.tile(st, n, s)
            allps = Rot(self.ps)
            ps = allps.next
            H = [[t("H%d_%d" % (d, g), [64, 8, 64]) for g in range(4)] for d in range(2)]
            for d in range(2):
                for g in range(4):
                    self.memset("pool", H[d][g][:], 0.0, [H[d][g]])
            KR = self.rot(st, "KR", [64, 8, 2, 128], 2)
            BT = self.rot(st, "BT", [64, 8, 128], 2)
            KtT = self.rot(st, "KtT", [64, 8, 128], 2)
            BH = self.rot(st, "BH", [128, 512], 2)
            KH = self.rot(st, "KH", [128, 512], 2)
            V = self.rot(st, "V", [128, 512], 2)
            PC = self.rot(st, "PC", [64, 8], 2)
            MF = self.rot(st, "MF", [128, 8, 128], 2)
            LKA = self.rot(st, "LKA", [128, 8, 256], 2)
            ARB = self.rot(st, "ARB", [128, 8, 128], 2)
            Rsb = self.rot(st, "Rsb", [128, 512], 2)
            Usb = self.rot(st, "Usb", [128, 512], 2)
            Y = self.rot(st, "Y", [128, 512], 2)
            Htmp = self.rot(st, "Htmp", [64, 512], 2)
            NSLOT = 3
            GL = [[t("GL%d_%d" % (g, i), [128, 4, 128]) for i in range(3)] for g in range(NSLOT)]
            GR = [self.rot(st, "GR%d_" % g, [128, 4, 128], 8) for g in range(NSLOT)]

            def flat(b_):
                return b_[:].rearrange("p h c -> p (h c)")

            def b4(m_, n=4):
                return m_[:].unsqueeze(1).broadcast_to([128, n, 128])

            def group_gen(d, kr, bt, ktt, lka, arb, mf, g4, slot):
                c1t, c1, c2 = GL[slot]
                gr = GR[slot]
                heads = [g4 * 4 + q for q in range(4)]
                h0 = heads[0]
                p1 = [ps(), ps()]
                for q, h in enumerate(heads):
                    krh = kr[:, h, :, :].rearrange("k a t -> k (a t)")
                    self.mm(p1[q // 2][:, (q % 2) * 256:(q % 2 + 1) * 256], bt[:, h, :], krh, True, True, [bt, kr], [p1[q // 2]])
                qT = gr.next()
                for k in range(2):
                    pv = p1[k][:, :].rearrange("p (h c) -> p h c", c=256)
                    self.tt("dve", qT[:, 2 * k:2 * k + 2, :], pv[:, :, 0:128], b4(self.NBD_st[d], 2), ALU.mult,
                            [p1[k], self.NBD_st[d]], [qT])
                    self.tt("dve", arb[:, h0 + 2 * k:h0 + 2 * k + 2, :], pv[:, :, 128:256], b4(self.incl_st[d], 2), ALU.mult,
                            [p1[k], self.incl_st[d]], [arb])
                    self.tt("dve", c1t[:, 2 * k:2 * k + 2, :], pv[:, :, 0:128], b4(self.C1_st[d], 2), ALU.mult,
                            [p1[k], self.C1_st[d]], [c1t])
                yield
                p2 = [ps(), ps()]
                for q, h in enumerate(heads):
                    krh = kr[:, h, :, :].rearrange("k a t -> k (a t)")
                    self.mm(p2[q // 2][:, (q % 2) * 256:(q % 2 + 1) * 256], ktt[:, h, :], krh, True, True, [ktt, kr], [p2[q // 2]])
                for k in range(2):
                    self.tt("dve", lka[:, h0 + 2 * k:h0 + 2 * k + 2, :], p2[k][:, :].rearrange("p (h c) -> p h c", c=256),
                            self.CM2[d][:].unsqueeze(1).broadcast_to([128, 2, 256]), ALU.mult, [p2[k], self.CM2[d]], [lka])
                yield
                p3 = ps()
                for q, h in enumerate(heads):
                    self.mm(p3[:, q * 128:(q + 1) * 128], kr[:, h, 0, :], bt[:, h, :], True, True, [kr, bt], [p3])
                p3v = p3[:, :].rearrange("p (h c) -> p h c", c=128)
                q0 = gr.next()
                self.tt("dve", q0[:], p3v, b4(self.NBD_ts[d]), ALU.mult, [p3, self.NBD_ts[d]], [q0])
                self.tt("dve", c1[:], p3v, b4(self.C1_ts[d]), ALU.mult, [p3, self.C1_ts[d]], [c1])
                self.tt("dve", c2[:], p3v, b4(self.C2_ts[d]), ALU.mult, [p3, self.C2_ts[d]], [c2])
                m = gr.next()
                self.tt("pool", m[:], qT[:], b4(self.ident), ALU.add, [qT, self.ident], [m])
                yield
                Q, QT = q0, qT
                for lvl in range(1, 5):
                    pq = ps()
                    for q in range(4):
                        self.mm(pq[:, q * 128:(q + 1) * 128], QT[:, q, :], Q[:, q, :], True, True, [QT, Q], [pq])
                    qn = gr.next()
                    self.cp("act", flat(qn), pq[:, :], [pq], [qn])
                    qtn = None
                    if lvl < 4:
                        pqt = ps()
                        for q in range(4):
                            self.mm(pqt[:, q * 128:(q + 1) * 128], Q[:, q, :], QT[:, q, :], True, True, [QT, Q], [pqt])
                        qtn = gr.next()
                        self.cp("act", flat(qtn), pqt[:, :], [pqt], [qtn])
                    yield
                    pm = ps()
                    for q in range(4):
                        self.mm(pm[:, q * 128:(q + 1) * 128], qn[:, q, :], m[:, q, :], True, True, [qn, m], [pm])
                    mn = gr.next()
                    self.tt("dve", flat(mn), pm[:, :], flat(m), ALU.add, [pm, m], [mn])
                    m = mn
                    Q, QT = qn, qtn
                    yield
                Tt = m
                pt = ps()
                for q in range(4):
                    self.tr(pt[:, q * 128:(q + 1) * 128], Tt[:, q, :], [Tt], [pt])
                Tn = gr.next()
                self.cp("act", flat(Tn), pt[:, :], [pt], [Tn])
                yield

                def step(lhs, rhs):
                    pp = ps()
                    for q in range(4):
                        self.mm(pp[:, q * 128:(q + 1) * 128], lhs[:, q, :], rhs[:, q, :], True, True, [lhs, rhs], [pp])
                    return pp
                py1 = step(c1t, Tn)
                y1 = gr.next()
                self.cp("act", flat(y1), py1[:, :], [py1], [y1])
                yield
                pz1 = step(Tt, y1)
                T64 = gr.next()
                self.tt("dve", flat(T64), flat(Tn), pz1[:, :], ALU.subtract, [Tn, pz1], [T64])
                yield
                py2 = step(c1, Tt)
                y2 = gr.next()
                self.cp("act", flat(y2), py2[:, :], [py2], [y2])
                yield
                pz2 = step(Tn, y2)
                Tt64 = gr.next()
                self.tt("dve", flat(Tt64), flat(Tt), pz2[:, :], ALU.subtract, [Tt, pz2], [Tt64])
                yield
                py3 = step(c2, Tt64)
                y3 = gr.next()
                self.cp("act", flat(y3), py3[:, :], [py3], [y3])
                yield
                pz3 = step(T64, y3)
                self.tt("dve", mf[:, h0:h0 + 4, :].rearrange("p h c -> p (h c)"), flat(Tt64), pz3[:, :], ALU.subtract,
                        [Tt64, pz3], [mf])
                yield

            def seq_gen(d, b, cg, kr, bh, kh, v, pc, lka, arb, mf):
                tok = slice(b * 128, (b + 1) * 128)
                cs = slice(cg * 512, (cg + 1) * 512)
                Hd = H[d][cg]
                pr = ps()
                for h8 in range(8):
                    o = pr[:, h8 * 64:(h8 + 1) * 64]
                    self.mm(o, kr[:, h8, 0, :], Hd[:, h8, :], True, False, [kr, Hd], [pr])
                    self.mm(o, lka[:, h8, 0:128], v[:, h8 * 64:(h8 + 1) * 64], False, True, [lka, v], [pr])
                rsb = Rsb.next()
                self.act(rsb[:], pr[:, :], AF.Copy, [pr], [rsb], scale=-1.0)
                yield
                pu = ps()
                for h8 in range(8):
                    self.mm(pu[:, h8 * 64:(h8 + 1) * 64], mf[:, h8, :], rsb[:, h8 * 64:(h8 + 1) * 64], True, True,
                            [mf, rsb], [pu])
                usb = Usb.next()
                self.cp("dve", usb[:], pu[:, :], [pu], [usb])
                yield
                py = ps()
                pn = ps()
                for h8 in range(8):
                    o = py[:, h8 * 64:(h8 + 1) * 64]
                    hsl = slice(h8 * 64, (h8 + 1) * 64)
                    self.mm(o, kr[:, h8, 1, :], Hd[:, h8, :], True, False, [kr, Hd], [py])
                    self.mm(o, arb[:, h8, :], usb[:, hsl], False, False, [arb, usb], [py])
                    self.mm(o, lka[:, h8, 128:256], v[:, hsl], False, True, [lka, v], [py])
                    o2 = pn[0:64, hsl]
                    self.mm(o2, bh[:, hsl], usb[:, hsl], True, False, [bh, usb], [pn])
                    self.mm(o2, kh[:, hsl], v[:, hsl], False, True, [kh, v], [pn])
                y = Y.next()
                self.cp("act", y[:], py[:, :], [py], [y])
                self.st_(y, S["yd"][d, tok, cs], y[:], W=[self.dbuf("yd", d, b, cg)])
                ht_ = Htmp.next()
                self.tt("pool", ht_[:].rearrange("k (h v) -> k h v", v=64), Hd[:, :, :],
                        pc[:].unsqueeze(2).broadcast_to([64, 8, 64]), ALU.mult, [Hd, pc], [ht_])
                self.tt("dve", Hd[:, :, :], ht_[:].rearrange("k (h v) -> k h v", v=64),
                        pn[0:64, :].rearrange("k (h v) -> k h v", v=64), ALU.add, [ht_, pn], [Hd])
                yield

            orders = [self.block_order(0), self.block_order(1)]
            units = []
            for step_i in range(self.NB):
                for cg in range(4):
                    for d in range(2):
                        units.append((d, orders[d][step_i], cg))
            ustate = {}

            def acquire(u):
                d, b, cg = units[u]
                tok = slice(b * 128, (b + 1) * 128)
                cs = slice(cg * 512, (cg + 1) * 512)
                hs = slice(cg * 8, (cg + 1) * 8)
                kr, bt, ktt, bh, kh, v, pc = KR.next(), BT.next(), KtT.next(), BH.next(), KH.next(), V.next(), PC.next()
                self.ld(kr, kr[:, :, 0, :], S["kT"][d, b, hs].rearrange("h k t -> k h t"), R=[self.dbuf("kT", d, b, cg)])
                self.ld(kr, kr[:, :, 1, :], S["rT"][d, b, hs].rearrange("h k t -> k h t"), R=[self.dbuf("rT", d, b, cg)])
                self.ld(bt, bt[:], S["bT"][d, b, hs].rearrange("h k t -> k h t"), R=[self.dbuf("bT", d, b, cg)])
                self.ld(ktt, ktt[:], S["ktT"][d, b, hs].rearrange("h k t -> k h t"), R=[self.dbuf("ktT", d, b, cg)])
                self.ld(bh, bh[:], S["bh"][d, tok, cs], R=[self.dbuf("bh", d, b, cg)])
                self.ld(kh, kh[:], S["kh"][d, tok, cs], R=[self.dbuf("kh", d, b, cg)])
                self.ld(v, v[:], S["v"][tok, cs], R=[self.dbuf("v", b, cg)])
                self.ld(pc, pc[:], S["pc"][d, b, :, hs], R=[self.dbuf("pc", d, b, cg)])
                mf, lka, arb = MF.next(), LKA.next(), ARB.next()
                ustate[u] = dict(t=(kr, bt, ktt, bh, kh, v, pc, mf, lka, arb), rem=2, done=False)

            from collections import deque
            tasks = deque((u, g4) for u in range(len(units)) for g4 in range(2))
            free_slots = list(range(NSLOT))
            running = []

            def try_start():
                if not tasks or not free_slots:
                    return False
                u, g4 = tasks[0]
                if u not in ustate:
                    if u >= 2 and not ustate[u - 2]["done"]:
                        return False
                    acquire(u)
                d, b, cg = units[u]
                kr, bt, ktt, bh, kh, v, pc, mf, lka, arb = ustate[u]["t"]
                slot = free_slots.pop(0)
                running.append([group_gen(d, kr, bt, ktt, lka, arb, mf, g4, slot), u, slot, "g"])
                tasks.popleft()
                return True

            while tasks or running:
                while try_start():
                    pass
                for r in list(running):
                    try:
                        next(r[0])
                    except StopIteration:
                        running.remove(r)
                        u = r[1]
                        if r[3] == "g":
                            free_slots.append(r[2])
                            ustate[u]["rem"] -= 1
                            if ustate[u]["rem"] == 0:
                                d, b, cg = units[u]
                                kr, bt, ktt, bh, kh, v, pc, mf, lka, arb = ustate[u]["t"]
                                running.append([seq_gen(d, b, cg, kr, bh, kh, v, pc, lka, arb, mf), u, None, "s"])
                        else:
                            ustate[u]["done"] = True
            self.p.flush()

    def pass_C(self, li, rw):
        j = li // 2
        I, S = self.I, self.S
        last = li == self.depth - 1
        with contextlib.ExitStack() as st:
            t = lambda n, s: self.tile(st, n, s)
            WO = t("WO", [128, 16, D])
            wsrc = I["rw_wo"][j] if rw else I["hg_wo"][j]
            for c4 in range(4):
                self.ld(WO, WO[:, c4 * 4:(c4 + 1) * 4, :],
                        wsrc[c4 * 512:(c4 + 1) * 512, :].rearrange("(c p) n -> p c n", p=128))
            pg = t("pg", [2, D])
            self.ld(pg, pg[:], I["post_g"][li].partition_broadcast(2))
            gprow = t("gprow", [2, D])
            gate = t("gate", [2, D])
            self.ld(gate, gate[:], S["mod"][:, 2 * D:3 * D], R=[self.dbuf("mod")])
            self.tt("dve", gprow[:], gate[:], pg[:], ALU.mult, [gate, pg], [gprow])
            GP = self.bcast_rows(st, gprow, 0, D, "GP")
            if rw:
                LNW = t("LNW", [128, DI])
                LNB = t("LNB", [128, DI])
                self.ld(LNW, LNW[:], I["rw_lnw"][j].partition_broadcast(128))
                self.ld(LNB, LNB[:], I["rw_lnb"][j].partition_broadcast(128))
            else:
                GN = t("GN", [128, 128])
                self.ld(GN, GN[:], I["hg_gn"][j].partition_broadcast(128))
            yf = self.rot(st, "yf", [128, 512], 2)
            yb = self.rot(st, "yb", [128, 512], 2)
            zt = self.rot(st, "zt", [128, 512], 2)
            vt = self.rot(st, "vt", [128, 512], 2)
            bont = self.rot(st, "bont", [128, 8], 2)
            w1 = t("w1", [128, 512])
            w2 = t("w2", [128, 512])
            w3 = t("w3", [128, 512])
            smA = t("smA", [128, 8])
            smB = t("smB", [128, 8])
            smC = t("smC", [128, 8])
            yzT = self.rot(st, "yzT", [128, 4, 128], 2)
            xt = self.rot(st, "xc", [128, D], 2)
            ym = t("ym", [128, D])
            junk = t("junkc", [128, D])
            ss = t("ssc", [128, 2])
            G_ = 64 if rw else 128
            ng = 512 // G_

            def v3(ap):
                return ap.rearrange("p (h k) -> p h k", k=G_)

            def bc(ap):
                return ap.unsqueeze(2).broadcast_to([128, ng, G_])
            acc = [self.ps[6], self.ps[7]]
            for b in range(self.NB):
                if last and b < self.NCB:
                    continue
                seg = 1 if b < self.NCB else 0
                tok = slice(b * 128, (b + 1) * 128)
                for cg in range(4):
                    cs = slice(cg * 512, (cg + 1) * 512)
                    f, bk, z = yf.next(), yb.next(), zt.next()
                    self.ld(f, f[:], S["yd"][0, tok, cs], R=[self.dbuf("yd", 0, b, cg)])
                    self.ld(bk, bk[:], S["yd"][1, tok, cs], R=[self.dbuf("yd", 1, b, cg)])
                    self.ld(z, z[:], S["z"][tok, cs], R=[self.dbuf("z", b, cg)])
                    self.tt("pool", w1[:], f[:], bk[:], ALU.add, [f, bk], [w1])
                    if rw:
                        v = vt.next()
                        bon = bont.next()
                        self.ld(v, v[:], S["v"][tok, cs], R=[self.dbuf("v", b, cg)])
                        self.ld(bon, bon[:], S["bon"][tok, cg * 8:(cg + 1) * 8], R=[self.dbuf("bon", b, cg)])
                        self.red(smA[:], v3(w1[:]), [w1], [smA])
                        self.ts("dve", smA[:], smA[:], 1.0 / 64, ALU.mult, [smA], [smA])
                        self.tt("dve", v3(w2[:]), v3(w1[:]), bc(smA[:]), ALU.subtract, [w1, smA], [w2])
                        self.tt("pool", w3[:], w2[:], w2[:], ALU.mult, [w2], [w3])
                        self.red(smB[:], v3(w3[:]), [w3], [smB])
                        self.act(smC[:], smB[:], AF.Ln, [smB], [smC], scale=1.0 / 64, bias=LN_X_EPS)
                        self.act(smC[:], smC[:], AF.Exp, [smC], [smC], scale=-0.5)
                        self.tt("dve", v3(w2[:]), v3(w2[:]), bc(smC[:]), ALU.mult, [w2, smC], [w2])
                        self.tt("pool", w2[:], w2[:], LNW[:, cs], ALU.mult, [w2, LNW], [w2])
                        self.tt("pool", w2[:], w2[:], LNB[:, cs], ALU.add, [w2, LNB], [w2])
                        self.tt("dve", v3(w3[:]), v3(v[:]), bc(bon[:]), ALU.mult, [v, bon], [w3])
                        self.tt("pool", w2[:], w2[:], w3[:], ALU.add, [w2, w3], [w2])
                        self.tt("dve", w2[:], w2[:], z[:], ALU.mult, [w2, z], [w2])
                    else:
                        self.tt("pool", w3[:], w1[:], w1[:], ALU.mult, [w1], [w3])
                        self.red(smB[:, 0:ng], v3(w3[:]), [w3], [smB])
                        self.act(smC[:, 0:ng], smB[:, 0:ng], AF.Ln, [smB], [smC], scale=1.0 / 128, bias=NORM_EPS)
                        self.act(smC[:, 0:ng], smC[:, 0:ng], AF.Exp, [smC], [smC], scale=-0.5)
                        self.tt("dve", v3(w2[:]), v3(w1[:]), bc(smC[:, 0:ng]), ALU.mult, [w1, smC], [w2])
                        self.tt("pool", v3(w2[:]), v3(w2[:]), GN[:].unsqueeze(1).broadcast_to([128, ng, G_]), ALU.mult,
                                [w2, GN], [w2])
                        self.tt("dve", w2[:], w2[:], z[:], ALU.mult, [w2, z], [w2])
                    yz = yzT.next()
                    pb = self.big()
                    for c in range(4):
                        self.tr(pb[:, c * 128:(c + 1) * 128], w2[:, c * 128:(c + 1) * 128], [w2], [pb])
                    self.cp("act", yz[:].rearrange("p c t -> p (c t)"), pb[:, :], [pb], [yz])
                    for c in range(4):
                        cc = cg * 4 + c
                        for hf in range(2):
                            self.mm(acc[hf][:, :], yz[:, c, :], WO[:, cc, hf * 512:(hf + 1) * 512], cc == 0, cc == 15,
                                    [yz, WO], [acc[hf]])
                for hf in range(2):
                    self.cp("act", ym[:, hf * 512:(hf + 1) * 512], acc[hf][:, :], [acc[hf]], [ym])
                self.act(junk[:], ym[:], AF.Square, [ym], [junk, ss], accum=ss[:, 0:1])
                self.act(ss[:, 1:2], ss[:, 0:1], AF.Ln, [ss], [ss], scale=1.0 / D, bias=NORM_EPS)
                self.act(ss[:, 1:2], ss[:, 1:2], AF.Exp, [ss], [ss], scale=-0.5)
                self.stt(ym[:], ym[:], ss[:, 1:2], GP[seg][:], ALU.mult, ALU.mult, [ym, ss, GP[seg]], [ym])
                xap, xb = self.x_ap(li, b)
                x = xt.next()
                self.ld(x, x[:], xap, R=[xb])
                self.tt("dve", x[:], x[:], ym[:], ALU.add, [x, ym], [x])
                oap, ob = self.x_out_ap(li, b)
                self.st_(x, oap, x[:], W=[ob])
            self.p.flush()

    def hg_pass_A(self, li):
        j = li // 2
        I, S = self.I, self.S
        STH = 4
        with contextlib.ExitStack() as st:
            t = lambda n, s: self.tile(st, n, s)
            LB = t("LB", [128, DI])
            OMLB = t("OMLB", [128, DI])
            self.ld(LB, LB[:], S["lb"][j].partition_broadcast(128), R=[self.dbuf("lb")])
            self.ts("dve", OMLB[:], LB[:], -1.0, ALU.mult, [LB], [OMLB], s2=1.0, op1=ALU.add)
            HT = self.rot(st, "HT", [128, 8, 128], 2 * STH)
            WT = self.rot(st, "WTh", [128, 8, 512], 3)
            wkq = [t("hk_q%d" % i, [128, 512]) for i in range(STH)]
            sets = []
            for si in range(2):
                wk = {n: t("hk%d_%s" % (si, n), [128, 512]) for n in
                      ("v", "g", "sg", "f", "k", "lf", "E1", "E2", "E3", "o2", "o3", "o1a", "o1b")}
                wk["TTq"] = t("TTq%d" % si, [128, 4, 128])
                wk["TTqb"] = t("TTqb%d" % si, [128, 4, 128])
                wk["TTk"] = t("TTk%d" % si, [128, 4, 128])
                wk["PCM"] = t("PCM%d" % si, [128, 4, 4])
                self.memset("pool", wk["o1a"][:], 0.0, [wk["o1a"]])
                self.memset("pool", wk["o1b"][:], 0.0, [wk["o1b"]])
                sets.append(wk)
            sbs = [list(range(a, min(a + STH, self.NB))) for a in range(0, self.NB, STH)]
            pieces = []
            for si_, blks in enumerate(sbs):
                for cg in range(4):
                    for kind, col0 in (("q", 0), ("i", 3 * DI), ("g", 4 * DI), ("f0", DI), ("f1", 2 * DI)):
                        pieces.append((si_, cg, kind, col0))
            wbuf = {}

            def wload(k):
                if k >= len(pieces) or k in wbuf:
                    return
                si_, cg, kind, col0 = pieces[k]
                w = WT.next()
                self.ld(w, w[:], I["hg_win"][j, :, col0 + cg * 512:col0 + (cg + 1) * 512].rearrange("(c p) n -> p c n", p=128))
                wbuf[k] = w
            hbuf = {}

            def hload(si_):
                if si_ >= len(sbs) or si_ in hbuf:
                    return
                lst = []
                for b in sbs[si_]:
                    hT = HT.next()
                    self.ld(hT, hT[:], S["hT"].rearrange("(c p) t -> p c t", p=128)[:, :, b * 128:(b + 1) * 128],
                            R=[self.dbuf("hT", b)])
                    lst.append(hT)
                hbuf[si_] = lst
            chains = []

            def run_round():
                for g in list(chains):
                    try:
                        next(g)
                    except StopIteration:
                        chains.remove(g)

            def chain_gen(wk, d, b, cg, q_, pb):
                tok = slice(b * 128, (b + 1) * 128)
                cs = slice(cg * 512, (cg + 1) * 512)
                self.act(wk["sg"][:], pb[:, :], AF.Sigmoid, [pb], [wk["sg"]])
                yield
                self.tt("dve", wk["f"][:], wk["sg"][:], OMLB[:, cs], ALU.mult, [wk["sg"], OMLB], [wk["f"]])
                self.tt("pool", wk["f"][:], wk["f"][:], LB[:, cs], ALU.add, [wk["f"], LB], [wk["f"]])
                yield
                self.ts("dve", wk["k"][:], wk["f"][:], -1.0, ALU.mult, [wk["f"]], [wk["k"]], s2=1.0, op1=ALU.add)
                self.act(wk["lf"][:], wk["f"][:], AF.Ln, [wk["f"]], [wk["lf"]])
                yield
                lf = wk["lf"]
                p1 = self.half()
                self.mm(p1[:, :], self.HD[d][:], lf[:], True, True, [self.HD[d], lf], [p1])
                p2 = self.half()
                self.mm(p2[:, :], self.REM64[d][:], lf[:], True, True, [self.REM64[d], lf], [p2])
                ph = self.half()
                for hh in range(4):
                    self.mm(ph[:, hh * 4:hh * 4 + 4], lf[:, hh * 128:(hh + 1) * 128], self.OM[d][:], True, True,
                            [lf, self.OM[d]], [ph])
                self.act(wk["E1"][:], p1[:, :], AF.Exp, [p1], [wk["E1"]])
                self.act(wk["E2"][:], p1[:, :], AF.Exp, [p1], [wk["E2"]], scale=-1.0)
                self.act(wk["E3"][:], p2[:, :], AF.Exp, [p2], [wk["E3"]])
                PCM = wk["PCM"]
                self.act(PCM[:].rearrange("p h a -> p (h a)"), ph[:, 0:16], AF.Exp, [ph], [PCM])
                self.st_(PCM, S["pcm"][d, b, :, cg * 4:(cg + 1) * 4, :], PCM[:], W=[self.dbuf("pcm", d, b, cg)])
                yield
                o1a, o1b = wk["o1a"], wk["o1b"]
                self.tt("dve", o1a[0:64, :], q_[0:64, :], wk["E1"][0:64, :], ALU.mult, [q_, wk["E1"]], [o1a])
                self.tt("dve", o1b[64:128, :], q_[64:128, :], wk["E1"][64:128, :], ALU.mult, [q_, wk["E1"]], [o1b])
                self.tt("pool", wk["o2"][:], wk["k"][:], wk["E2"][:], ALU.mult, [wk["k"], wk["E2"]], [wk["o2"]])
                self.tt("pool", wk["o3"][:], wk["k"][:], wk["E3"][:], ALU.mult, [wk["k"], wk["E3"]], [wk["o3"]])
                self.st_(wk["o3"], S["kh"][d, tok, cs], wk["o3"][:], W=[self.dbuf("kh", d, b, cg)])
                yield
                for src, dst, nm in ((o1a, wk["TTq"], "qT"), (o1b, wk["TTqb"], "qbT"), (wk["o2"], wk["TTk"], "gkT")):
                    pbt = self.half()
                    for hh in range(4):
                        self.tr(pbt[:, hh * 128:(hh + 1) * 128], src[:, hh * 128:(hh + 1) * 128], [src], [pbt])
                    self.cp("act" if nm != "qbT" else "dve", dst[:].rearrange("p h t -> p (h t)"), pbt[:, :], [pbt], [dst])
                    self.st_(dst, S[nm][d, b, cg * 4:(cg + 1) * 4].rearrange("h k t -> k h t"), dst[:],
                             W=[self.dbuf(nm, d, b, cg)])
                    yield

            hload(0)
            wload(0)
            wload(1)
            nchain = 0
            for k, (si_, cg, kind, col0) in enumerate(pieces):
                wload(k + 2)
                if kind == "q":
                    while chains:
                        run_round()
                if cg == 0 and kind == "q":
                    hload(si_ + 1)
                w = wbuf.pop(k)
                blks = sbs[si_]
                cs = slice(cg * 512, (cg + 1) * 512)
                for bi, b in enumerate(blks):
                    tok = slice(b * 128, (b + 1) * 128)
                    hT = hbuf[si_][bi]
                    pb = self.big()
                    for c in range(8):
                        self.mm(pb[:, :], hT[:, c, :], w[:, c, :], c == 0, c == 7, [hT, w], [pb])
                    if kind == "q":
                        self.act(wkq[bi][:], pb[:, :], AF.Silu, [pb], [wkq[bi]])
                        continue
                    wk = sets[nchain % 2]
                    nchain += 1
                    if kind == "i":
                        self.cp("act", wk["v"][:], pb[:, :], [pb], [wk["v"]])
                        self.st_(wk["v"], S["v"][tok, cs], wk["v"][:], W=[self.dbuf("v", b, cg)])
                        continue
                    if kind == "g":
                        self.act(wk["g"][:], pb[:, :], AF.Silu, [pb], [wk["g"]])
                        self.st_(wk["g"], S["z"][tok, cs], wk["g"][:], W=[self.dbuf("z", b, cg)])
                        continue
                    d = 0 if kind == "f0" else 1
                    chains.append(chain_gen(wk, d, b, cg, wkq[bi], pb))
                    while len(chains) >= 2:
                        run_round()
            while chains:
                run_round()
            self.p.flush()

    def hg_pass_B(self, li):
        S = self.S
        with contextlib.ExitStack() as st:
            t = lambda n, s: self.tile(st, n, s)
            allps = Rot(self.ps)
            ps = allps.next
            ST = [[t("ST%d_%d" % (d, g), [128, 4, 128]) for g in range(4)] for d in range(2)]
            for d in range(2):
                for g in range(4):
                    self.memset("pool", ST[d][g][:], 0.0, [ST[d][g]])
            NP = 4
            QT = self.rot(st, "QT", [128, 4, 128], NP)
            QBT = self.rot(st, "QBT", [128, 4, 128], NP)
            KT = self.rot(st, "KT", [128, 4, 128], NP)
            KH = self.rot(st, "KHh", [128, 512], NP)
            V = self.rot(st, "Vh", [128, 512], NP)
            PCM = self.rot(st, "PCMh", [128, 4, 4], NP)
            AT = self.rot(st, "AT", [128, 4, 128], 4)
            SS = self.rot(st, "SS", [128, 4, 128], 6)
            SM = self.rot(st, "SM", [128, 4, 128], 4)
            TM = self.rot(st, "TM", [128, 4, 128], 4)
            Y = self.rot(st, "Yh", [128, 512], 3)

            def flat(b_):
                return b_[:].rearrange("p h c -> p (h c)")

            def unit_gen(d, b, cg, qt, qbt, kt, kh, v, pcm):
                tok = slice(b * 128, (b + 1) * 128)
                cs = slice(cg * 512, (cg + 1) * 512)
                Sd = ST[d][cg]
                f0, f1 = (0, 1) if d == 0 else (1, 0)
                r0 = slice(f0 * 64, (f0 + 1) * 64)
                r1 = slice(f1 * 64, (f1 + 1) * 64)
                qts = (qt, qbt)

                def pcb(col):
                    return pcm[:, :, col:col + 1].broadcast_to([128, 4, 128])
                pa = ps()
                for hh in range(4):
                    o = pa[:, hh * 128:(hh + 1) * 128]
                    self.mm(o, kt[:, hh, :], qt[:, hh, :], True, False, [kt, qt], [pa])
                    self.mm(o, kt[:, hh, :], qbt[:, hh, :], False, True, [kt, qbt], [pa])
                at = AT.next()
                self.tt("dve", at[:], pa[:, :].rearrange("p (h c) -> p h c", c=128),
                        self.IB64[d][:].unsqueeze(1).broadcast_to([128, 4, 128]), ALU.mult, [pa, self.IB64[d]], [at])
                s0 = SS.next()
                self.tt("pool", s0[:], Sd[:], pcb(2 + f0), ALU.mult, [Sd, pcm], [s0])
                yield
                pn0 = ps()
                for hh in range(4):
                    hsl = slice(hh * 128, (hh + 1) * 128)
                    self.mm(pn0[:, hsl], kh[r0, hsl], v[r0, hsl], True, True, [kh, v], [pn0])
                tm = TM.next()
                self.tt("pool", tm[:], Sd[:], pcb(f0), ALU.mult, [Sd, pcm], [tm])
                smid = SM.next()
                self.tt("dve", flat(smid), flat(tm), pn0[:, :], ALU.add, [tm, pn0], [smid])
                yield
                s1 = SS.next()
                self.tt("pool", s1[:], smid[:], pcb(2 + f1), ALU.mult, [smid, pcm], [s1])
                pn1 = ps()
                for hh in range(4):
                    hsl = slice(hh * 128, (hh + 1) * 128)
                    self.mm(pn1[:, hsl], kh[r1, hsl], v[r1, hsl], True, True, [kh, v], [pn1])
                tm2 = TM.next()
                self.tt("pool", tm2[:], smid[:], pcb(f1), ALU.mult, [smid, pcm], [tm2])
                self.tt("dve", flat(Sd), flat(tm2), pn1[:, :], ALU.add, [tm2, pn1], [Sd])
                yield
                py = ps()
                for hh in range(4):
                    hsl = slice(hh * 128, (hh + 1) * 128)
                    self.mm(py[:, hsl], qts[f0][:, hh, :], s0[:, hh, :], True, False, [qts[f0], s0], [py])
                    self.mm(py[:, hsl], qts[f1][:, hh, :], s1[:, hh, :], False, False, [qts[f1], s1], [py])
                    self.mm(py[:, hsl], at[:, hh, :], v[:, hsl], False, True, [at, v], [py])
                y = Y.next()
                self.cp("act", y[:], py[:, :], [py], [y])
                self.st_(y, S["yd"][d, tok, cs], y[:], W=[self.dbuf("yd", d, b, cg)])
                yield

            active = []

            def run_round():
                for g in list(active):
                    try:
                        next(g)
                    except StopIteration:
                        active.remove(g)
            orders = [self.block_order(0), self.block_order(1)]
            for step_i in range(self.NB):
                for cg in range(4):
                    gens = []
                    for d in range(2):
                        b = orders[d][step_i]
                        tok = slice(b * 128, (b + 1) * 128)
                        cs = slice(cg * 512, (cg + 1) * 512)
                        qt, qbt, kt, kh, v, pcm = QT.next(), QBT.next(), KT.next(), KH.next(), V.next(), PCM.next()
                        hsel = slice(cg * 4, (cg + 1) * 4)
                        self.ld(qt, qt[:], S["qT"][d, b, hsel].rearrange("h k t -> k h t"), R=[self.dbuf("qT", d, b, cg)])
                        self.ld(qbt, qbt[:], S["qbT"][d, b, hsel].rearrange("h k t -> k h t"), R=[self.dbuf("qbT", d, b, cg)])
                        self.ld(kt, kt[:], S["gkT"][d, b, hsel].rearrange("h k t -> k h t"), R=[self.dbuf("gkT", d, b, cg)])
                        self.ld(kh, kh[:], S["kh"][d, tok, cs], R=[self.dbuf("kh", d, b, cg)])
                        self.ld(v, v[:], S["v"][tok, cs], R=[self.dbuf("v", b, cg)])
                        self.ld(pcm, pcm[:], S["pcm"][d, b, :, hsel, :], R=[self.dbuf("pcm", d, b, cg)])
                        gens.append(unit_gen(d, b, cg, qt, qbt, kt, kh, v, pcm))
                    active.extend(gens)
                    while active:
                        run_round()
            self.p.flush()


def build_nc(rows=64, ctx_len=256, depth=4, debug=False):
    nc = bass.Bass("TRN2", target_bir_lowering=False)
    bld = Builder(nc, rows=rows, ctx_len=ctx_len, depth=depth, debug=debug)
    bld.build()
    return nc, bld


PARAM_NAMES = ["mod_w", "mod_b", "pre_g", "post_g", "rw_mix", "rw_proj", "rw_wo", "rw_w0", "rw_w1", "rw_w2",
               "rw_a0", "rw_a1", "rw_a2", "rw_v0", "rw_v1", "rw_v2", "rw_kk", "rw_ka", "rw_rk", "rw_lnw", "rw_lnb",
               "hg_win", "hg_wo", "hg_gn", "hg_lb"]


def kernel(**inputs):
    x = np.ascontiguousarray(inputs["x"], dtype=np.float32)
    B, SEQ, _ = x.shape
    ctx = np.ascontiguousarray(inputs["ctx"], dtype=np.float32)
    c = np.ascontiguousarray(inputs["c"], dtype=np.float32)
    shared = {n: np.ascontiguousarray(inputs[n], dtype=np.float32) for n in PARAM_NAMES}
    shared["c_ctx"] = np.ascontiguousarray(inputs["c_ctx"], dtype=np.float32)
    nc, _ = build_nc(rows=SEQ // 64, ctx_len=ctx.shape[1], depth=4)
    in_maps = []
    for b in range(B):
        m = dict(shared)
        m["x"] = x[b]
        m["ctx"] = ctx[b]
        m["c"] = c[b]
        in_maps.append(m)
    res = run_bass_kernel_spmd(nc, in_maps, core_ids=list(range(B)))
    return np.stack([np.asarray(r["out"]) for r in res.results], axis=0).astype(np.float32)
```
